# Optimizing a Trainium2 kernel written in Bass

```python
import jax, jax.numpy as jnp
from jax import lax
import numpy as np

D_MODEL = 1024
BATCH = 16
SEQ = 2048
DEPTH = 4
DEC_BATCH = 8
DEC_SEQ = 4096
PAST_LEN = 128

N_META = 16
GRID_W = 64
CONV_DIM = 256
CONV_WIDTH = 3
NA_HEADS = 4
NA_HEAD_DIM = 64
NA_DIM = NA_HEADS * NA_HEAD_DIM
NA_MAX_WIN_H = 8
NA_WIN_W = 16
MLA_HEADS = 8
MLA_NOPE_DIM = 64
MLA_ROPE_DIM = 32
MLA_V_DIM = 64
MLA_Q_RANK = 768
MLA_KV_RANK = 256
MLA_DIM = MLA_HEADS * MLA_V_DIM
MIX_DIM = CONV_DIM + NA_DIM + MLA_DIM
IN_PROJ_DIM = 3 * CONV_DIM + 3 * NA_DIM + MLA_Q_RANK + MLA_KV_RANK + MLA_ROPE_DIM
FFN_DIM = 2816
ROPE_THETA = 10000.0
RMS_EPS = 1e-6
Q_BLOCK = 128
F32 = jnp.float32

kernel_name = 'hybrid_conv_natten_mla_encoder'


def _rms_norm(x, g):
    xf = x.astype(F32)
    y = xf * lax.rsqrt(jnp.mean(xf * xf, axis=-1, keepdims=True) + RMS_EPS)
    return (y * g.astype(F32)).astype(x.dtype)


def _swiglu(x, w_gu, w_down):
    gate, up = jnp.split(x @ w_gu, 2, axis=-1)
    return (jax.nn.silu(gate) * up) @ w_down


def _rope(x, pos):
    half = x.shape[-1] // 2
    inv_freq = ROPE_THETA ** (-jnp.arange(half, dtype=F32) / half)
    ang = pos.astype(F32)[:, None] * inv_freq[None, :]
    cos = jnp.cos(ang)[None, :, None, :].astype(x.dtype)
    sin = jnp.sin(ang)[None, :, None, :].astype(x.dtype)
    x1, x2 = x[..., :half], x[..., half:]
    return jnp.concatenate([x1 * cos - x2 * sin, x1 * sin + x2 * cos], axis=-1)


def _short_conv(u, w):
    up = jnp.pad(u, ((0, 0), (1, 1), (0, 0)))
    return w[0] * up[:, :-2] + w[1] * up[:, 1:-1] + w[2] * up[:, 2:]


def _neighbourhood_attention(q, k, v, rpb, meta_bias):
    B, L, H, dh = q.shape
    T = L - N_META
    rows = T // GRID_W
    win_h = min(NA_MAX_WIN_H, rows)
    scale = dh ** -0.5
    q_m, k_m, v_m = q[:, :N_META], k[:, :N_META], v[:, :N_META]
    mb = meta_bias.astype(F32)[None, :, None, :]
    s_mm = jnp.einsum('bqhd,bmhd->bhqm', q_m, k_m).astype(F32) * scale + mb
    p_mm = jax.nn.softmax(s_mm, axis=-1).astype(v.dtype)
    o_meta = jnp.einsum('bhqm,bmhd->bqhd', p_mm, v_m)
    q_g = q[:, N_META:].reshape(B, rows, GRID_W, H, dh)
    k_g = k[:, N_META:].reshape(B, rows, GRID_W, H, dh)
    v_g = v[:, N_META:].reshape(B, rows, GRID_W, H, dh)
    col_start = np.clip(np.arange(GRID_W) - NA_WIN_W // 2, 0, GRID_W - NA_WIN_W)
    col_idx = col_start[:, None] + np.arange(NA_WIN_W)[None, :]
    col_rel = col_idx - np.arange(GRID_W)[:, None] + (NA_WIN_W - 1)
    rpb_c = rpb.astype(F32)[:, :, col_rel]
    n_loc = win_h * NA_WIN_W

    def row_block(args):
        r, q_row = args
        rs = jnp.clip(r - win_h // 2, 0, rows - win_h)
        k_rows = lax.dynamic_slice_in_dim(k_g, rs, win_h, axis=1)
        v_rows = lax.dynamic_slice_in_dim(v_g, rs, win_h, axis=1)
        k_win = k_rows[:, :, col_idx]
        v_win = v_rows[:, :, col_idx]
        row_rel = rs + jnp.arange(win_h) - r + (NA_MAX_WIN_H - 1)
        bias = jnp.take(rpb_c, row_rel, axis=1).transpose(0, 2, 1, 3)
        s_loc = jnp.einsum('bchd,bwcvhd->bhcwv', q_row, k_win).astype(F32) * scale + bias[None]
        s_loc = s_loc.reshape(B, H, GRID_W, n_loc)
        s_met = jnp.einsum('bchd,bmhd->bhcm', q_row, k_m).astype(F32) * scale + mb
        p = jax.nn.softmax(jnp.concatenate([s_loc, s_met], axis=-1), axis=-1).astype(v.dtype)
        p_loc = p[..., :n_loc].reshape(B, H, GRID_W, win_h, NA_WIN_W)
        p_met = p[..., n_loc:]
        return (jnp.einsum('bhcwv,bwcvhd->bchd', p_loc, v_win)
                + jnp.einsum('bhcm,bmhd->bchd', p_met, v_m))

    o_grid = lax.map(row_block, (jnp.arange(rows, dtype=jnp.int32), jnp.moveaxis(q_g, 1, 0)))
    o_grid = jnp.moveaxis(o_grid, 0, 1).reshape(B, T, H, dh)
    return jnp.concatenate([o_meta, o_grid], axis=1)


def _mla(q_lat, kv_lat, k_pe_raw, pos, q_norm, w_uq, kv_norm, w_ukv):
    B, L, _ = q_lat.shape
    q = (_rms_norm(q_lat, q_norm) @ w_uq).reshape(B, L, MLA_HEADS, MLA_NOPE_DIM + MLA_ROPE_DIM)
    q_nope = q[..., :MLA_NOPE_DIM]
    q_pe = _rope(q[..., MLA_NOPE_DIM:], pos)
    kv = (_rms_norm(kv_lat, kv_norm) @ w_ukv).reshape(B, L, MLA_HEADS, MLA_NOPE_DIM + MLA_V_DIM)
    k_nope = kv[..., :MLA_NOPE_DIM]
    v = kv[..., MLA_NOPE_DIM:]
    k_pe = _rope(k_pe_raw[:, :, None, :], pos)[:, :, 0]
    scale = (MLA_NOPE_DIM + MLA_ROPE_DIM) ** -0.5

    def attend(args):
        qn, qp = args
        s = (jnp.einsum('bqhd,bkhd->bhqk', qn, k_nope)
             + jnp.einsum('bqhd,bkd->bhqk', qp, k_pe)).astype(F32) * scale
        p = jax.nn.softmax(s, axis=-1).astype(v.dtype)
        return jnp.einsum('bhqk,bkhd->bqhd', p, v)

    o_meta = attend((q_nope[:, :N_META], q_pe[:, :N_META]))
    T = L - N_META
    nblk = T // Q_BLOCK
    qn_b = jnp.moveaxis(q_nope[:, N_META:].reshape(B, nblk, Q_BLOCK, MLA_HEADS, MLA_NOPE_DIM), 1, 0)
    qp_b = jnp.moveaxis(q_pe[:, N_META:].reshape(B, nblk, Q_BLOCK, MLA_HEADS, MLA_ROPE_DIM), 1, 0)
    o_b = lax.map(attend, (qn_b, qp_b))
    o_real = jnp.moveaxis(o_b, 0, 1).reshape(B, T, MLA_HEADS, MLA_V_DIM)
    return jnp.concatenate([o_meta, o_real], axis=1).reshape(B, L, MLA_DIM)


SPLIT_POINTS = [int(i) for i in np.cumsum([CONV_DIM] * 3 + [NA_DIM] * 3 + [MLA_Q_RANK, MLA_KV_RANK])]


def _mixer(h, pos, w_in, conv_w, na_rpb, na_meta_bias, mla_q_norm, mla_w_uq, mla_kv_norm, mla_w_ukv,
           conv_out_norm, na_out_norm, mla_out_norm, w_o):
    B, L, _ = h.shape
    z = h @ w_in
    cb, cc, cu, nq, nk, nv, q_lat, kv_lat, k_pe_raw = jnp.split(z, SPLIT_POINTS, axis=-1)
    y_conv = cb * _short_conv(cc * cu, conv_w)
    shp = (B, L, NA_HEADS, NA_HEAD_DIM)
    y_na = _neighbourhood_attention(nq.reshape(shp), nk.reshape(shp), nv.reshape(shp),
                                    na_rpb, na_meta_bias).reshape(B, L, NA_DIM)
    y_mla = _mla(q_lat, kv_lat, k_pe_raw, pos, mla_q_norm, mla_w_uq, mla_kv_norm, mla_w_ukv)
    y = jnp.concatenate([_rms_norm(y_conv, conv_out_norm), _rms_norm(y_na, na_out_norm),
                         _rms_norm(y_mla, mla_out_norm)], axis=-1)
    return y @ w_o


def _layer(h, pos, w):
    (f1_pre, f1_gu, f1_down, f1_post, m_pre, w_in, conv_w, na_rpb, na_meta_bias, q_norm, w_uq,
     kv_norm, w_ukv, c_on, n_on, m_on, w_o, m_post, f2_pre, f2_gu, f2_down, f2_post) = w
    h = h + 0.5 * _rms_norm(_swiglu(_rms_norm(h, f1_pre), f1_gu, f1_down), f1_post)
    h = h + _rms_norm(_mixer(_rms_norm(h, m_pre), pos, w_in, conv_w, na_rpb, na_meta_bias, q_norm, w_uq,
                             kv_norm, w_ukv, c_on, n_on, m_on, w_o), m_post)
    h = h + 0.5 * _rms_norm(_swiglu(_rms_norm(h, f2_pre), f2_gu, f2_down), f2_post)
    return h


def _trunk(x, meta_tokens, weights):
    B, T, _ = x.shape
    L = N_META + T
    meta = jnp.broadcast_to(meta_tokens.astype(x.dtype)[None], (B, N_META, D_MODEL))
    h = jnp.concatenate([meta, x], axis=1)
    pos = jnp.arange(L, dtype=jnp.int32)
    for l in range(DEPTH):
        h = _layer(h, pos, tuple(wt[l] for wt in weights))
    return h[:, N_META:]


def setup_inputs(seed: int = 0) -> dict:
    key = jax.random.key(seed)
    ks = jax.random.split(key, 32)

    def w(k, shape, fan_in):
        return jax.random.normal(k, shape, F32) * fan_in ** -0.5

    def g(k, shape):
        return 1.0 + 0.05 * jax.random.normal(k, shape, F32)

    D = DEPTH
    return {
        'x_prompt': jax.random.normal(ks[0], (BATCH, SEQ, D_MODEL), F32),
        'x_sample': jax.random.normal(ks[1], (DEC_BATCH, DEC_SEQ, D_MODEL), F32),
        'meta_tokens': jax.random.normal(ks[2], (N_META, D_MODEL), F32),
        'ffn1_pre_norm': g(ks[3], (D, D_MODEL)),
        'ffn1_w_gu': w(ks[4], (D, D_MODEL, 2 * FFN_DIM), D_MODEL),
        'ffn1_w_down': w(ks[5], (D, FFN_DIM, D_MODEL), FFN_DIM),
        'ffn1_post_norm': g(ks[6], (D, D_MODEL)),
        'mix_pre_norm': g(ks[7], (D, D_MODEL)),
        'w_in': w(ks[8], (D, D_MODEL, IN_PROJ_DIM), D_MODEL),
        'conv_w': w(ks[9], (D, CONV_WIDTH, CONV_DIM), CONV_WIDTH),
        'na_rpb': 0.1 * jax.random.normal(ks[10], (D, NA_HEADS, 2 * NA_MAX_WIN_H - 1, 2 * NA_WIN_W - 1), F32),
        'na_meta_bias': 0.1 * jax.random.normal(ks[11], (D, NA_HEADS, N_META), F32),
        'mla_q_norm': g(ks[12], (D, MLA_Q_RANK)),
        'mla_w_uq': w(ks[13], (D, MLA_Q_RANK, MLA_HEADS * (MLA_NOPE_DIM + MLA_ROPE_DIM)), MLA_Q_RANK),
        'mla_kv_norm': g(ks[14], (D, MLA_KV_RANK)),
        'mla_w_ukv': w(ks[15], (D, MLA_KV_RANK, MLA_HEADS * (MLA_NOPE_DIM + MLA_V_DIM)), MLA_KV_RANK),
        'conv_out_norm': g(ks[16], (D, CONV_DIM)),
        'na_out_norm': g(ks[17], (D, NA_DIM)),
        'mla_out_norm': g(ks[18], (D, MLA_DIM)),
        'w_o': w(ks[19], (D, MIX_DIM, D_MODEL), MIX_DIM),
        'mix_post_norm': g(ks[20], (D, D_MODEL)),
        'ffn2_pre_norm': g(ks[21], (D, D_MODEL)),
        'ffn2_w_gu': w(ks[22], (D, D_MODEL, 2 * FFN_DIM), D_MODEL),
        'ffn2_w_down': w(ks[23], (D, FFN_DIM, D_MODEL), FFN_DIM),
        'ffn2_post_norm': g(ks[24], (D, D_MODEL)),
    }


def reference(x_prompt, x_sample, meta_tokens, ffn1_pre_norm, ffn1_w_gu, ffn1_w_down, ffn1_post_norm,
              mix_pre_norm, w_in, conv_w, na_rpb, na_meta_bias, mla_q_norm, mla_w_uq, mla_kv_norm, mla_w_ukv,
              conv_out_norm, na_out_norm, mla_out_norm, w_o, mix_post_norm, ffn2_pre_norm, ffn2_w_gu,
              ffn2_w_down, ffn2_post_norm):
    weights = (ffn1_pre_norm, ffn1_w_gu, ffn1_w_down, ffn1_post_norm, mix_pre_norm, w_in, conv_w, na_rpb,
               na_meta_bias, mla_q_norm, mla_w_uq, mla_kv_norm, mla_w_ukv, conv_out_norm, na_out_norm,
               mla_out_norm, w_o, mix_post_norm, ffn2_pre_norm, ffn2_w_gu, ffn2_w_down, ffn2_post_norm)
    y_prompt = _trunk(x_prompt, meta_tokens, weights)
    y_sample = _trunk(x_sample, meta_tokens, weights)
    return (y_prompt, y_sample)
```

```python
import contextlib
import numpy as np
import concourse.bass as bass
import concourse.mybir as mybir
from concourse.bass_utils import run_bass_kernel_spmd

F32 = mybir.dt.float32
BF16 = mybir.dt.bfloat16
AF = mybir.ActivationFunctionType
ALU = mybir.AluOpType

D_MODEL = 1024
FFN = 2816
NG = 70
EPS = 1e-6
NEG = -30000.0
MLA_SCALE = 96.0 ** -0.5
NA_SCALE = 0.125
G_F1PRE, G_F1POST, G_MPRE, G_QN, G_KVN, G_CON, G_NON, G_MON, G_MPOST, G_F2PRE, G_F2POST, G_CONVW = (
    0, 8, 16, 24, 30, 32, 34, 36, 40, 48, 56, 64)


ALL_TOKS = []


class Tok:
    __slots__ = ("w", "r")

    def __init__(self):
        self.w = None
        self.r = {}
        ALL_TOKS.append(self)


class Tile:
    def __init__(self, t, tok=None):
        self.t = t
        self.tok = tok or Tok()

    def __getitem__(self, idx):
        return self.t[idx]


class Prog:
    CE = ("pe", "act", "dve", "pool")
    ENG = ("pe", "act", "dve", "pool", "sp")
    NSETS = 4

    def __init__(self, nc, es, n_dma=40):
        self.nc = nc
        self.psem = {(e, s): es.enter_context(nc.semaphore("p_%s%d" % (e, s))) for e in self.CE for s in range(self.NSETS)}
        self.pcnt = {(e, s): 0 for e in self.CE for s in range(self.NSETS)}
        self.cur = 0
        self.dsem = [es.enter_context(nc.semaphore("d%d" % i)) for i in range(n_dma)]
        self.dcnt = [0] * n_dma
        self.dnext = 0
        self.seen = {e: {} for e in self.ENG}
        self.streams = {e: [] for e in self.ENG}
        self.nops = 0

    def _deps(self, reads, writes):
        d = {}
        for t in reads:
            if t.w is not None:
                k, v = t.w
                if d.get(k, 0) < v:
                    d[k] = v
        for t in writes:
            if t.w is not None:
                k, v = t.w
                if d.get(k, 0) < v:
                    d[k] = v
            for k, v in t.r.items():
                if d.get(k, 0) < v:
                    d[k] = v
        return d

    def _mark(self, reads, writes, ev):
        for t in writes:
            t.w = ev
            t.r = {}
        k, v = ev
        for t in reads:
            t.r[k] = v

    def _emit(self, eng, d, fn, evk, inc, strict=False):
        waits = []
        seen = self.seen[eng]
        for k, v in d.items():
            if eng == "pe" and k[0] == "p" and k[1] == "pe" and not strict:
                continue
            if seen.get(k, 0) < v:
                seen[k] = v
                waits.append((k, v))
        self.streams[eng].append((waits, fn, evk, inc))
        self.nops += 1

    def op(self, eng, fn, reads=(), writes=(), signal=True):
        d = self._deps(reads, writes)
        key = ("p", eng, self.cur)
        ck = (eng, self.cur)
        if signal:
            self.pcnt[ck] += 1
            assert self.pcnt[ck] < 65000, "progress semaphore overflow"
            ev = (key, self.pcnt[ck])
            self._emit(eng, d, fn, key, 1)
        else:
            ev = (key, self.pcnt[ck] + 1)
            self._emit(eng, d, fn, None, 0)
        self._mark(reads, writes, ev)

    def dma(self, q, fn, reads=(), writes=()):
        d = self._deps(reads, writes)
        i = self.dnext
        self.dnext = (i + 1) % len(self.dsem)
        k = ("d", i)
        if self.dcnt[i] > 0 and d.get(k, 0) < self.dcnt[i]:
            d[k] = self.dcnt[i]
        self.dcnt[i] += 16
        assert self.dcnt[i] < 65000, "dma semaphore overflow"
        ev = (k, self.dcnt[i])
        self._emit(q, d, fn, k, 16)
        self._mark(reads, writes, ev)

    def barrier(self, next_set=None):
        d = {("p", e, s): self.pcnt[(e, s)] for e in self.CE for s in range(self.NSETS) if self.pcnt[(e, s)] > 0}
        for i, c in enumerate(self.dcnt):
            if c > 0:
                d[("d", i)] = c
        for e in self.ENG:
            self._emit(e, dict(d), None, None, 0, strict=True)
        for t in ALL_TOKS:
            t.w = None
            t.r = {}
        if next_set is not None:
            self.cur = next_set

    def check(self):
        sem = {}
        pos = {e: 0 for e in self.ENG}
        n = {e: len(self.streams[e]) for e in self.ENG}
        while True:
            prog = False
            for e in self.ENG:
                st = self.streams[e]
                while pos[e] < n[e]:
                    waits, fn, evk, inc = st[pos[e]]
                    if any(sem.get(k, 0) < v for k, v in waits):
                        break
                    if evk is not None:
                        sem[evk] = sem.get(evk, 0) + inc
                    pos[e] += 1
                    prog = True
            if all(pos[e] == n[e] for e in self.ENG):
                return
            if not prog:
                msg = []
                for e in self.ENG:
                    if pos[e] < n[e]:
                        waits = self.streams[e][pos[e]][0]
                        msg.append((e, pos[e], [(k, v, sem.get(k, 0)) for k, v in waits if sem.get(k, 0) < v]))
                raise RuntimeError("DEADLOCK in generated program: %r" % (msg,))

    def run(self):
        nc = self.nc
        self.check()

        def semof(k):
            return self.psem[(k[1], k[2])] if k[0] == "p" else self.dsem[k[1]]

        def mk(name):
            stream = self.streams[name]

            def body(e):
                for waits, fn, evk, inc in stream:
                    for k, v in waits:
                        e.wait_ge(semof(k), v)
                    if fn is not None:
                        ins = fn(e)
                        if evk is not None:
                            ins.then_inc(semof(evk), inc)
            return body

        with nc.Block() as block:
            block.tensor(mk("pe"))
            block.scalar(mk("act"))
            block.vector(mk("dve"))
            block.gpsimd(mk("pool"))
            block.sync(mk("sp"))


def build(seqs, depth, lmax=None, debug=False):
    nseq = len(seqs)
    Lmax = lmax or (16 + max(seqs))
    nc = bass.Bass("TRN2", target_bir_lowering=False)

    def din(name, shape, dt=F32):
        return nc.dram_tensor(name, list(shape), dt, kind="ExternalInput").ap()

    def dscr(name, shape, dt):
        if debug and not name.startswith("wb_"):
            return nc.dram_tensor(name, list(shape), dt, kind="ExternalOutput").ap()
        return nc.dram_tensor(name, list(shape), dt).ap()

    xs = [din("x%d" % i, [T, D_MODEL]) for i, T in enumerate(seqs)]
    ys = [nc.dram_tensor("y%d" % i, [T, D_MODEL], F32, kind="ExternalOutput").ap() for i, T in enumerate(seqs)]
    meta_d = din("meta", [16, D_MODEL])
    wsrc = {
        "gu1": din("w_gu1", [depth, 1024, 2 * FFN]), "d1": din("w_d1", [depth, FFN, 1024]),
        "gu2": din("w_gu2", [depth, 1024, 2 * FFN]), "d2": din("w_d2", [depth, FFN, 1024]),
        "in": din("w_in", [depth, 1024, 2752]), "uq": din("w_uq", [depth, 768, 1024]),
        "ukv": din("w_ukv", [depth, 256, 1024]), "o": din("w_o", [depth, 1024, 1024]),
    }
    gv_d = din("gv", [128, depth * NG])
    mb_d = din("mb", [16, depth * 4])
    ropeC_d = din("ropeC", [32, Lmax])
    ropeS_d = din("ropeS", [32, Lmax])
    nab_d = din("nab", [depth, 3, 8, 4, 128, 512])
    ident_d = din("ident", [128, 128])

    wb = {k: dscr("wb_" + k, v.shape, BF16) for k, v in wsrc.items()}
    HT = dscr("HT", [1024, Lmax], F32)
    YT = dscr("YT", [1024, Lmax], BF16)
    CB = dscr("CB", [256, Lmax], F32)
    CV = dscr("CV", [256, Lmax + 2], F32)
    NQ = dscr("NQ", [256, Lmax], BF16)
    NK = dscr("NK", [256, Lmax], BF16)
    NV = dscr("NV", [Lmax, 260], BF16)
    QC = dscr("QC", [8, 96, Lmax], BF16)
    KC = dscr("KC", [8, 96, Lmax], BF16)
    MV = dscr("MV", [Lmax, 520], BF16)

    es = contextlib.ExitStack()
    with es:
        P = Prog(nc, es)
        P.cur = 3
        es.enter_context(nc.allow_non_contiguous_dma(reason="single pad columns / small strided tiles"))

        uniq = [0]

        def sb(stack, name, shape, dt):
            uniq[0] += 1
            return Tile(stack.enter_context(nc.sbuf_tensor("s%d_%s" % (uniq[0], name), list(shape), dt)))

        ps_t = es.enter_context(nc.psum_tensor("ps", [128, 8, 512], F32))
        banks = [Tile(ps_t[:, i, :]) for i in range(8)]
        gv = sb(es, "gv", [128, depth * NG], F32)
        mb = sb(es, "mb", [16, depth * 4], F32)
        ident = sb(es, "ident", [128, 128], F32)
        ones = sb(es, "ones", [128, 128], BF16)
        zero = sb(es, "zero", [128, 2], F32)
        P.dma("sp", lambda e: e.dma_start(out=gv[:, :], in_=gv_d[:, :]), writes=[gv.tok])
        P.dma("sp", lambda e: e.dma_start(out=mb[:, :], in_=mb_d[:, :]), writes=[mb.tok])
        P.dma("sp", lambda e: e.dma_start(out=ident[:, :], in_=ident_d[:, :]), writes=[ident.tok])
        P.op("dve", lambda e: e.memset(ones[:, :], 1.0), writes=[ones.tok])
        P.op("dve", lambda e: e.memset(zero[:, :], 0.0), writes=[zero.tok])

        def gcol(l, base):
            return gv[:, l * NG + base:l * NG + base + 1]

        def mm(out, lhsT, rhs, start, stop, reads, writes, contig=True):
            P.op("pe", lambda e: e.matmul(out, lhsT=lhsT, rhs=rhs, start=start, stop=stop), reads, writes,
                 signal=(stop or not contig))

        def act(out, in_, func, reads, writes, scale=1.0, bias=0.0, accum=None):
            if accum is None:
                P.op("act", lambda e: e.activation(out=out, in_=in_, func=func, scale=scale, bias=bias),
                     reads, writes)
            else:
                P.op("act", lambda e: e.activation(out=out, in_=in_, func=func, scale=scale, bias=bias,
                                                   accum_out=accum), reads, writes)

        def stt(out, in0, scalar, in1, op0, op1, reads, writes):
            P.op("dve", lambda e: e.scalar_tensor_tensor(out=out, in0=in0, scalar=scalar, in1=in1,
                                                         op0=op0, op1=op1), reads, writes)

        def tt(out, in0, in1, op, reads, writes, eng="dve"):
            P.op(eng, lambda e: e.tensor_tensor(out=out, in0=in0, in1=in1, op=op), reads, writes)

        def ts(out, in0, s1, s2, op0, op1, reads, writes):
            if s2 is None:
                P.op("dve", lambda e: e.tensor_scalar(out=out, in0=in0, scalar1=s1, scalar2=None, op0=op0),
                     reads, writes)
            else:
                P.op("dve", lambda e: e.tensor_scalar(out=out, in0=in0, scalar1=s1, scalar2=s2, op0=op0, op1=op1),
                     reads, writes)

        def recip(out, in_, reads, writes):
            P.op("dve", lambda e: e.reciprocal(out=out, in_=in_), reads, writes)

        def load(out, in_, writes, reads=()):
            P.dma("sp", lambda e: e.dma_start(out=out, in_=in_), reads, writes)

        def store(out, in_, reads):
            P.dma("pool", lambda e: e.dma_start(out=out, in_=in_), reads, ())

        with contextlib.ExitStack() as ps_es:
            CW = 2048
            fin = [sb(ps_es, "fin%d" % i, [128, CW], F32) for i in range(3)]
            fout = [sb(ps_es, "fout%d" % i, [128, CW], BF16) for i in range(3)]
            ci = 0
            for key in ("gu1", "d1", "in", "uq", "ukv", "o", "gu2", "d2"):
                src = wsrc[key]
                dst = wb[key]
                _, R, N = src.shape
                for l in range(depth):
                    for rc in range(R // 128):
                        for c0 in range(0, N, CW):
                            n = min(CW, N - c0)
                            a, b = fin[ci % 3], fout[ci % 3]
                            load(a[:, :n], src[l, rc * 128:(rc + 1) * 128, c0:c0 + n], [a.tok])
                            eng = ("pool", "dve", "act")[ci % 3]
                            if eng == "act":
                                P.op("act", lambda e, a=a, b=b, n=n: e.copy(out=b[:, :n], in_=a[:, :n]),
                                     [a.tok], [b.tok])
                            else:
                                P.op(eng, lambda e, a=a, b=b, n=n: e.tensor_copy(out=b[:, :n], in_=a[:, :n]),
                                     [a.tok], [b.tok])
                            store(dst[l, rc * 128:(rc + 1) * 128, c0:c0 + n], b[:, :n], [b.tok])
                            ci += 1
        P.barrier(next_set=0)

        bank_rr = [0]

        def bank(pool=(0, 1, 2, 3, 4, 5, 6)):
            b = banks[pool[bank_rr[0] % len(pool)]]
            bank_rr[0] += 1
            return b

        for si, T in enumerate(seqs):
            L = 16 + T
            if si > 0:
                P.barrier(next_set=si)
            blocks = [(0, 16)] + [(16 + 512 * i, 512) for i in range(T // 512)]
            HTv = HT.rearrange("(c p) l -> p c l", p=128)
            YTv = YT.rearrange("(c p) l -> p c l", p=128)

            with contextlib.ExitStack() as st:
                xt = [sb(st, "xt%d" % i, [128, 1024], F32) for i in range(2)]
                hb = [sb(st, "hb%d" % i, [128, 8, 512], F32) for i in range(2)]
                ti = 0
                for bi, (c0, nb) in enumerate(blocks):
                    h = hb[bi % 2]
                    for j in range((nb + 127) // 128):
                        qs = min(128, nb)
                        x_t = xt[ti % 2]
                        ti += 1
                        if bi == 0:
                            load(x_t[:qs, :], meta_d[:, :], [x_t.tok])
                        else:
                            t0 = c0 - 16 + j * 128
                            load(x_t[:qs, :], xs[si][t0:t0 + qs, :], [x_t.tok])
                        for c in range(8):
                            b = bank()
                            P.op("pe", lambda e, b=b, x_t=x_t, c=c, qs=qs: e.transpose(
                                out=b[:, :qs], in_=x_t[:qs, c * 128:(c + 1) * 128], identity=ident[:qs, :qs]),
                                [x_t.tok, ident.tok], [b.tok])
                            if c % 2 == 0:
                                P.op("dve", lambda e, b=b, h=h, c=c, j=j, qs=qs: e.tensor_copy(
                                    out=h[:, c, j * 128:j * 128 + qs], in_=b[:, :qs]), [b.tok], [h.tok])
                            else:
                                P.op("act", lambda e, b=b, h=h, c=c, j=j, qs=qs: e.copy(
                                    out=h[:, c, j * 128:j * 128 + qs], in_=b[:, :qs]), [b.tok], [h.tok])
                    store(HTv[:, :, c0:c0 + nb], h[:, :, :nb], [h.tok])
                CVv = CV.rearrange("(c p) l -> p c l", p=128)
                for c in range(2):
                    store(CVv[:, c, 0:1], zero[:, 0:1], [zero.tok])
                    store(CVv[:, c, L + 1:L + 2], zero[:, 1:2], [zero.tok])
            P.barrier()

            for l in range(depth + 1):
                with contextlib.ExitStack() as st:
                    hblk = sb(st, "hblk", [128, 8, 512], F32)
                    xn = sb(st, "xn", [128, 8, 512], BF16)
                    xn_tok = [Tok() for _ in range(8)]
                    actT = sb(st, "actT", [128, 22, 512], BF16)
                    actT_tok = [Tok() for _ in range(22)]
                    obuf = sb(st, "obuf", [128, 8, 512], F32)
                    obuf_tok = [Tok() for _ in range(8)]
                    ring = [sb(st, "ring%d" % i, [128, 4096], BF16) for i in range(6)]
                    ring_i = [0]
                    sqb = [sb(st, "sq%d" % i, [128, 512], BF16) for i in range(3)]
                    sq_i = [0]
                    rsb = [sb(st, "rs%d" % i, [128, 512], F32) for i in range(2)]
                    rs_i = [0]
                    sgb = [sb(st, "sg%d" % i, [128, 512], F32) for i in range(2)]
                    sg_i = [0]
                    ytb = sb(st, "ytb", [128, 8, 512], BF16)
                    qn = sb(st, "qn", [128, 6, 512], BF16)
                    kvn = sb(st, "kvn", [128, 2, 512], BF16)
                    cct = sb(st, "cct", [128, 2, 512], F32)
                    cbt = sb(st, "cbt", [128, 2, 512], F32)
                    cvt = sb(st, "cvt", [128, 2, 512], F32)
                    nqt = sb(st, "nqt", [128, 2, 512], BF16)
                    nkt = sb(st, "nkt", [128, 2, 512], BF16)
                    nvt = sb(st, "nvt", [128, 4, 260], BF16)
                    qct = sb(st, "qct", [96, 8, 512], BF16)
                    kct = sb(st, "kct", [96, 8, 512], BF16)
                    mvt = sb(st, "mvt", [128, 4, 520], BF16)
                    kper = sb(st, "kper", [96, 512], BF16)
                    rt1 = sb(st, "rt1", [96, 512], F32)
                    rt2 = sb(st, "rt2", [96, 512], F32)
                    rC = sb(st, "rC", [96, 512], F32)
                    rS = sb(st, "rS", [96, 512], F32)
                    yout = [sb(st, "yout%d" % i, [128, 1024], F32) for i in range(2)]
                    P.op("pool", lambda e, nvt=nvt: e.memset(nvt[:, :, :], 1.0), writes=[nvt.tok])
                    P.op("pool", lambda e, mvt=mvt: e.memset(mvt[:, :, :], 1.0), writes=[mvt.tok])

                    def load_w(W, KC_, c0, ncols, r0=0):
                        s = ring[ring_i[0] % len(ring)]
                        ring_i[0] += 1
                        view = s.t[:, 0:KC_ * ncols].rearrange("p (c n) -> p c n", c=KC_)
                        src = W.rearrange("(c p) n -> p c n", p=128)[:, r0:r0 + KC_, c0:c0 + ncols]
                        load(view, src, [s.tok])
                        return view, s.tok

                    def next_sq():
                        s = sqb[sq_i[0] % 3]
                        sq_i[0] += 1
                        return s

                    def stat_mm(sq, nb, first, last):
                        mm(banks[7][:, :nb], ones[:, :], sq[:, :nb], first, last, [sq.tok, ones.tok], [banks[7].tok],
                           contig=False)

                    def rstd_from_stat(nb, D):
                        r = rsb[rs_i[0] % 2]
                        rs_i[0] += 1
                        act(r[:, :nb], banks[7][:, :nb], AF.Sqrt, [banks[7].tok], [r.tok], scale=1.0 / D, bias=EPS)
                        recip(r[:, :nb], r[:, :nb], [r.tok], [r.tok])
                        return r

                    def rms_sbuf(srcs, nb, D):
                        n = len(srcs)
                        for i, (ap, tk) in enumerate(srcs):
                            s = next_sq()
                            act(s[:, :nb], ap, AF.Square, [tk], [s.tok])
                            stat_mm(s, nb, i == 0, i == n - 1)
                        return rstd_from_stat(nb, D)

                    def post_chunk(dc, b, nb):
                        act(obuf[:, dc, :nb], b[:, :nb], AF.Copy, [b.tok], [obuf_tok[dc]])
                        s = next_sq()
                        act(s[:, :nb], b[:, :nb], AF.Square, [b.tok], [s.tok])
                        stat_mm(s, nb, dc == 0, dc == 7)

                    def post_end(l_, gbase, coef, nb):
                        r = rstd_from_stat(nb, 1024)
                        for dc in range(8):
                            stt(obuf[:, dc, :nb], obuf[:, dc, :nb], gcol(l_, gbase + dc), r[:, :nb], ALU.mult, ALU.mult,
                                [obuf_tok[dc], r.tok, gv.tok], [obuf_tok[dc]])
                            stt(hblk[:, dc, :nb], obuf[:, dc, :nb], coef, hblk[:, dc, :nb], ALU.mult, ALU.add,
                                [obuf_tok[dc], hblk.tok], [hblk.tok])

                    def make_xn(l_, gbase, nb):
                        r = rms_sbuf([(hblk[:, c, :nb], hblk.tok) for c in range(8)], nb, 1024)
                        for c in range(8):
                            stt(xn[:, c, :nb], hblk[:, c, :nb], gcol(l_, gbase + c), r[:, :nb], ALU.mult, ALU.mult,
                                [hblk.tok, r.tok, gv.tok], [xn_tok[c]])

                    def ffn(l_, which, nb):
                        make_xn(l_, G_F1PRE if which == 1 else G_F2PRE, nb)
                        Wgu = wb["gu%d" % which][l_]
                        Wd = wb["d%d" % which][l_]
                        for fg in range(6):
                            f0 = fg * 512
                            nf = min(512, FFN - f0)
                            wg, wg_tok = load_w(Wgu, 8, f0, nf)
                            wu, wu_tok = load_w(Wgu, 8, FFN + f0, nf)
                            for jj in range(nf // 128):
                                j = fg * 4 + jj
                                bg = bank()
                                bu = bank()
                                for kc in range(8):
                                    mm(bg[:, :nb], wg[:, kc, jj * 128:(jj + 1) * 128], xn[:, kc, :nb], kc == 0, kc == 7,
                                       [wg_tok, xn_tok[kc]], [bg.tok])
                                for kc in range(8):
                                    mm(bu[:, :nb], wu[:, kc, jj * 128:(jj + 1) * 128], xn[:, kc, :nb], kc == 0, kc == 7,
                                       [wu_tok, xn_tok[kc]], [bu.tok])
                                s = sgb[sg_i[0] % 2]
                                sg_i[0] += 1
                                act(s[:, :nb], bg[:, :nb], AF.Silu, [bg.tok], [s.tok])
                                tt(actT[:, j, :nb], s[:, :nb], bu[:, :nb], ALU.mult, [s.tok, bu.tok], [actT_tok[j]])
                        parts = [(0, 8), (8, 16), (16, 22)]
                        for half in range(2):
                            sl = [load_w(Wd, j1 - j0, half * 512, 512, r0=j0) for (j0, j1) in parts]
                            for dcr in range(4):
                                dc = half * 4 + dcr
                                b = bank()
                                for j in range(22):
                                    pi = 0 if j < 8 else (1 if j < 16 else 2)
                                    w, wt = sl[pi]
                                    mm(b[:, :nb], w[:, j - parts[pi][0], dcr * 128:(dcr + 1) * 128], actT[:, j, :nb],
                                       j == 0, j == 21, [wt, actT_tok[j]], [b.tok])
                                post_chunk(dc, b, nb)
                        post_end(l_, G_F1POST if which == 1 else G_F2POST, 0.5, nb)

                    def mixer_out(l_, c0, nb):
                        load(ytb[:, :, :nb], YTv[:, :, c0:c0 + nb], [ytb.tok])
                        for half in range(2):
                            w, wt = load_w(wb["o"][l_], 8, half * 512, 512)
                            for dcr in range(4):
                                dc = half * 4 + dcr
                                b = bank()
                                for c in range(8):
                                    mm(b[:, :nb], w[:, c, dcr * 128:(dcr + 1) * 128], ytb[:, c, :nb], c == 0, c == 7,
                                       [wt, ytb.tok], [b.tok])
                                post_chunk(dc, b, nb)
                        post_end(l_, G_MPOST, 1.0, nb)

                    def projections(l_, c0, nb):
                        make_xn(l_, G_MPRE, nb)
                        Win = wb["in"][l_]
                        ql = obuf
                        load(rC[64:96, :nb], ropeC_d[:, c0:c0 + nb], [rC.tok])
                        load(rS[64:96, :nb], ropeS_d[:, c0:c0 + nb], [rS.tok])
                        ntile = (nb + 127) // 128
                        qs = min(128, nb)
                        for g in range(6):
                            gc0 = g * 512
                            ncol = min(512, 2752 - gc0)
                            w, wt = load_w(Win, 8, gc0, ncol)
                            if g == 5:
                                bA = bank()
                                bB = bank()
                                for kc in range(8):
                                    mm(bA[:96, :nb], w[:, kc, 0:96], xn[:, kc, :nb], kc == 0, kc == 7, [wt, xn_tok[kc]], [bA.tok])
                                for kc in range(8):
                                    mm(bB[:96, :nb], w[:, kc, 96:192], xn[:, kc, :nb], kc == 0, kc == 7, [wt, xn_tok[kc]], [bB.tok])
                                tt(rt1[64:96, :nb], bA[64:96, :nb], rC[64:96, :nb], ALU.mult, [bA.tok, rC.tok], [rt1.tok])
                                tt(rt2[64:96, :nb], bB[64:96, :nb], rS[64:96, :nb], ALU.mult, [bB.tok, rS.tok], [rt2.tok])
                                tt(kper[64:96, :nb], rt1[64:96, :nb], rt2[64:96, :nb], ALU.add, [rt1.tok, rt2.tok], [kper.tok])
                                continue
                            for oc_r in range(4):
                                oc = g * 4 + oc_r
                                if oc in (10, 11):
                                    continue
                                b = bank()
                                for kc in range(8):
                                    mm(b[:, :nb], w[:, kc, oc_r * 128:(oc_r + 1) * 128], xn[:, kc, :nb], kc == 0, kc == 7,
                                       [wt, xn_tok[kc]], [b.tok])
                                if oc < 2:
                                    act(cbt[:, oc, :nb], b[:, :nb], AF.Copy, [b.tok], [cbt.tok])
                                elif oc < 4:
                                    act(cct[:, oc - 2, :nb], b[:, :nb], AF.Copy, [b.tok], [cct.tok])
                                elif oc < 6:
                                    tt(cvt[:, oc - 4, :nb], cct[:, oc - 4, :nb], b[:, :nb], ALU.mult, [cct.tok, b.tok], [cvt.tok])
                                elif oc < 8:
                                    act(nqt[:, oc - 6, :nb], b[:, :nb], AF.Copy, [b.tok], [nqt.tok])
                                elif oc < 10:
                                    act(nkt[:, oc - 8, :nb], b[:, :nb], AF.Copy, [b.tok], [nkt.tok])
                                else:
                                    act(ql[:, oc - 12, :nb], b[:, :nb], AF.Copy, [b.tok], [obuf_tok[oc - 12]])
                            if g == 2:
                                for j in range(ntile):
                                    b = bank()
                                    for kc in range(8):
                                        mm(b[:qs, 0:256], xn[:, kc, j * 128:j * 128 + qs], w[:, kc, 256:512], kc == 0, kc == 7,
                                           [wt, xn_tok[kc]], [b.tok])
                                    P.op("dve", lambda e, b=b, j=j, nvt=nvt, qs=qs: e.tensor_copy(
                                        out=nvt[:qs, j, :].rearrange("p (h d) -> p h d", h=4)[:, :, 0:64],
                                        in_=b[:qs, 0:256].rearrange("p (h d) -> p h d", h=4)), [b.tok], [nvt.tok])
                        CBv = CB.rearrange("(c p) l -> p c l", p=128)
                        CVv = CV.rearrange("(c p) l -> p c l", p=128)
                        NQv = NQ.rearrange("(c p) l -> p c l", p=128)
                        NKv = NK.rearrange("(c p) l -> p c l", p=128)
                        store(CBv[:, :, c0:c0 + nb], cbt[:, :, :nb], [cbt.tok])
                        store(CVv[:, :, c0 + 1:c0 + 1 + nb], cvt[:, :, :nb], [cvt.tok])
                        store(NQv[:, :, c0:c0 + nb], nqt[:, :, :nb], [nqt.tok])
                        store(NKv[:, :, c0:c0 + nb], nkt[:, :, :nb], [nkt.tok])
                        if nb == 16:
                            store(NV[0:16, :], nvt[:16, 0, :], [nvt.tok])
                        else:
                            store(NV[c0:c0 + nb, :].rearrange("(t p) f -> p t f", p=128), nvt[:, :, :], [nvt.tok])
                        r = rms_sbuf([(ql[:, c, :nb], obuf_tok[c]) for c in range(6)], nb, 768)
                        for c in range(6):
                            stt(qn[:, c, :nb], ql[:, c, :nb], gcol(l_, G_QN + c), r[:, :nb], ALU.mult, ALU.mult,
                                [obuf_tok[c], r.tok, gv.tok], [qn.tok])
                        for hg in range(2):
                            w, wt = load_w(wb["uq"][l_], 6, hg * 512, 512)
                            for hr in range(4):
                                h = hg * 4 + hr
                                bm = bank()
                                bs = bank()
                                for c in range(6):
                                    mm(bm[:96, :nb], w[:, c, hr * 128:hr * 128 + 96], qn[:, c, :nb], c == 0, c == 5,
                                       [wt, qn.tok], [bm.tok])
                                for c in range(6):
                                    mm(bs[:96, :nb], w[:, c, hr * 128 + 32:hr * 128 + 128], qn[:, c, :nb], c == 0, c == 5,
                                       [wt, qn.tok], [bs.tok])
                                tt(rt1[64:96, :nb], bm[64:96, :nb], rC[64:96, :nb], ALU.mult, [bm.tok, rC.tok], [rt1.tok])
                                tt(rt2[64:96, :nb], bs[64:96, :nb], rS[64:96, :nb], ALU.mult, [bs.tok, rS.tok], [rt2.tok])
                                tt(qct[64:96, h, :nb], rt1[64:96, :nb], rt2[64:96, :nb], ALU.add, [rt1.tok, rt2.tok], [qct.tok])
                                act(qct[0:64, h, :nb], bm[0:64, :nb], AF.Copy, [bm.tok], [qct.tok])
                        store(QC.rearrange("h p l -> p h l")[:, :, c0:c0 + nb], qct[:, :, :nb], [qct.tok])
                        r = rms_sbuf([(ql[:, 6 + c, :nb], obuf_tok[6 + c]) for c in range(2)], nb, 256)
                        for c in range(2):
                            stt(kvn[:, c, :nb], ql[:, 6 + c, :nb], gcol(l_, G_KVN + c), r[:, :nb], ALU.mult, ALU.mult,
                                [obuf_tok[6 + c], r.tok, gv.tok], [kvn.tok])
                        w, wt = load_w(wb["ukv"][l_], 2, 0, 1024)
                        for h in range(8):
                            b = bank()
                            for c in range(2):
                                mm(b[:64, :nb], w[:, c, h * 128:h * 128 + 64], kvn[:, c, :nb], c == 0, c == 1,
                                   [wt, kvn.tok], [b.tok])
                            act(kct[0:64, h, :nb], b[0:64, :nb], AF.Copy, [b.tok], [kct.tok])
                            P.op("pool", lambda e, h=h, kct=kct, kper=kper, nb=nb: e.tensor_copy(out=kct[64:96, h, :nb], in_=kper[64:96, :nb]),
                                 [kper.tok], [kct.tok])
                        store(KC.rearrange("h p l -> p h l")[:, :, c0:c0 + nb], kct[:, :, :nb], [kct.tok])
                        wv = w.rearrange("p c (h d) -> p c h d", h=8)[:, :, :, 64:128]
                        for j in range(ntile):
                            b = bank()
                            for c in range(2):
                                mm(b[:qs, :].rearrange("p (h d) -> p h d", h=8), kvn[:, c, j * 128:j * 128 + qs], wv[:, c],
                                   c == 0, c == 1, [wt, kvn.tok], [b.tok])
                            P.op("dve", lambda e, b=b, j=j, mvt=mvt, qs=qs: e.tensor_copy(
                                out=mvt[:qs, j, :].rearrange("p (h d) -> p h d", h=8)[:, :, 0:64],
                                in_=b[:qs, :].rearrange("p (h d) -> p h d", h=8)), [b.tok], [mvt.tok])
                        if nb == 16:
                            store(MV[0:16, :], mvt[:16, 0, :], [mvt.tok])
                        else:
                            store(MV[c0:c0 + nb, :].rearrange("(t p) f -> p t f", p=128), mvt[:, :, :], [mvt.tok])

                    def final_out(c0, nb):
                        for j in range(nb // 128):
                            yo = yout[j % 2]
                            for half in range(2):
                                b = bank()
                                for cr in range(4):
                                    c = half * 4 + cr
                                    P.op("pe", lambda e, b=b, c=c, cr=cr, j=j, hblk=hblk: e.transpose(
                                        out=b[:, cr * 128:(cr + 1) * 128], in_=hblk[:, c, j * 128:(j + 1) * 128],
                                        identity=ident[:, :]), [hblk.tok, ident.tok], [b.tok])
                                if half == 0:
                                    P.op("dve", lambda e, b=b, yo=yo: e.tensor_copy(out=yo[:, 0:512], in_=b[:, :]),
                                         [b.tok], [yo.tok])
                                else:
                                    P.op("act", lambda e, b=b, yo=yo: e.copy(out=yo[:, 512:1024], in_=b[:, :]),
                                         [b.tok], [yo.tok])
                            t0 = c0 - 16 + j * 128
                            store(ys[si][t0:t0 + 128, :], yo[:, :], [yo.tok])

                    for (c0, nb) in blocks:
                        if l == depth and nb == 16:
                            continue
                        load(hblk[:, :, :nb], HTv[:, :, c0:c0 + nb], [hblk.tok])
                        if l > 0:
                            mixer_out(l - 1, c0, nb)
                            ffn(l - 1, 2, nb)
                        if l < depth:
                            ffn(l, 1, nb)
                            store(HTv[:, :, c0:c0 + nb], hblk[:, :, :nb], [hblk.tok])
                            projections(l, c0, nb)
                        else:
                            final_out(c0, nb)
                P.barrier()
                if l == depth:
                    break

                with contextlib.ExitStack() as st:
                    ntile_k = 1 + T // 128
                    kcsb = sb(st, "kcsb", [96, 8, L], BF16)
                    mvsb = sb(st, "mvsb", [128, ntile_k, 520], BF16)
                    qcb = [sb(st, "qcb%d" % i, [96, 8, 512], BF16) for i in range(2)]
                    ptb = [sb(st, "pt%d" % i, [128, 512], BF16) for i in range(3)]
                    pt_i = [0]
                    nkw = sb(st, "nkw", [128, 2, 16 + 1024], BF16)
                    nvw = sb(st, "nvw", [128, 9, 260], BF16)
                    nqb = sb(st, "nqb", [128, 2, 512], BF16)
                    nbias = [sb(st, "nbias%d" % i, [128, 512], F32) for i in range(3)]
                    nb_i = [0]
                    sbb = [sb(st, "sbb%d" % i, [128, 512], F32) for i in range(2)]
                    sb_i = [0]
                    ym = sb(st, "ym", [128, 4, 768], F32)
                    ym_tok = [Tok() for _ in range(4)]
                    yt = sb(st, "yt", [128, 8, 512], BF16)
                    cvh = sb(st, "cvh", [128, 2, 514], F32)
                    cbb = sb(st, "cbb", [128, 2, 512], F32)
                    ctmp = sb(st, "ctmp", [128, 2, 512], F32)
                    csq = sb(st, "csq", [128, 2, 512], BF16)
                    crs = sb(st, "crs", [128, 512], F32)
                    junk = sb(st, "junk", [128, 512], F32)
                    ssm = [sb(st, "ssm%d" % i, [128, 4], F32) for i in range(2)]
                    ss_i = [0]
                    rdt = [sb(st, "rdt%d" % i, [128, 1], F32) for i in range(4)]
                    rd_i = [0]

                    for h in range(8):
                        load(kcsb[:, h, :], KC[h, :, 0:L], [kcsb.tok])
                    load(mvsb[:16, 0, :], MV[0:16, :], [mvsb.tok])
                    load(mvsb[:, 1:, :], MV[16:16 + T, :].rearrange("(t p) f -> p t f", p=128), [mvsb.tok])
                    obanks = banks[0:4]
                    SB = (4, 5, 6)

                    def attention(nq, nheads, units, qk_ops, vsrc, ycol0, scale, post):
                        qtiles = (nq + 127) // 128
                        qs = min(128, nq)
                        for h in range(nheads):
                            n = len(units)

                            def qk(u):
                                b = bank(SB)
                                lhsT, rhs, rd = qk_ops(h, u)
                                mm(b[:units[u]["nk"], :nq], lhsT, rhs, True, True, rd, [b.tok])
                                return b

                            def pv(u, b):
                                nk = units[u]["nk"]
                                pt = post(h, u, b)
                                vap, vtok = vsrc(h, u)
                                for j in range(qtiles):
                                    mm(obanks[j][:qs, 0:65], pt[:nk, j * 128:j * 128 + qs], vap, u == 0, u == n - 1,
                                       [pt.tok, vtok], [obanks[j].tok], contig=False)

                            bcur = qk(0)
                            for u in range(n):
                                bnext = qk(u + 1) if u + 1 < n else None
                                pv(u, bcur)
                                bcur = bnext
                            for j in range(qtiles):
                                rd = rdt[rd_i[0] % 4]
                                rd_i[0] += 1
                                recip(rd[:qs, :], obanks[j][:qs, 64:65], [obanks[j].tok], [rd.tok])
                                ts(ym[:qs, j, ycol0 + h * 64:ycol0 + (h + 1) * 64], obanks[j][:qs, 0:64], rd[:qs, 0:1], None,
                                   ALU.mult, None, [obanks[j].tok, rd.tok], [ym_tok[j]])

                    nblk = T // 512
                    NQv = NQ.rearrange("(c p) l -> p c l", p=128)
                    NKv = NK.rearrange("(c p) l -> p c l", p=128)
                    CBv = CB.rearrange("(c p) l -> p c l", p=128)
                    CVv = CV.rearrange("(c p) l -> p c l", p=128)
                    for bi, (c0, nb) in enumerate(blocks):
                        qtiles = (nb + 127) // 128
                        qs = min(128, nb)
                        qc = qcb[bi % 2]
                        load(qc[:, :, :nb], QC.rearrange("h p l -> p h l")[:, :, c0:c0 + nb], [qc.tok])
                        m_units = [dict(nk=16, k0=0, vt=0)] + [dict(nk=128, k0=16 + 128 * t, vt=1 + t) for t in range(T // 128)]

                        def m_qk(h, u, qc=qc, nb=nb):
                            un = m_units[u]
                            return (kcsb[:, h, un["k0"]:un["k0"] + un["nk"]], qc[:, h, :nb], [kcsb.tok, qc.tok])

                        def m_post(h, u, b, nb=nb):
                            nk = m_units[u]["nk"]
                            pt = ptb[pt_i[0] % 3]
                            pt_i[0] += 1
                            act(pt[:nk, :nb], b[:nk, :nb], AF.Exp, [b.tok], [pt.tok], scale=MLA_SCALE)
                            return pt

                        def m_v(h, u):
                            un = m_units[u]
                            return mvsb[:un["nk"], un["vt"], h * 65:(h + 1) * 65], mvsb.tok

                        attention(nb, 8, m_units, m_qk, m_v, 256, MLA_SCALE, m_post)

                        load(nqb[:, :, :nb], NQv[:, :, c0:c0 + nb], [nqb.tok])
                        load(nkw[:, :, 0:16], NKv[:, :, 0:16], [nkw.tok])
                        load(nvw[:16, 0, :], NV[0:16, :], [nvw.tok])
                        n_units = [dict(nk=16, k0=0, vt=0, bias=None)]
                        if bi > 0:
                            i = bi - 1
                            cls = 0 if i == 0 else (2 if i == nblk - 1 else 1)
                            kt_lo = 4 * i + (0 if cls == 0 else -2)
                            nkt = 8 if cls == 1 else 6
                            kcol0 = 16 + 128 * kt_lo
                            load(nkw[:, :, 16:16 + 128 * nkt], NKv[:, :, kcol0:kcol0 + 128 * nkt], [nkw.tok])
                            load(nvw[:, 1:1 + nkt, :], NV[kcol0:kcol0 + 128 * nkt, :].rearrange("(t p) f -> p t f", p=128), [nvw.tok])
                            for kr in range(nkt):
                                n_units.append(dict(nk=128, k0=16 + 128 * kr, vt=1 + kr, bias=(cls, kr)))

                        def n_qk(h, u, nb=nb):
                            un = n_units[u]
                            po = 64 * (h % 2)
                            return (nkw[po:po + 64, h // 2, un["k0"]:un["k0"] + un["nk"]], nqb[po:po + 64, h // 2, :nb],
                                    [nkw.tok, nqb.tok])

                        def n_post(h, u, b, nb=nb):
                            un = n_units[u]
                            nk = un["nk"]
                            pt = ptb[pt_i[0] % 3]
                            pt_i[0] += 1
                            if un["bias"] is None:
                                act(pt[:nk, :nb], b[:nk, :nb], AF.Exp, [b.tok, mb.tok], [pt.tok], scale=NA_SCALE,
                                    bias=mb[:, l * 4 + h:l * 4 + h + 1])
                            else:
                                cls_, kr = un["bias"]
                                bt = nbias[nb_i[0] % 3]
                                nb_i[0] += 1
                                load(bt[:, :], nab_d[l, cls_, kr, h], [bt.tok])
                                s = sbb[sb_i[0] % 2]
                                sb_i[0] += 1
                                stt(s[:, :nb], b[:, :nb], NA_SCALE, bt[:, :nb], ALU.mult, ALU.add, [b.tok, bt.tok], [s.tok])
                                act(pt[:nk, :nb], s[:nk, :nb], AF.Exp, [s.tok], [pt.tok])
                            return pt

                        def n_v(h, u):
                            un = n_units[u]
                            return nvw[:un["nk"], un["vt"], h * 65:(h + 1) * 65], nvw.tok

                        attention(nb, 4, n_units, n_qk, n_v, 0, NA_SCALE, n_post)

                        for j in range(qtiles):
                            ss = ssm[ss_i[0] % 2]
                            ss_i[0] += 1
                            act(junk[:qs, 0:256], ym[:qs, j, 0:256], AF.Square, [ym_tok[j]], [junk.tok, ss.tok], accum=ss[:qs, 0:1])
                            act(junk[:qs, 0:512], ym[:qs, j, 256:768], AF.Square, [ym_tok[j]], [junk.tok, ss.tok], accum=ss[:qs, 1:2])
                            act(ss[:qs, 2:3], ss[:qs, 0:1], AF.Sqrt, [ss.tok], [ss.tok], scale=1.0 / 256, bias=EPS)
                            act(ss[:qs, 3:4], ss[:qs, 1:2], AF.Sqrt, [ss.tok], [ss.tok], scale=1.0 / 512, bias=EPS)
                            recip(ss[:qs, 2:4], ss[:qs, 2:4], [ss.tok], [ss.tok])
                            ts(ym[:qs, j, 0:256], ym[:qs, j, 0:256], ss[:qs, 2:3], None, ALU.mult, None, [ym_tok[j], ss.tok], [ym_tok[j]])
                            ts(ym[:qs, j, 256:768], ym[:qs, j, 256:768], ss[:qs, 3:4], None, ALU.mult, None, [ym_tok[j], ss.tok], [ym_tok[j]])
                            for c in range(6):
                                b = bank((4, 5, 6, 7))
                                P.op("pe", lambda e, b=b, c=c, j=j, qs=qs, ym=ym: e.transpose(
                                    out=b[:, :qs], in_=ym[:qs, j, c * 128:(c + 1) * 128], identity=ident[:qs, :qs]),
                                    [ym_tok[j], ident.tok], [b.tok])
                                gb = (G_NON + c) if c < 2 else (G_MON + c - 2)
                                act(yt[:, 2 + c, j * 128:j * 128 + qs], b[:, :qs], AF.Copy, [b.tok, gv.tok], [yt.tok],
                                    scale=gcol(l, gb))

                        load(cvh[:, :, 0:nb + 2], CVv[:, :, c0:c0 + nb + 2], [cvh.tok])
                        load(cbb[:, :, :nb], CBv[:, :, c0:c0 + nb], [cbb.tok])
                        for c in range(2):
                            ts(ctmp[:, c, :nb], cvh[:, c, 0:nb], gcol(l, G_CONVW + 0 + c), None, ALU.mult, None,
                               [cvh.tok, gv.tok], [ctmp.tok])
                            stt(ctmp[:, c, :nb], cvh[:, c, 1:nb + 1], gcol(l, G_CONVW + 2 + c), ctmp[:, c, :nb], ALU.mult, ALU.add,
                                [cvh.tok, ctmp.tok, gv.tok], [ctmp.tok])
                            stt(ctmp[:, c, :nb], cvh[:, c, 2:nb + 2], gcol(l, G_CONVW + 4 + c), ctmp[:, c, :nb], ALU.mult, ALU.add,
                                [cvh.tok, ctmp.tok, gv.tok], [ctmp.tok])
                            tt(ctmp[:, c, :nb], ctmp[:, c, :nb], cbb[:, c, :nb], ALU.mult, [ctmp.tok, cbb.tok], [ctmp.tok])
                            act(csq[:, c, :nb], ctmp[:, c, :nb], AF.Square, [ctmp.tok], [csq.tok])
                        for c in range(2):
                            mm(banks[7][:, :nb], ones[:, :], csq[:, c, :nb], c == 0, c == 1, [csq.tok, ones.tok], [banks[7].tok],
                               contig=False)
                        act(crs[:, :nb], banks[7][:, :nb], AF.Sqrt, [banks[7].tok], [crs.tok], scale=1.0 / 256, bias=EPS)
                        recip(crs[:, :nb], crs[:, :nb], [crs.tok], [crs.tok])
                        for c in range(2):
                            stt(yt[:, c, :nb], ctmp[:, c, :nb], gcol(l, G_CON + c), crs[:, :nb], ALU.mult, ALU.mult,
                                [ctmp.tok, crs.tok, gv.tok], [yt.tok])
                        store(YTv[:, :, c0:c0 + nb], yt[:, :, :nb], [yt.tok])
                P.barrier()

        P.barrier()
        P.run()
    return nc


def _na_tables(rpb):
    D = rpb.shape[0]
    R = 32
    out = np.full((D, 3, 8, 4, 128, 512), NEG, dtype=np.float32)
    qc = np.arange(64)
    kc = np.arange(64)
    cs = np.clip(qc - 8, 0, 48)
    col_valid = (kc[:, None] >= cs[None, :]) & (kc[:, None] < cs[None, :] + 16)
    col_rel = np.clip(kc[:, None] - qc[None, :] + 15, 0, 30)
    for cls, (r0, kt_lo, nkt) in enumerate(((0, 0, 6), (8, 2, 8), (24, 10, 6))):
        for kti in range(nkt):
            for p in range(2):
                kr = 2 * (kt_lo + kti) + p
                for j in range(8):
                    r = r0 + j
                    rs = min(max(r - 4, 0), R - 8)
                    if not (rs <= kr < rs + 8):
                        continue
                    row_rel = kr - r + 7
                    vals = rpb[:, :, row_rel, :][:, :, col_rel]
                    vals = np.where(col_valid[None, None], vals, np.float32(NEG))
                    out[:, cls, kti, :, p * 64:(p + 1) * 64, j * 64:(j + 1) * 64] = vals
    return out


def _prep_shared(inp, depth, Lmax):
    f = lambda a: np.ascontiguousarray(np.asarray(a, dtype=np.float32))
    w_in = f(inp["w_in"])
    pad64 = np.zeros((depth, 1024, 64), np.float32)
    w_in_ext = np.concatenate([w_in[:, :, :2560], pad64, w_in[:, :, 2560:2592], pad64,
                               w_in[:, :, 2576:2592], w_in[:, :, 2560:2576]], axis=2)
    w_uq = f(inp["mla_w_uq"]).reshape(depth, 768, 8, 96)
    w_uq_ext = np.concatenate([w_uq[..., 0:64], w_uq[..., 64:96], w_uq[..., 80:96], w_uq[..., 64:80]], axis=3)
    w_uq_ext = w_uq_ext.reshape(depth, 768, 1024)
    w_ukv_ext = f(inp["mla_w_ukv"])

    def cols(v):
        v = f(v)
        return v.reshape(depth, -1, 128).transpose(2, 0, 1)

    gparts = [cols(inp[k]) for k in ("ffn1_pre_norm", "ffn1_post_norm", "mix_pre_norm", "mla_q_norm", "mla_kv_norm",
                                     "conv_out_norm", "na_out_norm", "mla_out_norm", "mix_post_norm",
                                     "ffn2_pre_norm", "ffn2_post_norm")]
    cw = f(inp["conv_w"]).reshape(depth, 3, 2, 128).transpose(3, 0, 1, 2).reshape(128, depth, 6)
    gv = np.concatenate(gparts + [cw], axis=2)
    assert gv.shape == (128, depth, NG), gv.shape
    gv = np.ascontiguousarray(gv.reshape(128, depth * NG))
    mb = np.ascontiguousarray(f(inp["na_meta_bias"]).transpose(2, 0, 1).reshape(16, depth * 4))
    pos = np.arange(Lmax, dtype=np.float32)
    inv_freq = (np.float32(10000.0) ** (-np.arange(16, dtype=np.float32) / np.float32(16))).astype(np.float32)
    ang = (pos[None, :] * inv_freq[:, None]).astype(np.float32)
    cos = np.cos(ang).astype(np.float32)
    sin = np.sin(ang).astype(np.float32)
    ropeC = np.ascontiguousarray(np.concatenate([cos, cos], axis=0))
    ropeS = np.ascontiguousarray(np.concatenate([-sin, sin], axis=0))
    return {
        "meta": f(inp["meta_tokens"]),
        "w_gu1": f(inp["ffn1_w_gu"]), "w_d1": f(inp["ffn1_w_down"]),
        "w_gu2": f(inp["ffn2_w_gu"]), "w_d2": f(inp["ffn2_w_down"]),
        "w_in": np.ascontiguousarray(w_in_ext), "w_uq": np.ascontiguousarray(w_uq_ext),
        "w_ukv": np.ascontiguousarray(w_ukv_ext), "w_o": f(inp["w_o"]),
        "gv": gv, "mb": mb, "ropeC": ropeC, "ropeS": ropeS,
        "nab": _na_tables(f(inp["na_rpb"])), "ident": np.eye(128, dtype=np.float32),
    }


def kernel(**inputs):
    depth = 4
    ncores = 8
    xp = np.asarray(inputs["x_prompt"], dtype=np.float32)
    xsmp = np.asarray(inputs["x_sample"], dtype=np.float32)
    seqs = [xp.shape[1], xp.shape[1], xsmp.shape[1]]
    Lmax = 16 + max(seqs)
    shared = _prep_shared(inputs, depth, Lmax)
    nc = build(seqs, depth)
    in_maps = []
    for c in range(ncores):
        m = dict(shared)
        m["x0"] = np.ascontiguousarray(xp[2 * c])
        m["x1"] = np.ascontiguousarray(xp[2 * c + 1])
        m["x2"] = np.ascontiguousarray(xsmp[c])
        in_maps.append(m)
    res = run_bass_kernel_spmd(nc, in_maps, core_ids=list(range(ncores)))
    yp = np.empty_like(xp)
    ysm = np.empty_like(xsmp)
    for c in range(ncores):
        r = res.results[c]
        yp[2 * c] = r["y0"]
        yp[2 * c + 1] = r["y1"]
        ysm[c] = r["y2"]
    return (yp, ysm)
```

```python
import contextlib
import numpy as np
import concourse.bass as bass
import concourse.mybir as mybir
from concourse.bass_utils import run_bass_kernel_spmd

F32 = mybir.dt.float32
BF16 = mybir.dt.bfloat16
AF = mybir.ActivationFunctionType
ALU = mybir.AluOpType

D_MODEL = 1024
FFN = 2816
NG = 70
EPS = 1e-6
NEG = -30000.0
MLA_SCALE = 96.0 ** -0.5
NA_SCALE = 0.125
G_F1PRE, G_F1POST, G_MPRE, G_QN, G_KVN, G_CON, G_NON, G_MON, G_MPOST, G_F2PRE, G_F2POST, G_CONVW = (
    0, 8, 16, 24, 30, 32, 34, 36, 40, 48, 56, 64)


ALL_TOKS = []


class Tok:
    __slots__ = ("w", "r")

    def __init__(self):
        self.w = None
        self.r = {}
        ALL_TOKS.append(self)


class Tile:
    def __init__(self, t, tok=None):
        self.t = t
        self.tok = tok or Tok()

    def __getitem__(self, idx):
        return self.t[idx]


class Prog:
    CE = ("pe", "act", "dve", "pool")
    ENG = ("pe", "act", "dve", "pool", "sp")
    NSETS = 4

    def __init__(self, nc, es, n_dma=40):
        self.nc = nc
        self.psem = {(e, s): es.enter_context(nc.semaphore("p_%s%d" % (e, s))) for e in self.CE for s in range(self.NSETS)}
        self.pcnt = {(e, s): 0 for e in self.CE for s in range(self.NSETS)}
        self.cur = 0
        self.dsem = [es.enter_context(nc.semaphore("d%d" % i)) for i in range(n_dma)]
        self.dcnt = [0] * n_dma
        self.dnext = 0
        self.seen = {e: {} for e in self.ENG}
        self.streams = {e: [] for e in self.ENG}
        self.nops = 0

    def _deps(self, reads, writes):
        d = {}
        for t in reads:
            if t.w is not None:
                k, v = t.w
                if d.get(k, 0) < v:
                    d[k] = v
        for t in writes:
            if t.w is not None:
                k, v = t.w
                if d.get(k, 0) < v:
                    d[k] = v
            for k, v in t.r.items():
                if d.get(k, 0) < v:
                    d[k] = v
        return d

    def _mark(self, reads, writes, ev):
        for t in writes:
            t.w = ev
            t.r = {}
        k, v = ev
        for t in reads:
            t.r[k] = v

    def _emit(self, eng, d, fn, evk, inc, strict=False):
        waits = []
        seen = self.seen[eng]
        for k, v in d.items():
            if eng == "pe" and k[0] == "p" and k[1] == "pe" and not strict:
                continue
            if seen.get(k, 0) < v:
                seen[k] = v
                waits.append((k, v))
        self.streams[eng].append((waits, fn, evk, inc))
        self.nops += 1

    def op(self, eng, fn, reads=(), writes=(), signal=True):
        d = self._deps(reads, writes)
        key = ("p", eng, self.cur)
        ck = (eng, self.cur)
        if signal:
            self.pcnt[ck] += 1
            assert self.pcnt[ck] < 65000, "progress semaphore overflow"
            ev = (key, self.pcnt[ck])
            self._emit(eng, d, fn, key, 1)
        else:
            ev = (key, self.pcnt[ck] + 1)
            self._emit(eng, d, fn, None, 0)
        self._mark(reads, writes, ev)

    def dma(self, q, fn, reads=(), writes=()):
        d = self._deps(reads, writes)
        i = self.dnext
        self.dnext = (i + 1) % len(self.dsem)
        k = ("d", i)
        if self.dcnt[i] > 0 and d.get(k, 0) < self.dcnt[i]:
            d[k] = self.dcnt[i]
        self.dcnt[i] += 16
        assert self.dcnt[i] < 65000, "dma semaphore overflow"
        ev = (k, self.dcnt[i])
        self._emit(q, d, fn, k, 16)
        self._mark(reads, writes, ev)

    def barrier(self, next_set=None):
        d = {("p", e, s): self.pcnt[(e, s)] for e in self.CE for s in range(self.NSETS) if self.pcnt[(e, s)] > 0}
        for i, c in enumerate(self.dcnt):
            if c > 0:
                d[("d", i)] = c
        for e in self.ENG:
            self._emit(e, dict(d), None, None, 0, strict=True)
        for t in ALL_TOKS:
            t.w = None
            t.r = {}
        if next_set is not None:
            self.cur = next_set

    def check(self):
        sem = {}
        pos = {e: 0 for e in self.ENG}
        n = {e: len(self.streams[e]) for e in self.ENG}
        while True:
            prog = False
            for e in self.ENG:
                st = self.streams[e]
                while pos[e] < n[e]:
                    waits, fn, evk, inc = st[pos[e]]
                    if any(sem.get(k, 0) < v for k, v in waits):
                        break
                    if evk is not None:
                        sem[evk] = sem.get(evk, 0) + inc
                    pos[e] += 1
                    prog = True
            if all(pos[e] == n[e] for e in self.ENG):
                return
            if not prog:
                msg = []
                for e in self.ENG:
                    if pos[e] < n[e]:
                        waits = self.streams[e][pos[e]][0]
                        msg.append((e, pos[e], [(k, v, sem.get(k, 0)) for k, v in waits if sem.get(k, 0) < v]))
                raise RuntimeError("DEADLOCK in generated program: %r" % (msg,))

    def run(self):
        nc = self.nc
        self.check()

        def semof(k):
            return self.psem[(k[1], k[2])] if k[0] == "p" else self.dsem[k[1]]

        def mk(name):
            stream = self.streams[name]

            def body(e):
                for waits, fn, evk, inc in stream:
                    for k, v in waits:
                        e.wait_ge(semof(k), v)
                    if fn is not None:
                        ins = fn(e)
                        if evk is not None:
                            ins.then_inc(semof(evk), inc)
            return body

        with nc.Block() as block:
            block.tensor(mk("pe"))
            block.scalar(mk("act"))
            block.vector(mk("dve"))
            block.gpsimd(mk("pool"))
            block.sync(mk("sp"))


def build(seqs, depth, lmax=None, debug=False):
    nseq = len(seqs)
    Lmax = lmax or (16 + max(seqs))
    nc = bass.Bass("TRN2", target_bir_lowering=False)

    def din(name, shape, dt=F32):
        return nc.dram_tensor(name, list(shape), dt, kind="ExternalInput").ap()

    def dscr(name, shape, dt):
        if debug and not name.startswith("wb_"):
            return nc.dram_tensor(name, list(shape), dt, kind="ExternalOutput").ap()
        return nc.dram_tensor(name, list(shape), dt).ap()

    xs = [din("x%d" % i, [T, D_MODEL]) for i, T in enumerate(seqs)]
    ys = [nc.dram_tensor("y%d" % i, [T, D_MODEL], F32, kind="ExternalOutput").ap() for i, T in enumerate(seqs)]
    meta_d = din("meta", [16, D_MODEL])
    wsrc = {
        "gu1": din("w_gu1", [depth, 1024, 2 * FFN]), "d1": din("w_d1", [depth, FFN, 1024]),
        "gu2": din("w_gu2", [depth, 1024, 2 * FFN]), "d2": din("w_d2", [depth, FFN, 1024]),
        "in": din("w_in", [depth, 1024, 2752]), "uq": din("w_uq", [depth, 768, 1024]),
        "ukv": din("w_ukv", [depth, 256, 1024]), "o": din("w_o", [depth, 1024, 1024]),
    }
    gv_d = din("gv", [128, depth * NG])
    mb_d = din("mb", [16, depth * 4])
    ropeC_d = din("ropeC", [32, Lmax])
    ropeS_d = din("ropeS", [32, Lmax])
    nab_d = din("nab", [depth, 3, 8, 4, 128, 512])
    ident_d = din("ident", [128, 128])

    wb = {k: dscr("wb_" + k, v.shape, BF16) for k, v in wsrc.items()}
    HT = dscr("HT", [1024, Lmax], F32)
    YT = dscr("YT", [1024, Lmax], BF16)
    CB = dscr("CB", [256, Lmax], F32)
    CV = dscr("CV", [256, Lmax + 2], F32)
    NQ = dscr("NQ", [256, Lmax], BF16)
    NK = dscr("NK", [256, Lmax], BF16)
    NV = dscr("NV", [Lmax, 260], BF16)
    QC = dscr("QC", [8, 96, Lmax], BF16)
    KC = dscr("KC", [8, 96, Lmax], BF16)
    MV = dscr("MV", [Lmax, 520], BF16)

    es = contextlib.ExitStack()
    with es:
        P = Prog(nc, es)
        P.cur = 3
        es.enter_context(nc.allow_non_contiguous_dma(reason="single pad columns / small strided tiles"))

        uniq = [0]

        def sb(stack, name, shape, dt):
            uniq[0] += 1
            return Tile(stack.enter_context(nc.sbuf_tensor("s%d_%s" % (uniq[0], name), list(shape), dt)))

        ps_t = es.enter_context(nc.psum_tensor("ps", [128, 8, 512], F32))
        banks = [Tile(ps_t[:, i, :]) for i in range(8)]
        gv = sb(es, "gv", [128, depth * NG], F32)
        mb = sb(es, "mb", [16, depth * 4], F32)
        ident = sb(es, "ident", [128, 128], F32)
        ones = sb(es, "ones", [128, 128], BF16)
        zero = sb(es, "zero", [128, 2], F32)
        P.dma("sp", lambda e: e.dma_start(out=gv[:, :], in_=gv_d[:, :]), writes=[gv.tok])
        P.dma("sp", lambda e: e.dma_start(out=mb[:, :], in_=mb_d[:, :]), writes=[mb.tok])
        P.dma("sp", lambda e: e.dma_start(out=ident[:, :], in_=ident_d[:, :]), writes=[ident.tok])
        ones_f = sb(es, "ones_f", [128, 128], F32)
        gvh = sb(es, "gvh", [128, depth * NG], F32)
        P.op("dve", lambda e: e.memset(ones_f[:, :], 1.0), writes=[ones_f.tok])
        P.op("dve", lambda e: e.tensor_scalar(out=gvh[:, :], in0=gv[:, :], scalar1=0.5, scalar2=None, op0=ALU.mult),
             [gv.tok], [gvh.tok])
        P.op("dve", lambda e: e.memset(ones[:, :], 1.0), writes=[ones.tok])
        P.op("dve", lambda e: e.memset(zero[:, :], 0.0), writes=[zero.tok])

        def gcol(l, base):
            return gv[:, l * NG + base:l * NG + base + 1]

        def gcolh(l, base):
            return gvh[:, l * NG + base:l * NG + base + 1]

        def mm(out, lhsT, rhs, start, stop, reads, writes, contig=True):
            P.op("pe", lambda e: e.matmul(out, lhsT=lhsT, rhs=rhs, start=start, stop=stop), reads, writes,
                 signal=(stop or not contig))

        def act(out, in_, func, reads, writes, scale=1.0, bias=0.0, accum=None):
            if accum is None:
                P.op("act", lambda e: e.activation(out=out, in_=in_, func=func, scale=scale, bias=bias),
                     reads, writes)
            else:
                P.op("act", lambda e: e.activation(out=out, in_=in_, func=func, scale=scale, bias=bias,
                                                   accum_out=accum), reads, writes)

        def stt(out, in0, scalar, in1, op0, op1, reads, writes):
            P.op("dve", lambda e: e.scalar_tensor_tensor(out=out, in0=in0, scalar=scalar, in1=in1,
                                                         op0=op0, op1=op1), reads, writes)

        def tt(out, in0, in1, op, reads, writes, eng="dve"):
            P.op(eng, lambda e: e.tensor_tensor(out=out, in0=in0, in1=in1, op=op), reads, writes)

        def ts(out, in0, s1, s2, op0, op1, reads, writes):
            if s2 is None:
                P.op("dve", lambda e: e.tensor_scalar(out=out, in0=in0, scalar1=s1, scalar2=None, op0=op0),
                     reads, writes)
            else:
                P.op("dve", lambda e: e.tensor_scalar(out=out, in0=in0, scalar1=s1, scalar2=s2, op0=op0, op1=op1),
                     reads, writes)

        def recip(out, in_, reads, writes):
            P.op("dve", lambda e: e.reciprocal(out=out, in_=in_), reads, writes)

        def load(out, in_, writes, reads=()):
            P.dma("sp", lambda e: e.dma_start(out=out, in_=in_), reads, writes)

        def store(out, in_, reads):
            P.dma("pool", lambda e: e.dma_start(out=out, in_=in_), reads, ())

        with contextlib.ExitStack() as ps_es:
            CW = 2048
            fin = [sb(ps_es, "fin%d" % i, [128, CW], F32) for i in range(3)]
            fout = [sb(ps_es, "fout%d" % i, [128, CW], BF16) for i in range(3)]
            ci = 0
            for key in ("gu1", "d1", "in", "uq", "ukv", "o", "gu2", "d2"):
                src = wsrc[key]
                dst = wb[key]
                _, R, N = src.shape
                for l in range(depth):
                    for rc in range(R // 128):
                        for c0 in range(0, N, CW):
                            n = min(CW, N - c0)
                            a, b = fin[ci % 3], fout[ci % 3]
                            load(a[:, :n], src[l, rc * 128:(rc + 1) * 128, c0:c0 + n], [a.tok])
                            eng = ("pool", "dve", "act")[ci % 3]
                            if eng == "act":
                                P.op("act", lambda e, a=a, b=b, n=n: e.copy(out=b[:, :n], in_=a[:, :n]),
                                     [a.tok], [b.tok])
                            else:
                                P.op(eng, lambda e, a=a, b=b, n=n: e.tensor_copy(out=b[:, :n], in_=a[:, :n]),
                                     [a.tok], [b.tok])
                            store(dst[l, rc * 128:(rc + 1) * 128, c0:c0 + n], b[:, :n], [b.tok])
                            ci += 1
        P.barrier(next_set=0)

        bank_rr = [0]

        def bank(pool=(0, 1, 2, 3, 4, 5)):
            b = banks[pool[bank_rr[0] % len(pool)]]
            bank_rr[0] += 1
            return b

        for si, T in enumerate(seqs):
            L = 16 + T
            if si > 0:
                P.barrier(next_set=si)
            blocks = [(0, 16)] + [(16 + 512 * i, 512) for i in range(T // 512)]
            HTv = HT.rearrange("(c p) l -> p c l", p=128)
            YTv = YT.rearrange("(c p) l -> p c l", p=128)

            with contextlib.ExitStack() as st:
                xt = [sb(st, "xt%d" % i, [128, 1024], F32) for i in range(2)]
                hb = [sb(st, "hb%d" % i, [128, 8, 512], F32) for i in range(2)]
                ti = 0
                for bi, (c0, nb) in enumerate(blocks):
                    h = hb[bi % 2]
                    for j in range((nb + 127) // 128):
                        qs = min(128, nb)
                        x_t = xt[ti % 2]
                        ti += 1
                        if bi == 0:
                            load(x_t[:qs, :], meta_d[:, :], [x_t.tok])
                        else:
                            t0 = c0 - 16 + j * 128
                            load(x_t[:qs, :], xs[si][t0:t0 + qs, :], [x_t.tok])
                        for c in range(8):
                            b = bank()
                            P.op("pe", lambda e, b=b, x_t=x_t, c=c, qs=qs: e.transpose(
                                out=b[:, :qs], in_=x_t[:qs, c * 128:(c + 1) * 128], identity=ident[:qs, :qs]),
                                [x_t.tok, ident.tok], [b.tok])
                            if c % 2 == 0:
                                P.op("dve", lambda e, b=b, h=h, c=c, j=j, qs=qs: e.tensor_copy(
                                    out=h[:, c, j * 128:j * 128 + qs], in_=b[:, :qs]), [b.tok], [h.tok])
                            else:
                                P.op("act", lambda e, b=b, h=h, c=c, j=j, qs=qs: e.copy(
                                    out=h[:, c, j * 128:j * 128 + qs], in_=b[:, :qs]), [b.tok], [h.tok])
                    store(HTv[:, :, c0:c0 + nb], h[:, :, :nb], [h.tok])
                CVv = CV.rearrange("(c p) l -> p c l", p=128)
                for c in range(2):
                    store(CVv[:, c, 0:1], zero[:, 0:1], [zero.tok])
                    store(CVv[:, c, L + 1:L + 2], zero[:, 1:2], [zero.tok])
            P.barrier()

            for l in range(depth + 1):
                with contextlib.ExitStack() as st:
                    hblk = sb(st, "hblk", [128, 8, 512], F32)
                    hb_tok = [Tok() for _ in range(8)]
                    xn = sb(st, "xn", [128, 8, 512], BF16)
                    xn_tok = [Tok() for _ in range(8)]
                    actT = sb(st, "actT", [128, 22, 512], BF16)
                    actT_tok = [Tok() for _ in range(22)]
                    obuf = sb(st, "obuf", [128, 8, 512], F32)
                    obuf_tok = [Tok() for _ in range(8)]
                    ring = [sb(st, "ring%d" % i, [128, 4096], BF16) for i in range(6)]
                    ring_i = [0]
                    sq8 = sb(st, "sq8", [128, 8, 512], BF16)
                    sq8_tok = [Tok() for _ in range(8)]
                    rtm = [sb(st, "rtm%d" % i, [128, 8], F32) for i in range(2)]
                    rtm_i = [0]
                    dmt = [sb(st, "dm%d" % i, [128, 4, 128], F32) for i in range(2)]
                    dm_i = [0]
                    RB = banks[6]
                    SS = banks[7]
                    sgb = [sb(st, "sg%d" % i, [128, 512], F32) for i in range(2)]
                    sg_i = [0]
                    ytb = sb(st, "ytb", [128, 8, 512], BF16)
                    qn = sb(st, "qn", [128, 6, 512], BF16)
                    kvn = sb(st, "kvn", [128, 2, 512], BF16)
                    cct = sb(st, "cct", [128, 2, 512], F32)
                    cbt = sb(st, "cbt", [128, 2, 512], F32)
                    cvt = sb(st, "cvt", [128, 2, 512], F32)
                    nqt = sb(st, "nqt", [128, 2, 512], BF16)
                    nkt = sb(st, "nkt", [128, 2, 512], BF16)
                    nvt = sb(st, "nvt", [128, 4, 260], BF16)
                    qct = sb(st, "qct", [96, 8, 512], BF16)
                    kct = sb(st, "kct", [96, 8, 512], BF16)
                    mvt = sb(st, "mvt", [128, 4, 520], BF16)
                    kper = sb(st, "kper", [96, 512], BF16)
                    rt1 = sb(st, "rt1", [96, 512], F32)
                    rt2 = sb(st, "rt2", [96, 512], F32)
                    rC = sb(st, "rC", [96, 512], F32)
                    rS = sb(st, "rS", [96, 512], F32)
                    yout = [sb(st, "yout%d" % i, [128, 1024], F32) for i in range(2)]
                    P.op("pool", lambda e, nvt=nvt: e.memset(nvt[:, :, :], 1.0), writes=[nvt.tok])
                    P.op("pool", lambda e, mvt=mvt: e.memset(mvt[:, :, :], 1.0), writes=[mvt.tok])

                    def load_w(W, KC_, c0, ncols, r0=0):
                        s = ring[ring_i[0] % len(ring)]
                        ring_i[0] += 1
                        view = s.t[:, 0:KC_ * ncols].rearrange("p (c n) -> p c n", c=KC_)
                        src = W.rearrange("(c p) n -> p c n", p=128)[:, r0:r0 + KC_, c0:c0 + ncols]
                        load(view, src, [s.tok])
                        return view, s.tok

                    def square_to(c, ap, tk, nb, from_psum, k):
                        if from_psum or k % 3 == 0:
                            act(sq8[:, c, :nb], ap, AF.Square, [tk], [sq8_tok[c]])
                        elif k % 3 == 1:
                            tt(sq8[:, c, :nb], ap, ap, ALU.mult, [tk], [sq8_tok[c]])
                        else:
                            tt(sq8[:, c, :nb], ap, ap, ALU.mult, [tk], [sq8_tok[c]], eng="pool")

                    def rstd_bcast(clist, nb, D):
                        ntile = (nb + 127) // 128
                        qs = min(128, nb)
                        nch = len(clist)
                        for j in range(ntile):
                            for i, c in enumerate(clist):
                                mm(SS[:qs, j:j + 1], sq8[:, c, j * 128:j * 128 + qs], ones[:, 0:1], i == 0, i == nch - 1,
                                   [sq8_tok[c], ones.tok], [SS.tok])
                        r = rtm[rtm_i[0] % 2]
                        rtm_i[0] += 1
                        act(r[:qs, 0:ntile], SS[:qs, 0:ntile], AF.Sqrt, [SS.tok], [r.tok], scale=1.0 / D, bias=EPS)
                        recip(r[:qs, 0:ntile], r[:qs, 0:ntile], [r.tok], [r.tok])
                        dm = dmt[dm_i[0] % 2]
                        dm_i[0] += 1
                        tt(dm[:qs, 0:ntile, :qs], ident[:qs, :qs].unsqueeze(1).to_broadcast([qs, ntile, qs]),
                           r[:qs, 0:ntile].unsqueeze(2).to_broadcast([qs, ntile, qs]), ALU.mult, [ident.tok, r.tok], [dm.tok])
                        for j in range(ntile):
                            P.op("pe", lambda e, j=j, dm=dm, qs=qs: e.matmul(
                                RB[:, j * 128:j * 128 + qs], lhsT=ones_f[:qs, :], rhs=dm[:qs, j, :qs], start=True, stop=True),
                                [dm.tok, ones_f.tok], [RB.tok])
                        return RB

                    def rms_sbuf(srcs, nb, D, c0_=0):
                        for i, (ap, tk) in enumerate(srcs):
                            square_to(c0_ + i, ap, tk, nb, False, i)
                        return rstd_bcast([c0_ + i for i in range(len(srcs))], nb, D)

                    def post_chunk(dc, b, nb):
                        P.op("dve", lambda e, dc=dc, b=b, nb=nb, obuf=obuf: e.tensor_copy(out=obuf[:, dc, :nb], in_=b[:, :nb]),
                             [b.tok], [obuf_tok[dc]])
                        square_to(dc, obuf[:, dc, :nb], obuf_tok[dc], nb, True, 0)

                    def post_end(l_, gbase, coef, nb):
                        r = rstd_bcast(list(range(8)), nb, 1024)
                        for dc in range(8):
                            g_ = gcolh(l_, gbase + dc) if coef == 0.5 else gcol(l_, gbase + dc)
                            stt(obuf[:, dc, :nb], obuf[:, dc, :nb], g_, r[:, :nb], ALU.mult, ALU.mult,
                                [obuf_tok[dc], r.tok, gv.tok, gvh.tok], [obuf_tok[dc]])
                            tt(hblk[:, dc, :nb], hblk[:, dc, :nb], obuf[:, dc, :nb], ALU.add,
                               [obuf_tok[dc], hb_tok[dc]], [hb_tok[dc]], eng="pool")

                    def make_xn(l_, gbase, nb):
                        r = rms_sbuf([(hblk[:, c, :nb], hb_tok[c]) for c in range(8)], nb, 1024)
                        for c in range(8):
                            stt(xn[:, c, :nb], hblk[:, c, :nb], gcol(l_, gbase + c), r[:, :nb], ALU.mult, ALU.mult,
                                [hb_tok[c], r.tok, gv.tok], [xn_tok[c]])

                    def ffn(l_, which, nb):
                        make_xn(l_, G_F1PRE if which == 1 else G_F2PRE, nb)
                        Wgu = wb["gu%d" % which][l_]
                        Wd = wb["d%d" % which][l_]
                        for fg in range(6):
                            f0 = fg * 512
                            nf = min(512, FFN - f0)
                            wg, wg_tok = load_w(Wgu, 8, f0, nf)
                            wu, wu_tok = load_w(Wgu, 8, FFN + f0, nf)
                            for jj in range(nf // 128):
                                j = fg * 4 + jj
                                bg = bank()
                                bu = bank()
                                for kc in range(8):
                                    mm(bg[:, :nb], wg[:, kc, jj * 128:(jj + 1) * 128], xn[:, kc, :nb], kc == 0, kc == 7,
                                       [wg_tok, xn_tok[kc]], [bg.tok])
                                for kc in range(8):
                                    mm(bu[:, :nb], wu[:, kc, jj * 128:(jj + 1) * 128], xn[:, kc, :nb], kc == 0, kc == 7,
                                       [wu_tok, xn_tok[kc]], [bu.tok])
                                s = sgb[sg_i[0] % 2]
                                sg_i[0] += 1
                                act(s[:, :nb], bg[:, :nb], AF.Silu, [bg.tok], [s.tok])
                                tt(actT[:, j, :nb], s[:, :nb], bu[:, :nb], ALU.mult, [s.tok, bu.tok], [actT_tok[j]])
                        parts = [(0, 8), (8, 16), (16, 22)]
                        for half in range(2):
                            sl = [load_w(Wd, j1 - j0, half * 512, 512, r0=j0) for (j0, j1) in parts]
                            for dcr in range(4):
                                dc = half * 4 + dcr
                                b = bank()
                                for j in range(22):
                                    pi = 0 if j < 8 else (1 if j < 16 else 2)
                                    w, wt = sl[pi]
                                    mm(b[:, :nb], w[:, j - parts[pi][0], dcr * 128:(dcr + 1) * 128], actT[:, j, :nb],
                                       j == 0, j == 21, [wt, actT_tok[j]], [b.tok])
                                post_chunk(dc, b, nb)
                        post_end(l_, G_F1POST if which == 1 else G_F2POST, 0.5, nb)

                    def mixer_out(l_, c0, nb):
                        load(ytb[:, :, :nb], YTv[:, :, c0:c0 + nb], [ytb.tok])
                        for half in range(2):
                            w, wt = load_w(wb["o"][l_], 8, half * 512, 512)
                            for dcr in range(4):
                                dc = half * 4 + dcr
                                b = bank()
                                for c in range(8):
                                    mm(b[:, :nb], w[:, c, dcr * 128:(dcr + 1) * 128], ytb[:, c, :nb], c == 0, c == 7,
                                       [wt, ytb.tok], [b.tok])
                                post_chunk(dc, b, nb)
                        post_end(l_, G_MPOST, 1.0, nb)

                    def projections(l_, c0, nb):
                        make_xn(l_, G_MPRE, nb)
                        Win = wb["in"][l_]
                        ql = obuf
                        load(rC[64:96, :nb], ropeC_d[:, c0:c0 + nb], [rC.tok])
                        load(rS[64:96, :nb], ropeS_d[:, c0:c0 + nb], [rS.tok])
                        ntile = (nb + 127) // 128
                        qs = min(128, nb)
                        for g in range(6):
                            gc0 = g * 512
                            ncol = min(512, 2752 - gc0)
                            w, wt = load_w(Win, 8, gc0, ncol)
                            if g == 5:
                                bA = bank()
                                bB = bank()
                                for kc in range(8):
                                    mm(bA[:96, :nb], w[:, kc, 0:96], xn[:, kc, :nb], kc == 0, kc == 7, [wt, xn_tok[kc]], [bA.tok])
                                for kc in range(8):
                                    mm(bB[:96, :nb], w[:, kc, 96:192], xn[:, kc, :nb], kc == 0, kc == 7, [wt, xn_tok[kc]], [bB.tok])
                                tt(rt1[64:96, :nb], bA[64:96, :nb], rC[64:96, :nb], ALU.mult, [bA.tok, rC.tok], [rt1.tok])
                                tt(rt2[64:96, :nb], bB[64:96, :nb], rS[64:96, :nb], ALU.mult, [bB.tok, rS.tok], [rt2.tok])
                                tt(kper[64:96, :nb], rt1[64:96, :nb], rt2[64:96, :nb], ALU.add, [rt1.tok, rt2.tok], [kper.tok])
                                continue
                            for oc_r in range(4):
                                oc = g * 4 + oc_r
                                if oc in (10, 11):
                                    continue
                                b = bank()
                                for kc in range(8):
                                    mm(b[:, :nb], w[:, kc, oc_r * 128:(oc_r + 1) * 128], xn[:, kc, :nb], kc == 0, kc == 7,
                                       [wt, xn_tok[kc]], [b.tok])
                                if oc < 2:
                                    act(cbt[:, oc, :nb], b[:, :nb], AF.Copy, [b.tok], [cbt.tok])
                                elif oc < 4:
                                    act(cct[:, oc - 2, :nb], b[:, :nb], AF.Copy, [b.tok], [cct.tok])
                                elif oc < 6:
                                    tt(cvt[:, oc - 4, :nb], cct[:, oc - 4, :nb], b[:, :nb], ALU.mult, [cct.tok, b.tok], [cvt.tok])
                                elif oc < 8:
                                    act(nqt[:, oc - 6, :nb], b[:, :nb], AF.Copy, [b.tok], [nqt.tok])
                                elif oc < 10:
                                    act(nkt[:, oc - 8, :nb], b[:, :nb], AF.Copy, [b.tok], [nkt.tok])
                                else:
                                    act(ql[:, oc - 12, :nb], b[:, :nb], AF.Copy, [b.tok], [obuf_tok[oc - 12]])
                            if g == 2:
                                for j in range(ntile):
                                    b = bank()
                                    for kc in range(8):
                                        mm(b[:qs, 0:256], xn[:, kc, j * 128:j * 128 + qs], w[:, kc, 256:512], kc == 0, kc == 7,
                                           [wt, xn_tok[kc]], [b.tok])
                                    P.op("dve", lambda e, b=b, j=j, nvt=nvt, qs=qs: e.tensor_copy(
                                        out=nvt[:qs, j, :].rearrange("p (h d) -> p h d", h=4)[:, :, 0:64],
                                        in_=b[:qs, 0:256].rearrange("p (h d) -> p h d", h=4)), [b.tok], [nvt.tok])
                        CBv = CB.rearrange("(c p) l -> p c l", p=128)
                        CVv = CV.rearrange("(c p) l -> p c l", p=128)
                        NQv = NQ.rearrange("(c p) l -> p c l", p=128)
                        NKv = NK.rearrange("(c p) l -> p c l", p=128)
                        store(CBv[:, :, c0:c0 + nb], cbt[:, :, :nb], [cbt.tok])
                        store(CVv[:, :, c0 + 1:c0 + 1 + nb], cvt[:, :, :nb], [cvt.tok])
                        store(NQv[:, :, c0:c0 + nb], nqt[:, :, :nb], [nqt.tok])
                        store(NKv[:, :, c0:c0 + nb], nkt[:, :, :nb], [nkt.tok])
                        if nb == 16:
                            store(NV[0:16, :], nvt[:16, 0, :], [nvt.tok])
                        else:
                            store(NV[c0:c0 + nb, :].rearrange("(t p) f -> p t f", p=128), nvt[:, :, :], [nvt.tok])
                        r = rms_sbuf([(ql[:, c, :nb], obuf_tok[c]) for c in range(6)], nb, 768)
                        for c in range(6):
                            stt(qn[:, c, :nb], ql[:, c, :nb], gcol(l_, G_QN + c), r[:, :nb], ALU.mult, ALU.mult,
                                [obuf_tok[c], r.tok, gv.tok], [qn.tok])
                        for hg in range(2):
                            w, wt = load_w(wb["uq"][l_], 6, hg * 512, 512)
                            for hr in range(4):
                                h = hg * 4 + hr
                                bm = bank()
                                bs = bank()
                                for c in range(6):
                                    mm(bm[:96, :nb], w[:, c, hr * 128:hr * 128 + 96], qn[:, c, :nb], c == 0, c == 5,
                                       [wt, qn.tok], [bm.tok])
                                for c in range(6):
                                    mm(bs[:96, :nb], w[:, c, hr * 128 + 32:hr * 128 + 128], qn[:, c, :nb], c == 0, c == 5,
                                       [wt, qn.tok], [bs.tok])
                                tt(rt1[64:96, :nb], bm[64:96, :nb], rC[64:96, :nb], ALU.mult, [bm.tok, rC.tok], [rt1.tok])
                                tt(rt2[64:96, :nb], bs[64:96, :nb], rS[64:96, :nb], ALU.mult, [bs.tok, rS.tok], [rt2.tok])
                                tt(qct[64:96, h, :nb], rt1[64:96, :nb], rt2[64:96, :nb], ALU.add, [rt1.tok, rt2.tok], [qct.tok])
                                act(qct[0:64, h, :nb], bm[0:64, :nb], AF.Copy, [bm.tok], [qct.tok])
                        store(QC.rearrange("h p l -> p h l")[:, :, c0:c0 + nb], qct[:, :, :nb], [qct.tok])
                        r = rms_sbuf([(ql[:, 6 + c, :nb], obuf_tok[6 + c]) for c in range(2)], nb, 256, c0_=6)
                        for c in range(2):
                            stt(kvn[:, c, :nb], ql[:, 6 + c, :nb], gcol(l_, G_KVN + c), r[:, :nb], ALU.mult, ALU.mult,
                                [obuf_tok[6 + c], r.tok, gv.tok], [kvn.tok])
                        w, wt = load_w(wb["ukv"][l_], 2, 0, 1024)
                        for h in range(8):
                            b = bank()
                            for c in range(2):
                                mm(b[:64, :nb], w[:, c, h * 128:h * 128 + 64], kvn[:, c, :nb], c == 0, c == 1,
                                   [wt, kvn.tok], [b.tok])
                            act(kct[0:64, h, :nb], b[0:64, :nb], AF.Copy, [b.tok], [kct.tok])
                            P.op("pool", lambda e, h=h, kct=kct, kper=kper, nb=nb: e.tensor_copy(out=kct[64:96, h, :nb], in_=kper[64:96, :nb]),
                                 [kper.tok], [kct.tok])
                        store(KC.rearrange("h p l -> p h l")[:, :, c0:c0 + nb], kct[:, :, :nb], [kct.tok])
                        wv = w.rearrange("p c (h d) -> p c h d", h=8)[:, :, :, 64:128]
                        for j in range(ntile):
                            b = bank()
                            for c in range(2):
                                mm(b[:qs, :].rearrange("p (h d) -> p h d", h=8), kvn[:, c, j * 128:j * 128 + qs], wv[:, c],
                                   c == 0, c == 1, [wt, kvn.tok], [b.tok])
                            P.op("dve", lambda e, b=b, j=j, mvt=mvt, qs=qs: e.tensor_copy(
                                out=mvt[:qs, j, :].rearrange("p (h d) -> p h d", h=8)[:, :, 0:64],
                                in_=b[:qs, :].rearrange("p (h d) -> p h d", h=8)), [b.tok], [mvt.tok])
                        if nb == 16:
                            store(MV[0:16, :], mvt[:16, 0, :], [mvt.tok])
                        else:
                            store(MV[c0:c0 + nb, :].rearrange("(t p) f -> p t f", p=128), mvt[:, :, :], [mvt.tok])

                    def final_out(c0, nb):
                        for j in range(nb // 128):
                            yo = yout[j % 2]
                            for half in range(2):
                                b = bank()
                                for cr in range(4):
                                    c = half * 4 + cr
                                    P.op("pe", lambda e, b=b, c=c, cr=cr, j=j, hblk=hblk: e.transpose(
                                        out=b[:, cr * 128:(cr + 1) * 128], in_=hblk[:, c, j * 128:(j + 1) * 128],
                                        identity=ident[:, :]), hb_tok + [ident.tok], [b.tok])
                                if half == 0:
                                    P.op("dve", lambda e, b=b, yo=yo: e.tensor_copy(out=yo[:, 0:512], in_=b[:, :]),
                                         [b.tok], [yo.tok])
                                else:
                                    P.op("act", lambda e, b=b, yo=yo: e.copy(out=yo[:, 512:1024], in_=b[:, :]),
                                         [b.tok], [yo.tok])
                            t0 = c0 - 16 + j * 128
                            store(ys[si][t0:t0 + 128, :], yo[:, :], [yo.tok])

                    for (c0, nb) in blocks:
                        if l == depth and nb == 16:
                            continue
                        load(hblk[:, :, :nb], HTv[:, :, c0:c0 + nb], hb_tok)
                        if l > 0:
                            mixer_out(l - 1, c0, nb)
                            ffn(l - 1, 2, nb)
                        if l < depth:
                            ffn(l, 1, nb)
                            store(HTv[:, :, c0:c0 + nb], hblk[:, :, :nb], hb_tok)
                            projections(l, c0, nb)
                        else:
                            final_out(c0, nb)
                P.barrier()
                if l == depth:
                    break

                with contextlib.ExitStack() as st:
                    ntile_k = 1 + T // 128
                    kcsb = sb(st, "kcsb", [96, 8, L], BF16)
                    mvsb = sb(st, "mvsb", [128, ntile_k, 520], BF16)
                    qcb = [sb(st, "qcb%d" % i, [96, 8, 512], BF16) for i in range(2)]
                    ptb = [sb(st, "pt%d" % i, [128, 512], BF16) for i in range(3)]
                    pt_i = [0]
                    nkw = sb(st, "nkw", [128, 2, 16 + 1024], BF16)
                    nvw = sb(st, "nvw", [128, 9, 260], BF16)
                    nqb = sb(st, "nqb", [128, 2, 512], BF16)
                    nbias = [sb(st, "nbias%d" % i, [128, 512], F32) for i in range(3)]
                    nb_i = [0]
                    sbb = [sb(st, "sbb%d" % i, [128, 512], F32) for i in range(2)]
                    sb_i = [0]
                    ym = sb(st, "ym", [128, 4, 768], F32)
                    ym_tok = [Tok() for _ in range(4)]
                    yt = sb(st, "yt", [128, 8, 512], BF16)
                    cvh = sb(st, "cvh", [128, 2, 514], F32)
                    cbb = sb(st, "cbb", [128, 2, 512], F32)
                    ctmp = sb(st, "ctmp", [128, 2, 512], F32)
                    csq = sb(st, "csq", [128, 2, 512], BF16)
                    crs = sb(st, "crs", [128, 512], F32)
                    junk = sb(st, "junk", [128, 512], F32)
                    ssm = [sb(st, "ssm%d" % i, [128, 4], F32) for i in range(2)]
                    ss_i = [0]
                    rdt = [sb(st, "rdt%d" % i, [128, 1], F32) for i in range(4)]
                    rd_i = [0]

                    for h in range(8):
                        load(kcsb[:, h, :], KC[h, :, 0:L], [kcsb.tok])
                    load(mvsb[:16, 0, :], MV[0:16, :], [mvsb.tok])
                    load(mvsb[:, 1:, :], MV[16:16 + T, :].rearrange("(t p) f -> p t f", p=128), [mvsb.tok])
                    obanks = banks[0:4]
                    SB = (4, 5, 6)

                    def attention(nq, nheads, units, qk_ops, vsrc, ycol0, scale, post):
                        qtiles = (nq + 127) // 128
                        qs = min(128, nq)
                        for h in range(nheads):
                            n = len(units)

                            def qk(u):
                                b = bank(SB)
                                lhsT, rhs, rd = qk_ops(h, u)
                                mm(b[:units[u]["nk"], :nq], lhsT, rhs, True, True, rd, [b.tok])
                                return b

                            def pv(u, b):
                                nk = units[u]["nk"]
                                pt = post(h, u, b)
                                vap, vtok = vsrc(h, u)
                                for j in range(qtiles):
                                    mm(obanks[j][:qs, 0:65], pt[:nk, j * 128:j * 128 + qs], vap, u == 0, u == n - 1,
                                       [pt.tok, vtok], [obanks[j].tok], contig=False)

                            bcur = qk(0)
                            for u in range(n):
                                bnext = qk(u + 1) if u + 1 < n else None
                                pv(u, bcur)
                                bcur = bnext
                            for j in range(qtiles):
                                rd = rdt[rd_i[0] % 4]
                                rd_i[0] += 1
                                recip(rd[:qs, :], obanks[j][:qs, 64:65], [obanks[j].tok], [rd.tok])
                                ts(ym[:qs, j, ycol0 + h * 64:ycol0 + (h + 1) * 64], obanks[j][:qs, 0:64], rd[:qs, 0:1], None,
                                   ALU.mult, None, [obanks[j].tok, rd.tok], [ym_tok[j]])

                    nblk = T // 512
                    NQv = NQ.rearrange("(c p) l -> p c l", p=128)
                    NKv = NK.rearrange("(c p) l -> p c l", p=128)
                    CBv = CB.rearrange("(c p) l -> p c l", p=128)
                    CVv = CV.rearrange("(c p) l -> p c l", p=128)
                    for bi, (c0, nb) in enumerate(blocks):
                        qtiles = (nb + 127) // 128
                        qs = min(128, nb)
                        qc = qcb[bi % 2]
                        load(qc[:, :, :nb], QC.rearrange("h p l -> p h l")[:, :, c0:c0 + nb], [qc.tok])
                        m_units = [dict(nk=16, k0=0, vt=0)] + [dict(nk=128, k0=16 + 128 * t, vt=1 + t) for t in range(T // 128)]

                        def m_qk(h, u, qc=qc, nb=nb):
                            un = m_units[u]
                            return (kcsb[:, h, un["k0"]:un["k0"] + un["nk"]], qc[:, h, :nb], [kcsb.tok, qc.tok])

                        def m_post(h, u, b, nb=nb):
                            nk = m_units[u]["nk"]
                            pt = ptb[pt_i[0] % 3]
                            pt_i[0] += 1
                            act(pt[:nk, :nb], b[:nk, :nb], AF.Exp, [b.tok], [pt.tok], scale=MLA_SCALE)
                            return pt

                        def m_v(h, u):
                            un = m_units[u]
                            return mvsb[:un["nk"], un["vt"], h * 65:(h + 1) * 65], mvsb.tok

                        attention(nb, 8, m_units, m_qk, m_v, 256, MLA_SCALE, m_post)

                        load(nqb[:, :, :nb], NQv[:, :, c0:c0 + nb], [nqb.tok])
                        load(nkw[:, :, 0:16], NKv[:, :, 0:16], [nkw.tok])
                        load(nvw[:16, 0, :], NV[0:16, :], [nvw.tok])
                        n_units = [dict(nk=16, k0=0, vt=0, bias=None)]
                        if bi > 0:
                            i = bi - 1
                            cls = 0 if i == 0 else (2 if i == nblk - 1 else 1)
                            kt_lo = 4 * i + (0 if cls == 0 else -2)
                            nkt = 8 if cls == 1 else 6
                            kcol0 = 16 + 128 * kt_lo
                            load(nkw[:, :, 16:16 + 128 * nkt], NKv[:, :, kcol0:kcol0 + 128 * nkt], [nkw.tok])
                            load(nvw[:, 1:1 + nkt, :], NV[kcol0:kcol0 + 128 * nkt, :].rearrange("(t p) f -> p t f", p=128), [nvw.tok])
                            for kr in range(nkt):
                                n_units.append(dict(nk=128, k0=16 + 128 * kr, vt=1 + kr, bias=(cls, kr)))

                        def n_qk(h, u, nb=nb):
                            un = n_units[u]
                            po = 64 * (h % 2)
                            return (nkw[po:po + 64, h // 2, un["k0"]:un["k0"] + un["nk"]], nqb[po:po + 64, h // 2, :nb],
                                    [nkw.tok, nqb.tok])

                        def n_post(h, u, b, nb=nb):
                            un = n_units[u]
                            nk = un["nk"]
                            pt = ptb[pt_i[0] % 3]
                            pt_i[0] += 1
                            if un["bias"] is None:
                                act(pt[:nk, :nb], b[:nk, :nb], AF.Exp, [b.tok, mb.tok], [pt.tok], scale=NA_SCALE,
                                    bias=mb[:, l * 4 + h:l * 4 + h + 1])
                            else:
                                cls_, kr = un["bias"]
                                bt = nbias[nb_i[0] % 3]
                                nb_i[0] += 1
                                load(bt[:, :], nab_d[l, cls_, kr, h], [bt.tok])
                                s = sbb[sb_i[0] % 2]
                                sb_i[0] += 1
                                stt(s[:, :nb], b[:, :nb], NA_SCALE, bt[:, :nb], ALU.mult, ALU.add, [b.tok, bt.tok], [s.tok])
                                act(pt[:nk, :nb], s[:nk, :nb], AF.Exp, [s.tok], [pt.tok])
                            return pt

                        def n_v(h, u):
                            un = n_units[u]
                            return nvw[:un["nk"], un["vt"], h * 65:(h + 1) * 65], nvw.tok

                        attention(nb, 4, n_units, n_qk, n_v, 0, NA_SCALE, n_post)

                        for j in range(qtiles):
                            ss = ssm[ss_i[0] % 2]
                            ss_i[0] += 1
                            act(junk[:qs, 0:256], ym[:qs, j, 0:256], AF.Square, [ym_tok[j]], [junk.tok, ss.tok], accum=ss[:qs, 0:1])
                            act(junk[:qs, 0:512], ym[:qs, j, 256:768], AF.Square, [ym_tok[j]], [junk.tok, ss.tok], accum=ss[:qs, 1:2])
                            act(ss[:qs, 2:3], ss[:qs, 0:1], AF.Sqrt, [ss.tok], [ss.tok], scale=1.0 / 256, bias=EPS)
                            act(ss[:qs, 3:4], ss[:qs, 1:2], AF.Sqrt, [ss.tok], [ss.tok], scale=1.0 / 512, bias=EPS)
                            recip(ss[:qs, 2:4], ss[:qs, 2:4], [ss.tok], [ss.tok])
                            ts(ym[:qs, j, 0:256], ym[:qs, j, 0:256], ss[:qs, 2:3], None, ALU.mult, None, [ym_tok[j], ss.tok], [ym_tok[j]])
                            ts(ym[:qs, j, 256:768], ym[:qs, j, 256:768], ss[:qs, 3:4], None, ALU.mult, None, [ym_tok[j], ss.tok], [ym_tok[j]])
                            for c in range(6):
                                b = bank((4, 5, 6, 7))
                                P.op("pe", lambda e, b=b, c=c, j=j, qs=qs, ym=ym: e.transpose(
                                    out=b[:, :qs], in_=ym[:qs, j, c * 128:(c + 1) * 128], identity=ident[:qs, :qs]),
                                    [ym_tok[j], ident.tok], [b.tok])
                                gb = (G_NON + c) if c < 2 else (G_MON + c - 2)
                                act(yt[:, 2 + c, j * 128:j * 128 + qs], b[:, :qs], AF.Copy, [b.tok, gv.tok], [yt.tok],
                                    scale=gcol(l, gb))

                        load(cvh[:, :, 0:nb + 2], CVv[:, :, c0:c0 + nb + 2], [cvh.tok])
                        load(cbb[:, :, :nb], CBv[:, :, c0:c0 + nb], [cbb.tok])
                        for c in range(2):
                            ts(ctmp[:, c, :nb], cvh[:, c, 0:nb], gcol(l, G_CONVW + 0 + c), None, ALU.mult, None,
                               [cvh.tok, gv.tok], [ctmp.tok])
                            stt(ctmp[:, c, :nb], cvh[:, c, 1:nb + 1], gcol(l, G_CONVW + 2 + c), ctmp[:, c, :nb], ALU.mult, ALU.add,
                                [cvh.tok, ctmp.tok, gv.tok], [ctmp.tok])
                            stt(ctmp[:, c, :nb], cvh[:, c, 2:nb + 2], gcol(l, G_CONVW + 4 + c), ctmp[:, c, :nb], ALU.mult, ALU.add,
                                [cvh.tok, ctmp.tok, gv.tok], [ctmp.tok])
                            tt(ctmp[:, c, :nb], ctmp[:, c, :nb], cbb[:, c, :nb], ALU.mult, [ctmp.tok, cbb.tok], [ctmp.tok])
                            act(csq[:, c, :nb], ctmp[:, c, :nb], AF.Square, [ctmp.tok], [csq.tok])
                        for c in range(2):
                            mm(banks[7][:, :nb], ones[:, :], csq[:, c, :nb], c == 0, c == 1, [csq.tok, ones.tok], [banks[7].tok],
                               contig=False)
                        act(crs[:, :nb], banks[7][:, :nb], AF.Sqrt, [banks[7].tok], [crs.tok], scale=1.0 / 256, bias=EPS)
                        recip(crs[:, :nb], crs[:, :nb], [crs.tok], [crs.tok])
                        for c in range(2):
                            stt(yt[:, c, :nb], ctmp[:, c, :nb], gcol(l, G_CON + c), crs[:, :nb], ALU.mult, ALU.mult,
                                [ctmp.tok, crs.tok, gv.tok], [yt.tok])
                        store(YTv[:, :, c0:c0 + nb], yt[:, :, :nb], [yt.tok])
                P.barrier()

        P.barrier()
        P.run()
    return nc


def _na_tables(rpb):
    D = rpb.shape[0]
    R = 32
    out = np.full((D, 3, 8, 4, 128, 512), NEG, dtype=np.float32)
    qc = np.arange(64)
    kc = np.arange(64)
    cs = np.clip(qc - 8, 0, 48)
    col_valid = (kc[:, None] >= cs[None, :]) & (kc[:, None] < cs[None, :] + 16)
    col_rel = np.clip(kc[:, None] - qc[None, :] + 15, 0, 30)
    for cls, (r0, kt_lo, nkt) in enumerate(((0, 0, 6), (8, 2, 8), (24, 10, 6))):
        for kti in range(nkt):
            for p in range(2):
                kr = 2 * (kt_lo + kti) + p
                for j in range(8):
                    r = r0 + j
                    rs = min(max(r - 4, 0), R - 8)
                    if not (rs <= kr < rs + 8):
                        continue
                    row_rel = kr - r + 7
                    vals = rpb[:, :, row_rel, :][:, :, col_rel]
                    vals = np.where(col_valid[None, None], vals, np.float32(NEG))
                    out[:, cls, kti, :, p * 64:(p + 1) * 64, j * 64:(j + 1) * 64] = vals
    return out


def _prep_shared(inp, depth, Lmax):
    f = lambda a: np.ascontiguousarray(np.asarray(a, dtype=np.float32))
    w_in = f(inp["w_in"])
    pad64 = np.zeros((depth, 1024, 64), np.float32)
    w_in_ext = np.concatenate([w_in[:, :, :2560], pad64, w_in[:, :, 2560:2592], pad64,
                               w_in[:, :, 2576:2592], w_in[:, :, 2560:2576]], axis=2)
    w_uq = f(inp["mla_w_uq"]).reshape(depth, 768, 8, 96)
    w_uq_ext = np.concatenate([w_uq[..., 0:64], w_uq[..., 64:96], w_uq[..., 80:96], w_uq[..., 64:80]], axis=3)
    w_uq_ext = w_uq_ext.reshape(depth, 768, 1024)
    w_ukv_ext = f(inp["mla_w_ukv"])

    def cols(v):
        v = f(v)
        return v.reshape(depth, -1, 128).transpose(2, 0, 1)

    gparts = [cols(inp[k]) for k in ("ffn1_pre_norm", "ffn1_post_norm", "mix_pre_norm", "mla_q_norm", "mla_kv_norm",
                                     "conv_out_norm", "na_out_norm", "mla_out_norm", "mix_post_norm",
                                     "ffn2_pre_norm", "ffn2_post_norm")]
    cw = f(inp["conv_w"]).reshape(depth, 3, 2, 128).transpose(3, 0, 1, 2).reshape(128, depth, 6)
    gv = np.concatenate(gparts + [cw], axis=2)
    assert gv.shape == (128, depth, NG), gv.shape
    gv = np.ascontiguousarray(gv.reshape(128, depth * NG))
    mb = np.ascontiguousarray(f(inp["na_meta_bias"]).transpose(2, 0, 1).reshape(16, depth * 4))
    pos = np.arange(Lmax, dtype=np.float32)
    inv_freq = (np.float32(10000.0) ** (-np.arange(16, dtype=np.float32) / np.float32(16))).astype(np.float32)
    ang = (pos[None, :] * inv_freq[:, None]).astype(np.float32)
    cos = np.cos(ang).astype(np.float32)
    sin = np.sin(ang).astype(np.float32)
    ropeC = np.ascontiguousarray(np.concatenate([cos, cos], axis=0))
    ropeS = np.ascontiguousarray(np.concatenate([-sin, sin], axis=0))
    return {
        "meta": f(inp["meta_tokens"]),
        "w_gu1": f(inp["ffn1_w_gu"]), "w_d1": f(inp["ffn1_w_down"]),
        "w_gu2": f(inp["ffn2_w_gu"]), "w_d2": f(inp["ffn2_w_down"]),
        "w_in": np.ascontiguousarray(w_in_ext), "w_uq": np.ascontiguousarray(w_uq_ext),
        "w_ukv": np.ascontiguousarray(w_ukv_ext), "w_o": f(inp["w_o"]),
        "gv": gv, "mb": mb, "ropeC": ropeC, "ropeS": ropeS,
        "nab": _na_tables(f(inp["na_rpb"])), "ident": np.eye(128, dtype=np.float32),
    }


def kernel(**inputs):
    depth = 4
    ncores = 8
    xp = np.asarray(inputs["x_prompt"], dtype=np.float32)
    xsmp = np.asarray(inputs["x_sample"], dtype=np.float32)
    seqs = [xp.shape[1], xp.shape[1], xsmp.shape[1]]
    Lmax = 16 + max(seqs)
    shared = _prep_shared(inputs, depth, Lmax)
    nc = build(seqs, depth)
    in_maps = []
    for c in range(ncores):
        m = dict(shared)
        m["x0"] = np.ascontiguousarray(xp[2 * c])
        m["x1"] = np.ascontiguousarray(xp[2 * c + 1])
        m["x2"] = np.ascontiguousarray(xsmp[c])
        in_maps.append(m)
    res = run_bass_kernel_spmd(nc, in_maps, core_ids=list(range(ncores)))
    yp = np.empty_like(xp)
    ysm = np.empty_like(xsmp)
    for c in range(ncores):
        r = res.results[c]
        yp[2 * c] = r["y0"]
        yp[2 * c + 1] = r["y1"]
        ysm[c] = r["y2"]
    return (yp, ysm)
```

```python
import contextlib
import numpy as np
import concourse.bass as bass
import concourse.mybir as mybir
from concourse.bass_utils import run_bass_kernel_spmd

F32 = mybir.dt.float32
BF16 = mybir.dt.bfloat16
AF = mybir.ActivationFunctionType
ALU = mybir.AluOpType

D_MODEL = 1024
FFN = 2816
NG = 70
EPS = 1e-6
NEG = -30000.0
MLA_SCALE = 96.0 ** -0.5
NA_SCALE = 0.125
G_F1PRE, G_F1POST, G_MPRE, G_QN, G_KVN, G_CON, G_NON, G_MON, G_MPOST, G_F2PRE, G_F2POST, G_CONVW = (
    0, 8, 16, 24, 30, 32, 34, 36, 40, 48, 56, 64)


ALL_TOKS = []


class Tok:
    __slots__ = ("w", "r")

    def __init__(self):
        self.w = None
        self.r = {}
        ALL_TOKS.append(self)


class Tile:
    def __init__(self, t, tok=None):
        self.t = t
        self.tok = tok or Tok()

    def __getitem__(self, idx):
        return self.t[idx]


class Prog:
    CE = ("pe", "act", "dve", "pool")
    ENG = ("pe", "act", "dve", "pool", "sp")
    NSETS = 4

    def __init__(self, nc, es, n_dma=40):
        self.nc = nc
        self.psem = {(e, s): es.enter_context(nc.semaphore("p_%s%d" % (e, s))) for e in self.CE for s in range(self.NSETS)}
        self.pcnt = {(e, s): 0 for e in self.CE for s in range(self.NSETS)}
        self.cur = 0
        self.dsem = [es.enter_context(nc.semaphore("d%d" % i)) for i in range(n_dma)]
        self.dcnt = [0] * n_dma
        self.dnext = 0
        self.seen = {e: {} for e in self.ENG}
        self.streams = {e: [] for e in self.ENG}
        self.nops = 0

    def _deps(self, reads, writes):
        d = {}
        for t in reads:
            if t.w is not None:
                k, v = t.w
                if d.get(k, 0) < v:
                    d[k] = v
        for t in writes:
            if t.w is not None:
                k, v = t.w
                if d.get(k, 0) < v:
                    d[k] = v
            for k, v in t.r.items():
                if d.get(k, 0) < v:
                    d[k] = v
        return d

    def _mark(self, reads, writes, ev):
        for t in writes:
            t.w = ev
            t.r = {}
        k, v = ev
        for t in reads:
            t.r[k] = v

    def _emit(self, eng, d, fn, evk, inc, strict=False):
        waits = []
        seen = self.seen[eng]
        for k, v in d.items():
            if eng == "pe" and k[0] == "p" and k[1] == "pe" and not strict:
                continue
            if seen.get(k, 0) < v:
                seen[k] = v
                waits.append((k, v))
        self.streams[eng].append((waits, fn, evk, inc))
        self.nops += 1

    def op(self, eng, fn, reads=(), writes=(), signal=True):
        d = self._deps(reads, writes)
        key = ("p", eng, self.cur)
        ck = (eng, self.cur)
        if signal:
            self.pcnt[ck] += 1
            assert self.pcnt[ck] < 65000, "progress semaphore overflow"
            ev = (key, self.pcnt[ck])
            self._emit(eng, d, fn, key, 1)
        else:
            ev = (key, self.pcnt[ck] + 1)
            self._emit(eng, d, fn, None, 0)
        self._mark(reads, writes, ev)

    def dma(self, q, fn, reads=(), writes=()):
        d = self._deps(reads, writes)
        i = self.dnext
        self.dnext = (i + 1) % len(self.dsem)
        k = ("d", i)
        if self.dcnt[i] > 0 and d.get(k, 0) < self.dcnt[i]:
            d[k] = self.dcnt[i]
        self.dcnt[i] += 16
        assert self.dcnt[i] < 65000, "dma semaphore overflow"
        ev = (k, self.dcnt[i])
        self._emit(q, d, fn, k, 16)
        self._mark(reads, writes, ev)

    def barrier(self, next_set=None):
        d = {("p", e, s): self.pcnt[(e, s)] for e in self.CE for s in range(self.NSETS) if self.pcnt[(e, s)] > 0}
        for i, c in enumerate(self.dcnt):
            if c > 0:
                d[("d", i)] = c
        for e in self.ENG:
            self._emit(e, dict(d), None, None, 0, strict=True)
        for t in ALL_TOKS:
            t.w = None
            t.r = {}
        if next_set is not None:
            self.cur = next_set

    def check(self):
        sem = {}
        pos = {e: 0 for e in self.ENG}
        n = {e: len(self.streams[e]) for e in self.ENG}
        while True:
            prog = False
            for e in self.ENG:
                st = self.streams[e]
                while pos[e] < n[e]:
                    waits, fn, evk, inc = st[pos[e]]
                    if any(sem.get(k, 0) < v for k, v in waits):
                        break
                    if evk is not None:
                        sem[evk] = sem.get(evk, 0) + inc
                    pos[e] += 1
                    prog = True
            if all(pos[e] == n[e] for e in self.ENG):
                return
            if not prog:
                msg = []
                for e in self.ENG:
                    if pos[e] < n[e]:
                        waits = self.streams[e][pos[e]][0]
                        msg.append((e, pos[e], [(k, v, sem.get(k, 0)) for k, v in waits if sem.get(k, 0) < v]))
                raise RuntimeError("DEADLOCK in generated program: %r" % (msg,))

    def run(self):
        nc = self.nc
        self.check()

        def semof(k):
            return self.psem[(k[1], k[2])] if k[0] == "p" else self.dsem[k[1]]

        def mk(name):
            stream = self.streams[name]

            def body(e):
                for waits, fn, evk, inc in stream:
                    for k, v in waits:
                        e.wait_ge(semof(k), v)
                    if fn is not None:
                        ins = fn(e)
                        if evk is not None:
                            ins.then_inc(semof(evk), inc)
            return body

        with nc.Block() as block:
            block.tensor(mk("pe"))
            block.scalar(mk("act"))
            block.vector(mk("dve"))
            block.gpsimd(mk("pool"))
            block.sync(mk("sp"))


def build(seqs, depth, lmax=None, debug=False):
    nseq = len(seqs)
    Lmax = lmax or (16 + max(seqs))
    nc = bass.Bass("TRN2", target_bir_lowering=False)

    def din(name, shape, dt=F32):
        return nc.dram_tensor(name, list(shape), dt, kind="ExternalInput").ap()

    def dscr(name, shape, dt):
        if debug and not name.startswith("wb_"):
            return nc.dram_tensor(name, list(shape), dt, kind="ExternalOutput").ap()
        return nc.dram_tensor(name, list(shape), dt).ap()

    xs = [din("x%d" % i, [T, D_MODEL]) for i, T in enumerate(seqs)]
    ys = [nc.dram_tensor("y%d" % i, [T, D_MODEL], F32, kind="ExternalOutput").ap() for i, T in enumerate(seqs)]
    meta_d = din("meta", [16, D_MODEL])
    wsrc = {
        "gu1": din("w_gu1", [depth, 1024, 2 * FFN]), "d1": din("w_d1", [depth, FFN, 1024]),
        "gu2": din("w_gu2", [depth, 1024, 2 * FFN]), "d2": din("w_d2", [depth, FFN, 1024]),
        "in": din("w_in", [depth, 1024, 2752]), "uq": din("w_uq", [depth, 768, 1024]),
        "ukv": din("w_ukv", [depth, 256, 1024]), "o": din("w_o", [depth, 1024, 1024]),
    }
    gv_d = din("gv", [128, depth * NG])
    mb_d = din("mb", [16, depth * 4])
    ropeC_d = din("ropeC", [32, Lmax])
    ropeS_d = din("ropeS", [32, Lmax])
    nab_d = din("nab", [depth, 3, 8, 4, 128, 512])
    ident_d = din("ident", [128, 128])

    wb = {k: dscr("wb_" + k, v.shape, BF16) for k, v in wsrc.items()}
    HT = dscr("HT", [1024, Lmax], F32)
    YT = dscr("YT", [1024, Lmax], BF16)
    CB = dscr("CB", [256, Lmax], F32)
    CV = dscr("CV", [256, Lmax + 2], F32)
    NQ = dscr("NQ", [256, Lmax], BF16)
    NK = dscr("NK", [256, Lmax], BF16)
    NV = dscr("NV", [Lmax, 260], BF16)
    QC = dscr("QC", [8, 96, Lmax], BF16)
    KC = dscr("KC", [8, 96, Lmax], BF16)
    MV = dscr("MV", [Lmax, 520], BF16)

    es = contextlib.ExitStack()
    with es:
        P = Prog(nc, es)
        P.cur = 3
        es.enter_context(nc.allow_non_contiguous_dma(reason="single pad columns / small strided tiles"))

        uniq = [0]

        def sb(stack, name, shape, dt):
            uniq[0] += 1
            return Tile(stack.enter_context(nc.sbuf_tensor("s%d_%s" % (uniq[0], name), list(shape), dt)))

        ps_t = es.enter_context(nc.psum_tensor("ps", [128, 8, 512], F32))
        banks = [Tile(ps_t[:, i, :]) for i in range(8)]
        gv = sb(es, "gv", [128, depth * NG], F32)
        mb = sb(es, "mb", [16, depth * 4], F32)
        ident = sb(es, "ident", [128, 128], F32)
        ones = sb(es, "ones", [128, 128], BF16)
        zero = sb(es, "zero", [128, 2], F32)
        P.dma("sp", lambda e: e.dma_start(out=gv[:, :], in_=gv_d[:, :]), writes=[gv.tok])
        P.dma("sp", lambda e: e.dma_start(out=mb[:, :], in_=mb_d[:, :]), writes=[mb.tok])
        P.dma("sp", lambda e: e.dma_start(out=ident[:, :], in_=ident_d[:, :]), writes=[ident.tok])
        ones_f = sb(es, "ones_f", [128, 128], F32)
        gvh = sb(es, "gvh", [128, depth * NG], F32)
        P.op("dve", lambda e: e.memset(ones_f[:, :], 1.0), writes=[ones_f.tok])
        P.op("dve", lambda e: e.tensor_scalar(out=gvh[:, :], in0=gv[:, :], scalar1=0.5, scalar2=None, op0=ALU.mult),
             [gv.tok], [gvh.tok])
        P.op("dve", lambda e: e.memset(ones[:, :], 1.0), writes=[ones.tok])
        P.op("dve", lambda e: e.memset(zero[:, :], 0.0), writes=[zero.tok])

        def gcol(l, base):
            return gv[:, l * NG + base:l * NG + base + 1]

        def gcolh(l, base):
            return gvh[:, l * NG + base:l * NG + base + 1]

        def mm(out, lhsT, rhs, start, stop, reads, writes, contig=True):
            P.op("pe", lambda e: e.matmul(out, lhsT=lhsT, rhs=rhs, start=start, stop=stop), reads, writes,
                 signal=(stop or not contig))

        def act(out, in_, func, reads, writes, scale=1.0, bias=0.0, accum=None):
            if accum is None:
                P.op("act", lambda e: e.activation(out=out, in_=in_, func=func, scale=scale, bias=bias),
                     reads, writes)
            else:
                P.op("act", lambda e: e.activation(out=out, in_=in_, func=func, scale=scale, bias=bias,
                                                   accum_out=accum), reads, writes)

        def stt(out, in0, scalar, in1, op0, op1, reads, writes):
            P.op("dve", lambda e: e.scalar_tensor_tensor(out=out, in0=in0, scalar=scalar, in1=in1,
                                                         op0=op0, op1=op1), reads, writes)

        def tt(out, in0, in1, op, reads, writes, eng="dve"):
            P.op(eng, lambda e: e.tensor_tensor(out=out, in0=in0, in1=in1, op=op), reads, writes)

        def ts(out, in0, s1, s2, op0, op1, reads, writes):
            if s2 is None:
                P.op("dve", lambda e: e.tensor_scalar(out=out, in0=in0, scalar1=s1, scalar2=None, op0=op0),
                     reads, writes)
            else:
                P.op("dve", lambda e: e.tensor_scalar(out=out, in0=in0, scalar1=s1, scalar2=s2, op0=op0, op1=op1),
                     reads, writes)

        def recip(out, in_, reads, writes):
            P.op("dve", lambda e: e.reciprocal(out=out, in_=in_), reads, writes)

        def load(out, in_, writes, reads=()):
            P.dma("sp", lambda e: e.dma_start(out=out, in_=in_), reads, writes)

        def store(out, in_, reads):
            P.dma("pool", lambda e: e.dma_start(out=out, in_=in_), reads, ())

        with contextlib.ExitStack() as ps_es:
            CW = 2048
            fin = [sb(ps_es, "fin%d" % i, [128, CW], F32) for i in range(3)]
            fout = [sb(ps_es, "fout%d" % i, [128, CW], BF16) for i in range(3)]
            ci = 0
            for key in ("gu1", "d1", "in", "uq", "ukv", "o", "gu2", "d2"):
                src = wsrc[key]
                dst = wb[key]
                _, R, N = src.shape
                for l in range(depth):
                    for rc in range(R // 128):
                        for c0 in range(0, N, CW):
                            n = min(CW, N - c0)
                            a, b = fin[ci % 3], fout[ci % 3]
                            load(a[:, :n], src[l, rc * 128:(rc + 1) * 128, c0:c0 + n], [a.tok])
                            eng = ("pool", "dve", "act")[ci % 3]
                            if eng == "act":
                                P.op("act", lambda e, a=a, b=b, n=n: e.copy(out=b[:, :n], in_=a[:, :n]),
                                     [a.tok], [b.tok])
                            else:
                                P.op(eng, lambda e, a=a, b=b, n=n: e.tensor_copy(out=b[:, :n], in_=a[:, :n]),
                                     [a.tok], [b.tok])
                            store(dst[l, rc * 128:(rc + 1) * 128, c0:c0 + n], b[:, :n], [b.tok])
                            ci += 1
        P.barrier(next_set=0)

        bank_rr = [0]

        def bank(pool=(0, 1, 2, 3, 4, 5)):
            b = banks[pool[bank_rr[0] % len(pool)]]
            bank_rr[0] += 1
            return b

        for si, T in enumerate(seqs):
            L = 16 + T
            if si > 0:
                P.barrier(next_set=si)
            blocks = [(0, 16)] + [(16 + 512 * i, 512) for i in range(T // 512)]
            HTv = HT.rearrange("(c p) l -> p c l", p=128)
            YTv = YT.rearrange("(c p) l -> p c l", p=128)

            with contextlib.ExitStack() as st:
                xt = [sb(st, "xt%d" % i, [128, 1024], F32) for i in range(2)]
                hb = [sb(st, "hb%d" % i, [128, 8, 512], F32) for i in range(2)]
                ti = 0
                for bi, (c0, nb) in enumerate(blocks):
                    h = hb[bi % 2]
                    for j in range((nb + 127) // 128):
                        qs = min(128, nb)
                        x_t = xt[ti % 2]
                        ti += 1
                        if bi == 0:
                            load(x_t[:qs, :], meta_d[:, :], [x_t.tok])
                        else:
                            t0 = c0 - 16 + j * 128
                            load(x_t[:qs, :], xs[si][t0:t0 + qs, :], [x_t.tok])
                        for c in range(8):
                            b = bank()
                            P.op("pe", lambda e, b=b, x_t=x_t, c=c, qs=qs: e.transpose(
                                out=b[:, :qs], in_=x_t[:qs, c * 128:(c + 1) * 128], identity=ident[:qs, :qs]),
                                [x_t.tok, ident.tok], [b.tok])
                            if c % 2 == 0:
                                P.op("dve", lambda e, b=b, h=h, c=c, j=j, qs=qs: e.tensor_copy(
                                    out=h[:, c, j * 128:j * 128 + qs], in_=b[:, :qs]), [b.tok], [h.tok])
                            else:
                                P.op("act", lambda e, b=b, h=h, c=c, j=j, qs=qs: e.copy(
                                    out=h[:, c, j * 128:j * 128 + qs], in_=b[:, :qs]), [b.tok], [h.tok])
                    store(HTv[:, :, c0:c0 + nb], h[:, :, :nb], [h.tok])
                CVv = CV.rearrange("(c p) l -> p c l", p=128)
                for c in range(2):
                    store(CVv[:, c, 0:1], zero[:, 0:1], [zero.tok])
                    store(CVv[:, c, L + 1:L + 2], zero[:, 1:2], [zero.tok])
            P.barrier()

            for l in range(depth + 1):
                with contextlib.ExitStack() as st:
                    hblk = sb(st, "hblk", [128, 8, 512], F32)
                    hb_tok = [Tok() for _ in range(8)]
                    xn = sb(st, "xn", [128, 8, 512], BF16)
                    xn_tok = [Tok() for _ in range(8)]
                    actT = sb(st, "actT", [128, 22, 512], BF16)
                    actT_tok = [Tok() for _ in range(22)]
                    obuf = sb(st, "obuf", [128, 8, 512], F32)
                    obuf_tok = [Tok() for _ in range(8)]
                    ring = [sb(st, "ring%d" % i, [128, 4096], BF16) for i in range(6)]
                    ring_i = [0]
                    sq8 = sb(st, "sq8", [128, 8, 512], BF16)
                    sq8_tok = [Tok() for _ in range(8)]
                    rtm = [sb(st, "rtm%d" % i, [128, 8], F32) for i in range(2)]
                    rtm_i = [0]
                    dmt = [sb(st, "dm%d" % i, [128, 4, 128], F32) for i in range(2)]
                    dm_i = [0]
                    RB = banks[6]
                    SS = banks[7]
                    sgb = [sb(st, "sg%d" % i, [128, 512], F32) for i in range(2)]
                    sg_i = [0]
                    ytb = sb(st, "ytb", [128, 8, 512], BF16)
                    qn = sb(st, "qn", [128, 6, 512], BF16)
                    kvn = sb(st, "kvn", [128, 2, 512], BF16)
                    cct = sb(st, "cct", [128, 2, 512], F32)
                    cbt = sb(st, "cbt", [128, 2, 512], F32)
                    cvt = sb(st, "cvt", [128, 2, 512], F32)
                    nqt = sb(st, "nqt", [128, 2, 512], BF16)
                    nkt = sb(st, "nkt", [128, 2, 512], BF16)
                    nvt = sb(st, "nvt", [128, 4, 260], BF16)
                    qct = sb(st, "qct", [96, 8, 512], BF16)
                    kct = sb(st, "kct", [96, 8, 512], BF16)
                    mvt = sb(st, "mvt", [128, 4, 520], BF16)
                    kper = sb(st, "kper", [96, 512], BF16)
                    rt1 = sb(st, "rt1", [96, 512], F32)
                    rt2 = sb(st, "rt2", [96, 512], F32)
                    rC = sb(st, "rC", [96, 512], F32)
                    rS = sb(st, "rS", [96, 512], F32)
                    yout = [sb(st, "yout%d" % i, [128, 1024], F32) for i in range(2)]
                    P.op("pool", lambda e, nvt=nvt: e.memset(nvt[:, :, :], 1.0), writes=[nvt.tok])
                    P.op("pool", lambda e, mvt=mvt: e.memset(mvt[:, :, :], 1.0), writes=[mvt.tok])

                    def load_w(W, KC_, c0, ncols, r0=0):
                        s = ring[ring_i[0] % len(ring)]
                        ring_i[0] += 1
                        view = s.t[:, 0:KC_ * ncols].rearrange("p (c n) -> p c n", c=KC_)
                        src = W.rearrange("(c p) n -> p c n", p=128)[:, r0:r0 + KC_, c0:c0 + ncols]
                        load(view, src, [s.tok])
                        return view, s.tok

                    def square_to(c, ap, tk, nb, from_psum, k):
                        if from_psum or k % 2 == 0:
                            act(sq8[:, c, :nb], ap, AF.Square, [tk], [sq8_tok[c]])
                        else:
                            tt(sq8[:, c, :nb], ap, ap, ALU.mult, [tk], [sq8_tok[c]])

                    def rstd_bcast(clist, nb, D):
                        ntile = (nb + 127) // 128
                        qs = min(128, nb)
                        nch = len(clist)
                        for j in range(ntile):
                            for i, c in enumerate(clist):
                                mm(SS[:qs, j:j + 1], sq8[:, c, j * 128:j * 128 + qs], ones[:, 0:1], i == 0, i == nch - 1,
                                   [sq8_tok[c], ones.tok], [SS.tok])
                        r = rtm[rtm_i[0] % 2]
                        rtm_i[0] += 1
                        act(r[:qs, 0:ntile], SS[:qs, 0:ntile], AF.Sqrt, [SS.tok], [r.tok], scale=1.0 / D, bias=EPS)
                        recip(r[:qs, 0:ntile], r[:qs, 0:ntile], [r.tok], [r.tok])
                        dm = dmt[dm_i[0] % 2]
                        dm_i[0] += 1
                        tt(dm[:qs, 0:ntile, :qs], ident[:qs, :qs].unsqueeze(1).to_broadcast([qs, ntile, qs]),
                           r[:qs, 0:ntile].unsqueeze(2).to_broadcast([qs, ntile, qs]), ALU.mult, [ident.tok, r.tok], [dm.tok])
                        for j in range(ntile):
                            P.op("pe", lambda e, j=j, dm=dm, qs=qs: e.matmul(
                                RB[:, j * 128:j * 128 + qs], lhsT=ones_f[:qs, :], rhs=dm[:qs, j, :qs], start=True, stop=True),
                                [dm.tok, ones_f.tok], [RB.tok])
                        return RB

                    def rms_sbuf(srcs, nb, D, c0_=0, act_only=False):
                        for i, (ap, tk) in enumerate(srcs):
                            square_to(c0_ + i, ap, tk, nb, False, 0 if act_only else i)
                        return rstd_bcast([c0_ + i for i in range(len(srcs))], nb, D)

                    def post_chunk(dc, b, nb):
                        P.op("dve", lambda e, dc=dc, b=b, nb=nb, obuf=obuf: e.tensor_copy(out=obuf[:, dc, :nb], in_=b[:, :nb]),
                             [b.tok], [obuf_tok[dc]])
                        square_to(dc, obuf[:, dc, :nb], obuf_tok[dc], nb, True, 0)

                    def post_end(l_, gbase, coef, nb):
                        r = rstd_bcast(list(range(8)), nb, 1024)
                        for dc in range(8):
                            g_ = gcolh(l_, gbase + dc) if coef == 0.5 else gcol(l_, gbase + dc)
                            stt(obuf[:, dc, :nb], obuf[:, dc, :nb], g_, r[:, :nb], ALU.mult, ALU.mult,
                                [obuf_tok[dc], r.tok, gv.tok, gvh.tok], [obuf_tok[dc]])
                            tt(hblk[:, dc, :nb], hblk[:, dc, :nb], obuf[:, dc, :nb], ALU.add,
                               [obuf_tok[dc], hb_tok[dc]], [hb_tok[dc]], eng=("pool" if dc % 4 == 3 else "dve"))

                    def make_xn(l_, gbase, nb):
                        r = rms_sbuf([(hblk[:, c, :nb], hb_tok[c]) for c in range(8)], nb, 1024, act_only=True)
                        for c in range(8):
                            stt(xn[:, c, :nb], hblk[:, c, :nb], gcol(l_, gbase + c), r[:, :nb], ALU.mult, ALU.mult,
                                [hb_tok[c], r.tok, gv.tok], [xn_tok[c]])

                    def ffn(l_, which, nb):
                        make_xn(l_, G_F1PRE if which == 1 else G_F2PRE, nb)
                        Wgu = wb["gu%d" % which][l_]
                        Wd = wb["d%d" % which][l_]
                        for fg in range(6):
                            f0 = fg * 512
                            nf = min(512, FFN - f0)
                            wg, wg_tok = load_w(Wgu, 8, f0, nf)
                            wu, wu_tok = load_w(Wgu, 8, FFN + f0, nf)
                            for jj in range(nf // 128):
                                j = fg * 4 + jj
                                bg = bank()
                                bu = bank()
                                for kc in range(8):
                                    mm(bg[:, :nb], wg[:, kc, jj * 128:(jj + 1) * 128], xn[:, kc, :nb], kc == 0, kc == 7,
                                       [wg_tok, xn_tok[kc]], [bg.tok])
                                for kc in range(8):
                                    mm(bu[:, :nb], wu[:, kc, jj * 128:(jj + 1) * 128], xn[:, kc, :nb], kc == 0, kc == 7,
                                       [wu_tok, xn_tok[kc]], [bu.tok])
                                s = sgb[sg_i[0] % 2]
                                sg_i[0] += 1
                                act(s[:, :nb], bg[:, :nb], AF.Silu, [bg.tok], [s.tok])
                                tt(actT[:, j, :nb], s[:, :nb], bu[:, :nb], ALU.mult, [s.tok, bu.tok], [actT_tok[j]])
                        parts = [(0, 8), (8, 16), (16, 22)]
                        for half in range(2):
                            sl = [load_w(Wd, j1 - j0, half * 512, 512, r0=j0) for (j0, j1) in parts]
                            for dcr in range(4):
                                dc = half * 4 + dcr
                                b = bank()
                                for j in range(22):
                                    pi = 0 if j < 8 else (1 if j < 16 else 2)
                                    w, wt = sl[pi]
                                    mm(b[:, :nb], w[:, j - parts[pi][0], dcr * 128:(dcr + 1) * 128], actT[:, j, :nb],
                                       j == 0, j == 21, [wt, actT_tok[j]], [b.tok])
                                post_chunk(dc, b, nb)
                        post_end(l_, G_F1POST if which == 1 else G_F2POST, 0.5, nb)

                    def mixer_out(l_, c0, nb):
                        load(ytb[:, :, :nb], YTv[:, :, c0:c0 + nb], [ytb.tok])
                        for half in range(2):
                            w, wt = load_w(wb["o"][l_], 8, half * 512, 512)
                            for dcr in range(4):
                                dc = half * 4 + dcr
                                b = bank()
                                for c in range(8):
                                    mm(b[:, :nb], w[:, c, dcr * 128:(dcr + 1) * 128], ytb[:, c, :nb], c == 0, c == 7,
                                       [wt, ytb.tok], [b.tok])
                                post_chunk(dc, b, nb)
                        post_end(l_, G_MPOST, 1.0, nb)

                    def projections(l_, c0, nb):
                        make_xn(l_, G_MPRE, nb)
                        Win = wb["in"][l_]
                        ql = obuf
                        load(rC[64:96, :nb], ropeC_d[:, c0:c0 + nb], [rC.tok])
                        load(rS[64:96, :nb], ropeS_d[:, c0:c0 + nb], [rS.tok])
                        ntile = (nb + 127) // 128
                        qs = min(128, nb)
                        for g in range(6):
                            gc0 = g * 512
                            ncol = min(512, 2752 - gc0)
                            w, wt = load_w(Win, 8, gc0, ncol)
                            if g == 5:
                                bA = bank()
                                bB = bank()
                                for kc in range(8):
                                    mm(bA[:96, :nb], w[:, kc, 0:96], xn[:, kc, :nb], kc == 0, kc == 7, [wt, xn_tok[kc]], [bA.tok])
                                for kc in range(8):
                                    mm(bB[:96, :nb], w[:, kc, 96:192], xn[:, kc, :nb], kc == 0, kc == 7, [wt, xn_tok[kc]], [bB.tok])
                                tt(rt1[64:96, :nb], bA[64:96, :nb], rC[64:96, :nb], ALU.mult, [bA.tok, rC.tok], [rt1.tok])
                                tt(rt2[64:96, :nb], bB[64:96, :nb], rS[64:96, :nb], ALU.mult, [bB.tok, rS.tok], [rt2.tok])
                                tt(kper[64:96, :nb], rt1[64:96, :nb], rt2[64:96, :nb], ALU.add, [rt1.tok, rt2.tok], [kper.tok])
                                continue
                            for oc_r in range(4):
                                oc = g * 4 + oc_r
                                if oc in (10, 11):
                                    continue
                                b = bank()
                                for kc in range(8):
                                    mm(b[:, :nb], w[:, kc, oc_r * 128:(oc_r + 1) * 128], xn[:, kc, :nb], kc == 0, kc == 7,
                                       [wt, xn_tok[kc]], [b.tok])
                                if oc < 2:
                                    act(cbt[:, oc, :nb], b[:, :nb], AF.Copy, [b.tok], [cbt.tok])
                                elif oc < 4:
                                    act(cct[:, oc - 2, :nb], b[:, :nb], AF.Copy, [b.tok], [cct.tok])
                                elif oc < 6:
                                    tt(cvt[:, oc - 4, :nb], cct[:, oc - 4, :nb], b[:, :nb], ALU.mult, [cct.tok, b.tok], [cvt.tok])
                                elif oc < 8:
                                    act(nqt[:, oc - 6, :nb], b[:, :nb], AF.Copy, [b.tok], [nqt.tok])
                                elif oc < 10:
                                    act(nkt[:, oc - 8, :nb], b[:, :nb], AF.Copy, [b.tok], [nkt.tok])
                                else:
                                    act(ql[:, oc - 12, :nb], b[:, :nb], AF.Copy, [b.tok], [obuf_tok[oc - 12]])
                            if g == 2:
                                for j in range(ntile):
                                    b = bank()
                                    for kc in range(8):
                                        mm(b[:qs, 0:256], xn[:, kc, j * 128:j * 128 + qs], w[:, kc, 256:512], kc == 0, kc == 7,
                                           [wt, xn_tok[kc]], [b.tok])
                                    P.op("dve", lambda e, b=b, j=j, nvt=nvt, qs=qs: e.tensor_copy(
                                        out=nvt[:qs, j, :].rearrange("p (h d) -> p h d", h=4)[:, :, 0:64],
                                        in_=b[:qs, 0:256].rearrange("p (h d) -> p h d", h=4)), [b.tok], [nvt.tok])
                        CBv = CB.rearrange("(c p) l -> p c l", p=128)
                        CVv = CV.rearrange("(c p) l -> p c l", p=128)
                        NQv = NQ.rearrange("(c p) l -> p c l", p=128)
                        NKv = NK.rearrange("(c p) l -> p c l", p=128)
                        store(CBv[:, :, c0:c0 + nb], cbt[:, :, :nb], [cbt.tok])
                        store(CVv[:, :, c0 + 1:c0 + 1 + nb], cvt[:, :, :nb], [cvt.tok])
                        store(NQv[:, :, c0:c0 + nb], nqt[:, :, :nb], [nqt.tok])
                        store(NKv[:, :, c0:c0 + nb], nkt[:, :, :nb], [nkt.tok])
                        if nb == 16:
                            store(NV[0:16, :], nvt[:16, 0, :], [nvt.tok])
                        else:
                            store(NV[c0:c0 + nb, :].rearrange("(t p) f -> p t f", p=128), nvt[:, :, :], [nvt.tok])
                        r = rms_sbuf([(ql[:, c, :nb], obuf_tok[c]) for c in range(6)], nb, 768)
                        for c in range(6):
                            stt(qn[:, c, :nb], ql[:, c, :nb], gcol(l_, G_QN + c), r[:, :nb], ALU.mult, ALU.mult,
                                [obuf_tok[c], r.tok, gv.tok], [qn.tok])
                        for hg in range(2):
                            w, wt = load_w(wb["uq"][l_], 6, hg * 512, 512)
                            for hr in range(4):
                                h = hg * 4 + hr
                                bm = bank()
                                bs = bank()
                                for c in range(6):
                                    mm(bm[:96, :nb], w[:, c, hr * 128:hr * 128 + 96], qn[:, c, :nb], c == 0, c == 5,
                                       [wt, qn.tok], [bm.tok])
                                for c in range(6):
                                    mm(bs[:96, :nb], w[:, c, hr * 128 + 32:hr * 128 + 128], qn[:, c, :nb], c == 0, c == 5,
                                       [wt, qn.tok], [bs.tok])
                                tt(rt1[64:96, :nb], bm[64:96, :nb], rC[64:96, :nb], ALU.mult, [bm.tok, rC.tok], [rt1.tok])
                                tt(rt2[64:96, :nb], bs[64:96, :nb], rS[64:96, :nb], ALU.mult, [bs.tok, rS.tok], [rt2.tok])
                                tt(qct[64:96, h, :nb], rt1[64:96, :nb], rt2[64:96, :nb], ALU.add, [rt1.tok, rt2.tok], [qct.tok])
                                act(qct[0:64, h, :nb], bm[0:64, :nb], AF.Copy, [bm.tok], [qct.tok])
                        store(QC.rearrange("h p l -> p h l")[:, :, c0:c0 + nb], qct[:, :, :nb], [qct.tok])
                        r = rms_sbuf([(ql[:, 6 + c, :nb], obuf_tok[6 + c]) for c in range(2)], nb, 256, c0_=6)
                        for c in range(2):
                            stt(kvn[:, c, :nb], ql[:, 6 + c, :nb], gcol(l_, G_KVN + c), r[:, :nb], ALU.mult, ALU.mult,
                                [obuf_tok[6 + c], r.tok, gv.tok], [kvn.tok])
                        w, wt = load_w(wb["ukv"][l_], 2, 0, 1024)
                        for h in range(8):
                            b = bank()
                            for c in range(2):
                                mm(b[:64, :nb], w[:, c, h * 128:h * 128 + 64], kvn[:, c, :nb], c == 0, c == 1,
                                   [wt, kvn.tok], [b.tok])
                            act(kct[0:64, h, :nb], b[0:64, :nb], AF.Copy, [b.tok], [kct.tok])
                            P.op("pool", lambda e, h=h, kct=kct, kper=kper, nb=nb: e.tensor_copy(out=kct[64:96, h, :nb], in_=kper[64:96, :nb]),
                                 [kper.tok], [kct.tok])
                        store(KC.rearrange("h p l -> p h l")[:, :, c0:c0 + nb], kct[:, :, :nb], [kct.tok])
                        wv = w.rearrange("p c (h d) -> p c h d", h=8)[:, :, :, 64:128]
                        for j in range(ntile):
                            b = bank()
                            for c in range(2):
                                mm(b[:qs, :].rearrange("p (h d) -> p h d", h=8), kvn[:, c, j * 128:j * 128 + qs], wv[:, c],
                                   c == 0, c == 1, [wt, kvn.tok], [b.tok])
                            P.op("dve", lambda e, b=b, j=j, mvt=mvt, qs=qs: e.tensor_copy(
                                out=mvt[:qs, j, :].rearrange("p (h d) -> p h d", h=8)[:, :, 0:64],
                                in_=b[:qs, :].rearrange("p (h d) -> p h d", h=8)), [b.tok], [mvt.tok])
                        if nb == 16:
                            store(MV[0:16, :], mvt[:16, 0, :], [mvt.tok])
                        else:
                            store(MV[c0:c0 + nb, :].rearrange("(t p) f -> p t f", p=128), mvt[:, :, :], [mvt.tok])

                    def final_out(c0, nb):
                        for j in range(nb // 128):
                            yo = yout[j % 2]
                            for half in range(2):
                                b = bank()
                                for cr in range(4):
                                    c = half * 4 + cr
                                    P.op("pe", lambda e, b=b, c=c, cr=cr, j=j, hblk=hblk: e.transpose(
                                        out=b[:, cr * 128:(cr + 1) * 128], in_=hblk[:, c, j * 128:(j + 1) * 128],
                                        identity=ident[:, :]), hb_tok + [ident.tok], [b.tok])
                                if half == 0:
                                    P.op("dve", lambda e, b=b, yo=yo: e.tensor_copy(out=yo[:, 0:512], in_=b[:, :]),
                                         [b.tok], [yo.tok])
                                else:
                                    P.op("act", lambda e, b=b, yo=yo: e.copy(out=yo[:, 512:1024], in_=b[:, :]),
                                         [b.tok], [yo.tok])
                            t0 = c0 - 16 + j * 128
                            store(ys[si][t0:t0 + 128, :], yo[:, :], [yo.tok])

                    for (c0, nb) in blocks:
                        if l == depth and nb == 16:
                            continue
                        load(hblk[:, :, :nb], HTv[:, :, c0:c0 + nb], hb_tok)
                        if l > 0:
                            mixer_out(l - 1, c0, nb)
                            ffn(l - 1, 2, nb)
                        if l < depth:
                            ffn(l, 1, nb)
                            store(HTv[:, :, c0:c0 + nb], hblk[:, :, :nb], hb_tok)
                            projections(l, c0, nb)
                        else:
                            final_out(c0, nb)
                P.barrier()
                if l == depth:
                    break

                with contextlib.ExitStack() as st:
                    ntile_k = 1 + T // 128
                    kcsb = sb(st, "kcsb", [128, 8, L], BF16)
                    mvsb = sb(st, "mvsb", [128, ntile_k, 520], BF16)
                    qcb = [sb(st, "qcb%d" % i, [128, 8, 512], BF16) for i in range(2)]
                    ptb = [sb(st, "pt%d" % i, [128, 512], BF16) for i in range(3)]
                    pt_i = [0]
                    nkw = sb(st, "nkw", [128, 2, 16 + 1024], BF16)
                    nvw = sb(st, "nvw", [128, 9, 260], BF16)
                    nqb = sb(st, "nqb", [128, 2, 512], BF16)
                    nbias = [sb(st, "nbias%d" % i, [128, 512], F32) for i in range(3)]
                    nb_i = [0]
                    sbb = [sb(st, "sbb%d" % i, [128, 512], F32) for i in range(2)]
                    sb_i = [0]
                    ym = sb(st, "ym", [128, 4, 768], F32)
                    ym_tok = [Tok() for _ in range(4)]
                    yt = sb(st, "yt", [128, 8, 512], BF16)
                    cvh = sb(st, "cvh", [128, 2, 514], F32)
                    cbb = sb(st, "cbb", [128, 2, 512], F32)
                    ctmp = sb(st, "ctmp", [128, 2, 512], F32)
                    csq = sb(st, "csq", [128, 2, 512], BF16)
                    crs = sb(st, "crs", [128, 512], F32)
                    junk = sb(st, "junk", [128, 512], F32)
                    ssm = [sb(st, "ssm%d" % i, [128, 4], F32) for i in range(2)]
                    ss_i = [0]
                    rdt = [sb(st, "rdt%d" % i, [128, 1], F32) for i in range(4)]
                    rd_i = [0]

                    P.op("pool", lambda e, kcsb=kcsb: e.memset(kcsb[96:128, :, :], 0.0), writes=[kcsb.tok])
                    for q_ in qcb:
                        P.op("pool", lambda e, q_=q_: e.memset(q_[96:128, :, :], 0.0), writes=[q_.tok])
                    for h in range(8):
                        load(kcsb[0:96, h, :], KC[h, :, 0:L], [kcsb.tok])
                    load(mvsb[:16, 0, :], MV[0:16, :], [mvsb.tok])
                    load(mvsb[:, 1:, :], MV[16:16 + T, :].rearrange("(t p) f -> p t f", p=128), [mvsb.tok])
                    obanks = banks[0:4]
                    SB = (4, 5, 6)

                    def attention(nq, nheads, units, qk_ops, vsrc, ycol0, scale, post):
                        qtiles = (nq + 127) // 128
                        qs = min(128, nq)
                        for h in range(nheads):
                            n = len(units)

                            def qk(u):
                                b = bank(SB)
                                lhsT, rhs, rd = qk_ops(h, u)
                                mm(b[:units[u]["nk"], :nq], lhsT, rhs, True, True, rd, [b.tok])
                                return b

                            def pv(u, b):
                                nk = units[u]["nk"]
                                pt = post(h, u, b)
                                vap, vtok = vsrc(h, u)
                                for j in range(qtiles):
                                    mm(obanks[j][:qs, 0:65], pt[:nk, j * 128:j * 128 + qs], vap, u == 0, u == n - 1,
                                       [pt.tok, vtok], [obanks[j].tok], contig=False)

                            bcur = qk(0)
                            for u in range(n):
                                bnext = qk(u + 1) if u + 1 < n else None
                                pv(u, bcur)
                                bcur = bnext
                            for j in range(qtiles):
                                rd = rdt[rd_i[0] % 4]
                                rd_i[0] += 1
                                recip(rd[:qs, :], obanks[j][:qs, 64:65], [obanks[j].tok], [rd.tok])
                                ts(ym[:qs, j, ycol0 + h * 64:ycol0 + (h + 1) * 64], obanks[j][:qs, 0:64], rd[:qs, 0:1], None,
                                   ALU.mult, None, [obanks[j].tok, rd.tok], [ym_tok[j]])

                    nblk = T // 512
                    NQv = NQ.rearrange("(c p) l -> p c l", p=128)
                    NKv = NK.rearrange("(c p) l -> p c l", p=128)
                    CBv = CB.rearrange("(c p) l -> p c l", p=128)
                    CVv = CV.rearrange("(c p) l -> p c l", p=128)
                    for bi, (c0, nb) in enumerate(blocks):
                        qtiles = (nb + 127) // 128
                        qs = min(128, nb)
                        qc = qcb[bi % 2]
                        load(qc[0:96, :, :nb], QC.rearrange("h p l -> p h l")[:, :, c0:c0 + nb], [qc.tok])
                        m_units = [dict(nk=16, k0=0, vt=0)] + [dict(nk=128, k0=16 + 128 * t, vt=1 + t) for t in range(T // 128)]

                        def m_qk(h, u, qc=qc, nb=nb):
                            un = m_units[u]
                            return (kcsb[:, h, un["k0"]:un["k0"] + un["nk"]], qc[:, h, :nb], [kcsb.tok, qc.tok])

                        def m_post(h, u, b, nb=nb):
                            nk = m_units[u]["nk"]
                            pt = ptb[pt_i[0] % 3]
                            pt_i[0] += 1
                            act(pt[:nk, :nb], b[:nk, :nb], AF.Exp, [b.tok], [pt.tok], scale=MLA_SCALE)
                            return pt

                        def m_v(h, u):
                            un = m_units[u]
                            return mvsb[:un["nk"], un["vt"], h * 65:(h + 1) * 65], mvsb.tok

                        attention(nb, 8, m_units, m_qk, m_v, 256, MLA_SCALE, m_post)

                        load(nqb[:, :, :nb], NQv[:, :, c0:c0 + nb], [nqb.tok])
                        load(nkw[:, :, 0:16], NKv[:, :, 0:16], [nkw.tok])
                        load(nvw[:16, 0, :], NV[0:16, :], [nvw.tok])
                        n_units = [dict(nk=16, k0=0, vt=0, bias=None)]
                        if bi > 0:
                            i = bi - 1
                            cls = 0 if i == 0 else (2 if i == nblk - 1 else 1)
                            kt_lo = 4 * i + (0 if cls == 0 else -2)
                            nkt = 8 if cls == 1 else 6
                            kcol0 = 16 + 128 * kt_lo
                            load(nkw[:, :, 16:16 + 128 * nkt], NKv[:, :, kcol0:kcol0 + 128 * nkt], [nkw.tok])
                            load(nvw[:, 1:1 + nkt, :], NV[kcol0:kcol0 + 128 * nkt, :].rearrange("(t p) f -> p t f", p=128), [nvw.tok])
                            for kr in range(nkt):
                                n_units.append(dict(nk=128, k0=16 + 128 * kr, vt=1 + kr, bias=(cls, kr)))

                        def n_qk(h, u, nb=nb):
                            un = n_units[u]
                            po = 64 * (h % 2)
                            return (nkw[po:po + 64, h // 2, un["k0"]:un["k0"] + un["nk"]], nqb[po:po + 64, h // 2, :nb],
                                    [nkw.tok, nqb.tok])

                        def n_post(h, u, b, nb=nb):
                            un = n_units[u]
                            nk = un["nk"]
                            pt = ptb[pt_i[0] % 3]
                            pt_i[0] += 1
                            if un["bias"] is None:
                                act(pt[:nk, :nb], b[:nk, :nb], AF.Exp, [b.tok, mb.tok], [pt.tok], scale=NA_SCALE,
                                    bias=mb[:, l * 4 + h:l * 4 + h + 1])
                            else:
                                cls_, kr = un["bias"]
                                bt = nbias[nb_i[0] % 3]
                                nb_i[0] += 1
                                load(bt[:, :], nab_d[l, cls_, kr, h], [bt.tok])
                                s = sbb[sb_i[0] % 2]
                                sb_i[0] += 1
                                stt(s[:, :nb], b[:, :nb], NA_SCALE, bt[:, :nb], ALU.mult, ALU.add, [b.tok, bt.tok], [s.tok])
                                act(pt[:nk, :nb], s[:nk, :nb], AF.Exp, [s.tok], [pt.tok])
                            return pt

                        def n_v(h, u):
                            un = n_units[u]
                            return nvw[:un["nk"], un["vt"], h * 65:(h + 1) * 65], nvw.tok

                        attention(nb, 4, n_units, n_qk, n_v, 0, NA_SCALE, n_post)

                        for j in range(qtiles):
                            ss = ssm[ss_i[0] % 2]
                            ss_i[0] += 1
                            act(junk[:qs, 0:256], ym[:qs, j, 0:256], AF.Square, [ym_tok[j]], [junk.tok, ss.tok], accum=ss[:qs, 0:1])
                            act(junk[:qs, 0:512], ym[:qs, j, 256:768], AF.Square, [ym_tok[j]], [junk.tok, ss.tok], accum=ss[:qs, 1:2])
                            act(ss[:qs, 2:3], ss[:qs, 0:1], AF.Sqrt, [ss.tok], [ss.tok], scale=1.0 / 256, bias=EPS)
                            act(ss[:qs, 3:4], ss[:qs, 1:2], AF.Sqrt, [ss.tok], [ss.tok], scale=1.0 / 512, bias=EPS)
                            recip(ss[:qs, 2:4], ss[:qs, 2:4], [ss.tok], [ss.tok])
                            ts(ym[:qs, j, 0:256], ym[:qs, j, 0:256], ss[:qs, 2:3], None, ALU.mult, None, [ym_tok[j], ss.tok], [ym_tok[j]])
                            ts(ym[:qs, j, 256:768], ym[:qs, j, 256:768], ss[:qs, 3:4], None, ALU.mult, None, [ym_tok[j], ss.tok], [ym_tok[j]])
                            for c in range(6):
                                b = bank((4, 5, 6, 7))
                                P.op("pe", lambda e, b=b, c=c, j=j, qs=qs, ym=ym: e.transpose(
                                    out=b[:, :qs], in_=ym[:qs, j, c * 128:(c + 1) * 128], identity=ident[:qs, :qs]),
                                    [ym_tok[j], ident.tok], [b.tok])
                                gb = (G_NON + c) if c < 2 else (G_MON + c - 2)
                                act(yt[:, 2 + c, j * 128:j * 128 + qs], b[:, :qs], AF.Copy, [b.tok, gv.tok], [yt.tok],
                                    scale=gcol(l, gb))

                        load(cvh[:, :, 0:nb + 2], CVv[:, :, c0:c0 + nb + 2], [cvh.tok])
                        load(cbb[:, :, :nb], CBv[:, :, c0:c0 + nb], [cbb.tok])
                        for c in range(2):
                            ts(ctmp[:, c, :nb], cvh[:, c, 0:nb], gcol(l, G_CONVW + 0 + c), None, ALU.mult, None,
                               [cvh.tok, gv.tok], [ctmp.tok])
                            stt(ctmp[:, c, :nb], cvh[:, c, 1:nb + 1], gcol(l, G_CONVW + 2 + c), ctmp[:, c, :nb], ALU.mult, ALU.add,
                                [cvh.tok, ctmp.tok, gv.tok], [ctmp.tok])
                            stt(ctmp[:, c, :nb], cvh[:, c, 2:nb + 2], gcol(l, G_CONVW + 4 + c), ctmp[:, c, :nb], ALU.mult, ALU.add,
                                [cvh.tok, ctmp.tok, gv.tok], [ctmp.tok])
                            tt(ctmp[:, c, :nb], ctmp[:, c, :nb], cbb[:, c, :nb], ALU.mult, [ctmp.tok, cbb.tok], [ctmp.tok])
                            act(csq[:, c, :nb], ctmp[:, c, :nb], AF.Square, [ctmp.tok], [csq.tok])
                        for c in range(2):
                            mm(banks[7][:, :nb], ones[:, :], csq[:, c, :nb], c == 0, c == 1, [csq.tok, ones.tok], [banks[7].tok],
                               contig=False)
                        act(crs[:, :nb], banks[7][:, :nb], AF.Sqrt, [banks[7].tok], [crs.tok], scale=1.0 / 256, bias=EPS)
                        recip(crs[:, :nb], crs[:, :nb], [crs.tok], [crs.tok])
                        for c in range(2):
                            stt(yt[:, c, :nb], ctmp[:, c, :nb], gcol(l, G_CON + c), crs[:, :nb], ALU.mult, ALU.mult,
                                [ctmp.tok, crs.tok, gv.tok], [yt.tok])
                        store(YTv[:, :, c0:c0 + nb], yt[:, :, :nb], [yt.tok])
                P.barrier()

        P.barrier()
        P.run()
    return nc


def _na_tables(rpb):
    D = rpb.shape[0]
    R = 32
    out = np.full((D, 3, 8, 4, 128, 512), NEG, dtype=np.float32)
    qc = np.arange(64)
    kc = np.arange(64)
    cs = np.clip(qc - 8, 0, 48)
    col_valid = (kc[:, None] >= cs[None, :]) & (kc[:, None] < cs[None, :] + 16)
    col_rel = np.clip(kc[:, None] - qc[None, :] + 15, 0, 30)
    for cls, (r0, kt_lo, nkt) in enumerate(((0, 0, 6), (8, 2, 8), (24, 10, 6))):
        for kti in range(nkt):
            for p in range(2):
                kr = 2 * (kt_lo + kti) + p
                for j in range(8):
                    r = r0 + j
                    rs = min(max(r - 4, 0), R - 8)
                    if not (rs <= kr < rs + 8):
                        continue
                    row_rel = kr - r + 7
                    vals = rpb[:, :, row_rel, :][:, :, col_rel]
                    vals = np.where(col_valid[None, None], vals, np.float32(NEG))
                    out[:, cls, kti, :, p * 64:(p + 1) * 64, j * 64:(j + 1) * 64] = vals
    return out


def _prep_shared(inp, depth, Lmax):
    f = lambda a: np.ascontiguousarray(np.asarray(a, dtype=np.float32))
    w_in = f(inp["w_in"])
    pad64 = np.zeros((depth, 1024, 64), np.float32)
    w_in_ext = np.concatenate([w_in[:, :, :2560], pad64, w_in[:, :, 2560:2592], pad64,
                               w_in[:, :, 2576:2592], w_in[:, :, 2560:2576]], axis=2)
    w_uq = f(inp["mla_w_uq"]).reshape(depth, 768, 8, 96)
    w_uq_ext = np.concatenate([w_uq[..., 0:64], w_uq[..., 64:96], w_uq[..., 80:96], w_uq[..., 64:80]], axis=3)
    w_uq_ext = w_uq_ext.reshape(depth, 768, 1024)
    w_ukv_ext = f(inp["mla_w_ukv"])

    def cols(v):
        v = f(v)
        return v.reshape(depth, -1, 128).transpose(2, 0, 1)

    gparts = [cols(inp[k]) for k in ("ffn1_pre_norm", "ffn1_post_norm", "mix_pre_norm", "mla_q_norm", "mla_kv_norm",
                                     "conv_out_norm", "na_out_norm", "mla_out_norm", "mix_post_norm",
                                     "ffn2_pre_norm", "ffn2_post_norm")]
    cw = f(inp["conv_w"]).reshape(depth, 3, 2, 128).transpose(3, 0, 1, 2).reshape(128, depth, 6)
    gv = np.concatenate(gparts + [cw], axis=2)
    assert gv.shape == (128, depth, NG), gv.shape
    gv = np.ascontiguousarray(gv.reshape(128, depth * NG))
    mb = np.ascontiguousarray(f(inp["na_meta_bias"]).transpose(2, 0, 1).reshape(16, depth * 4))
    pos = np.arange(Lmax, dtype=np.float32)
    inv_freq = (np.float32(10000.0) ** (-np.arange(16, dtype=np.float32) / np.float32(16))).astype(np.float32)
    ang = (pos[None, :] * inv_freq[:, None]).astype(np.float32)
    cos = np.cos(ang).astype(np.float32)
    sin = np.sin(ang).astype(np.float32)
    ropeC = np.ascontiguousarray(np.concatenate([cos, cos], axis=0))
    ropeS = np.ascontiguousarray(np.concatenate([-sin, sin], axis=0))
    return {
        "meta": f(inp["meta_tokens"]),
        "w_gu1": f(inp["ffn1_w_gu"]), "w_d1": f(inp["ffn1_w_down"]),
        "w_gu2": f(inp["ffn2_w_gu"]), "w_d2": f(inp["ffn2_w_down"]),
        "w_in": np.ascontiguousarray(w_in_ext), "w_uq": np.ascontiguousarray(w_uq_ext),
        "w_ukv": np.ascontiguousarray(w_ukv_ext), "w_o": f(inp["w_o"]),
        "gv": gv, "mb": mb, "ropeC": ropeC, "ropeS": ropeS,
        "nab": _na_tables(f(inp["na_rpb"])), "ident": np.eye(128, dtype=np.float32),
    }


def kernel(**inputs):
    depth = 4
    ncores = 8
    xp = np.asarray(inputs["x_prompt"], dtype=np.float32)
    xsmp = np.asarray(inputs["x_sample"], dtype=np.float32)
    seqs = [xp.shape[1], xp.shape[1], xsmp.shape[1]]
    Lmax = 16 + max(seqs)
    shared = _prep_shared(inputs, depth, Lmax)
    nc = build(seqs, depth)
    in_maps = []
    for c in range(ncores):
        m = dict(shared)
        m["x0"] = np.ascontiguousarray(xp[2 * c])
        m["x1"] = np.ascontiguousarray(xp[2 * c + 1])
        m["x2"] = np.ascontiguousarray(xsmp[c])
        in_maps.append(m)
    res = run_bass_kernel_spmd(nc, in_maps, core_ids=list(range(ncores)))
    yp = np.empty_like(xp)
    ysm = np.empty_like(xsmp)
    for c in range(ncores):
        r = res.results[c]
        yp[2 * c] = r["y0"]
        yp[2 * c + 1] = r["y1"]
        ysm[c] = r["y2"]
    return (yp, ysm)
```

```python
import contextlib
import numpy as np
import concourse.bass as bass
import concourse.mybir as mybir
from concourse.bass_utils import run_bass_kernel_spmd

F32 = mybir.dt.float32
BF16 = mybir.dt.bfloat16
AF = mybir.ActivationFunctionType
ALU = mybir.AluOpType

D_MODEL = 1024
FFN = 2816
NG = 70
EPS = 1e-6
NEG = -30000.0
MLA_SCALE = 96.0 ** -0.5
NA_SCALE = 0.125
G_F1PRE, G_F1POST, G_MPRE, G_QN, G_KVN, G_CON, G_NON, G_MON, G_MPOST, G_F2PRE, G_F2POST, G_CONVW = (
    0, 8, 16, 24, 30, 32, 34, 36, 40, 48, 56, 64)


ALL_TOKS = []


class Tok:
    __slots__ = ("w", "r")

    def __init__(self):
        self.w = None
        self.r = {}
        ALL_TOKS.append(self)


class Tile:
    def __init__(self, t, tok=None):
        self.t = t
        self.tok = tok or Tok()

    def __getitem__(self, idx):
        return self.t[idx]


class Prog:
    CE = ("pe", "act", "dve", "pool")
    ENG = ("pe", "act", "dve", "pool", "sp")
    NSETS = 4

    def __init__(self, nc, es, n_dma=40):
        self.nc = nc
        self.psem = {(e, s): es.enter_context(nc.semaphore("p_%s%d" % (e, s))) for e in self.CE for s in range(self.NSETS)}
        self.pcnt = {(e, s): 0 for e in self.CE for s in range(self.NSETS)}
        self.cur = 0
        self.dsem = [es.enter_context(nc.semaphore("d%d" % i)) for i in range(n_dma)]
        self.dcnt = [0] * n_dma
        self.dnext = 0
        self.seen = {e: {} for e in self.ENG}
        self.streams = {e: [] for e in self.ENG}
        self.nops = 0

    def _deps(self, reads, writes):
        d = {}
        for t in reads:
            if t.w is not None:
                k, v = t.w
                if d.get(k, 0) < v:
                    d[k] = v
        for t in writes:
            if t.w is not None:
                k, v = t.w
                if d.get(k, 0) < v:
                    d[k] = v
            for k, v in t.r.items():
                if d.get(k, 0) < v:
                    d[k] = v
        return d

    def _mark(self, reads, writes, ev):
        for t in writes:
            t.w = ev
            t.r = {}
        k, v = ev
        for t in reads:
            t.r[k] = v

    def _emit(self, eng, d, fn, evk, inc, strict=False):
        waits = []
        seen = self.seen[eng]
        for k, v in d.items():
            if eng == "pe" and k[0] == "p" and k[1] == "pe" and not strict:
                continue
            if seen.get(k, 0) < v:
                seen[k] = v
                waits.append((k, v))
        self.streams[eng].append((waits, fn, evk, inc))
        self.nops += 1

    def op(self, eng, fn, reads=(), writes=(), signal=True):
        d = self._deps(reads, writes)
        key = ("p", eng, self.cur)
        ck = (eng, self.cur)
        if signal:
            self.pcnt[ck] += 1
            assert self.pcnt[ck] < 65000, "progress semaphore overflow"
            ev = (key, self.pcnt[ck])
            self._emit(eng, d, fn, key, 1)
        else:
            ev = (key, self.pcnt[ck] + 1)
            self._emit(eng, d, fn, None, 0)
        self._mark(reads, writes, ev)

    def dma(self, q, fn, reads=(), writes=()):
        d = self._deps(reads, writes)
        i = self.dnext
        self.dnext = (i + 1) % len(self.dsem)
        k = ("d", i)
        if self.dcnt[i] > 0 and d.get(k, 0) < self.dcnt[i]:
            d[k] = self.dcnt[i]
        self.dcnt[i] += 16
        assert self.dcnt[i] < 65000, "dma semaphore overflow"
        ev = (k, self.dcnt[i])
        self._emit(q, d, fn, k, 16)
        self._mark(reads, writes, ev)

    def barrier(self, next_set=None):
        d = {("p", e, s): self.pcnt[(e, s)] for e in self.CE for s in range(self.NSETS) if self.pcnt[(e, s)] > 0}
        for i, c in enumerate(self.dcnt):
            if c > 0:
                d[("d", i)] = c
        for e in self.ENG:
            self._emit(e, dict(d), None, None, 0, strict=True)
        for t in ALL_TOKS:
            t.w = None
            t.r = {}
        if next_set is not None:
            self.cur = next_set

    def check(self):
        sem = {}
        pos = {e: 0 for e in self.ENG}
        n = {e: len(self.streams[e]) for e in self.ENG}
        while True:
            prog = False
            for e in self.ENG:
                st = self.streams[e]
                while pos[e] < n[e]:
                    waits, fn, evk, inc = st[pos[e]]
                    if any(sem.get(k, 0) < v for k, v in waits):
                        break
                    if evk is not None:
                        sem[evk] = sem.get(evk, 0) + inc
                    pos[e] += 1
                    prog = True
            if all(pos[e] == n[e] for e in self.ENG):
                return
            if not prog:
                msg = []
                for e in self.ENG:
                    if pos[e] < n[e]:
                        waits = self.streams[e][pos[e]][0]
                        msg.append((e, pos[e], [(k, v, sem.get(k, 0)) for k, v in waits if sem.get(k, 0) < v]))
                raise RuntimeError("DEADLOCK in generated program: %r" % (msg,))

    def run(self):
        nc = self.nc
        self.check()

        def semof(k):
            return self.psem[(k[1], k[2])] if k[0] == "p" else self.dsem[k[1]]

        def mk(name):
            stream = self.streams[name]

            def body(e):
                for waits, fn, evk, inc in stream:
                    for k, v in waits:
                        e.wait_ge(semof(k), v)
                    if fn is not None:
                        ins = fn(e)
                        if evk is not None:
                            ins.then_inc(semof(evk), inc)
            return body

        with nc.Block() as block:
            block.tensor(mk("pe"))
            block.scalar(mk("act"))
            block.vector(mk("dve"))
            block.gpsimd(mk("pool"))
            block.sync(mk("sp"))


def build(seqs, depth, lmax=None, debug=False):
    nseq = len(seqs)
    Lmax = lmax or (16 + max(seqs))
    nc = bass.Bass("TRN2", target_bir_lowering=False)

    def din(name, shape, dt=F32):
        return nc.dram_tensor(name, list(shape), dt, kind="ExternalInput").ap()

    def dscr(name, shape, dt):
        if debug and not name.startswith("wb_"):
            return nc.dram_tensor(name, list(shape), dt, kind="ExternalOutput").ap()
        return nc.dram_tensor(name, list(shape), dt).ap()

    xs = [din("x%d" % i, [T, D_MODEL]) for i, T in enumerate(seqs)]
    ys = [nc.dram_tensor("y%d" % i, [T, D_MODEL], F32, kind="ExternalOutput").ap() for i, T in enumerate(seqs)]
    meta_d = din("meta", [16, D_MODEL])
    wsrc = {
        "gu1": din("w_gu1", [depth, 1024, 2 * FFN]), "d1": din("w_d1", [depth, FFN, 1024]),
        "gu2": din("w_gu2", [depth, 1024, 2 * FFN]), "d2": din("w_d2", [depth, FFN, 1024]),
        "in": din("w_in", [depth, 1024, 2752]), "uq": din("w_uq", [depth, 768, 1024]),
        "ukv": din("w_ukv", [depth, 256, 1024]), "o": din("w_o", [depth, 1024, 1024]),
    }
    gv_d = din("gv", [128, depth * NG])
    mb_d = din("mb", [16, depth * 4])
    ropeC_d = din("ropeC", [32, Lmax])
    ropeS_d = din("ropeS", [32, Lmax])
    nab_d = din("nab", [depth, 3, 8, 4, 128, 512])
    ident_d = din("ident", [128, 128])

    wb = {k: dscr("wb_" + k, v.shape, BF16) for k, v in wsrc.items()}
    HT = dscr("HT", [1024, Lmax], F32)
    YT = dscr("YT", [1024, Lmax], BF16)
    CB = dscr("CB", [256, Lmax], F32)
    CV = dscr("CV", [256, Lmax + 2], F32)
    NQ = dscr("NQ", [256, Lmax], BF16)
    NK = dscr("NK", [256, Lmax], BF16)
    NV = dscr("NV", [Lmax, 260], BF16)
    QC = dscr("QC", [8, 96, Lmax], BF16)
    KC = dscr("KC", [8, 96, Lmax], BF16)
    MV = dscr("MV", [Lmax, 520], BF16)

    es = contextlib.ExitStack()
    with es:
        P = Prog(nc, es)
        P.cur = 3
        es.enter_context(nc.allow_non_contiguous_dma(reason="single pad columns / small strided tiles"))

        uniq = [0]

        def sb(stack, name, shape, dt):
            uniq[0] += 1
            return Tile(stack.enter_context(nc.sbuf_tensor("s%d_%s" % (uniq[0], name), list(shape), dt)))

        ps_t = es.enter_context(nc.psum_tensor("ps", [128, 8, 512], F32))
        banks = [Tile(ps_t[:, i, :]) for i in range(8)]
        gv = sb(es, "gv", [128, depth * NG], F32)
        mb = sb(es, "mb", [16, depth * 4], F32)
        ident = sb(es, "ident", [128, 128], F32)
        ones = sb(es, "ones", [128, 128], BF16)
        zero = sb(es, "zero", [128, 2], F32)
        P.dma("sp", lambda e: e.dma_start(out=gv[:, :], in_=gv_d[:, :]), writes=[gv.tok])
        P.dma("sp", lambda e: e.dma_start(out=mb[:, :], in_=mb_d[:, :]), writes=[mb.tok])
        P.dma("sp", lambda e: e.dma_start(out=ident[:, :], in_=ident_d[:, :]), writes=[ident.tok])
        ones_f = sb(es, "ones_f", [128, 128], F32)
        gvh = sb(es, "gvh", [128, depth * NG], F32)
        P.op("dve", lambda e: e.memset(ones_f[:, :], 1.0), writes=[ones_f.tok])
        P.op("dve", lambda e: e.tensor_scalar(out=gvh[:, :], in0=gv[:, :], scalar1=0.5, scalar2=None, op0=ALU.mult),
             [gv.tok], [gvh.tok])
        P.op("dve", lambda e: e.memset(ones[:, :], 1.0), writes=[ones.tok])
        P.op("dve", lambda e: e.memset(zero[:, :], 0.0), writes=[zero.tok])

        def gcol(l, base):
            return gv[:, l * NG + base:l * NG + base + 1]

        def gcolh(l, base):
            return gvh[:, l * NG + base:l * NG + base + 1]

        def mm(out, lhsT, rhs, start, stop, reads, writes, contig=True):
            P.op("pe", lambda e: e.matmul(out, lhsT=lhsT, rhs=rhs, start=start, stop=stop), reads, writes,
                 signal=(stop or not contig))

        def act(out, in_, func, reads, writes, scale=1.0, bias=0.0, accum=None):
            if accum is None:
                P.op("act", lambda e: e.activation(out=out, in_=in_, func=func, scale=scale, bias=bias),
                     reads, writes)
            else:
                P.op("act", lambda e: e.activation(out=out, in_=in_, func=func, scale=scale, bias=bias,
                                                   accum_out=accum), reads, writes)

        def stt(out, in0, scalar, in1, op0, op1, reads, writes):
            P.op("dve", lambda e: e.scalar_tensor_tensor(out=out, in0=in0, scalar=scalar, in1=in1,
                                                         op0=op0, op1=op1), reads, writes)

        def tt(out, in0, in1, op, reads, writes, eng="dve"):
            P.op(eng, lambda e: e.tensor_tensor(out=out, in0=in0, in1=in1, op=op), reads, writes)

        def ts(out, in0, s1, s2, op0, op1, reads, writes):
            if s2 is None:
                P.op("dve", lambda e: e.tensor_scalar(out=out, in0=in0, scalar1=s1, scalar2=None, op0=op0),
                     reads, writes)
            else:
                P.op("dve", lambda e: e.tensor_scalar(out=out, in0=in0, scalar1=s1, scalar2=s2, op0=op0, op1=op1),
                     reads, writes)

        def recip(out, in_, reads, writes):
            P.op("dve", lambda e: e.reciprocal(out=out, in_=in_), reads, writes)

        def load(out, in_, writes, reads=()):
            P.dma("sp", lambda e: e.dma_start(out=out, in_=in_), reads, writes)

        def store(out, in_, reads):
            P.dma("pool", lambda e: e.dma_start(out=out, in_=in_), reads, ())

        CW = 2048

        def cast_iter(l, fin, fout, engs):
            ci = 0
            for key in ("gu1", "d1", "in", "uq", "ukv", "o", "gu2", "d2"):
                src = wsrc[key]
                dst = wb[key]
                _, R, N = src.shape
                for rc in range(R // 128):
                    for c0 in range(0, N, CW):
                        n = min(CW, N - c0)
                        a, b = fin[ci % len(fin)], fout[ci % len(fout)]
                        load(a[:, :n], src[l, rc * 128:(rc + 1) * 128, c0:c0 + n], [a.tok])
                        eng = engs[ci % len(engs)]
                        if eng == "act":
                            P.op("act", lambda e, a=a, b=b, n=n: e.copy(out=b[:, :n], in_=a[:, :n]), [a.tok], [b.tok])
                        else:
                            P.op(eng, lambda e, a=a, b=b, n=n: e.tensor_copy(out=b[:, :n], in_=a[:, :n]), [a.tok], [b.tok])
                        store(dst[l, rc * 128:(rc + 1) * 128, c0:c0 + n], b[:, :n], [b.tok])
                        ci += 1
                        yield

        with contextlib.ExitStack() as ps_es:
            fin = [sb(ps_es, "fin%d" % i, [128, CW], F32) for i in range(3)]
            fout = [sb(ps_es, "fout%d" % i, [128, CW], BF16) for i in range(3)]
            for _ in cast_iter(0, fin, fout, ("pool", "dve", "act", "dve")):
                pass
        P.barrier(next_set=0)

        bank_rr = [0]

        def bank(pool=(0, 1, 2, 3, 4, 5)):
            b = banks[pool[bank_rr[0] % len(pool)]]
            bank_rr[0] += 1
            return b

        for si, T in enumerate(seqs):
            L = 16 + T
            if si > 0:
                P.barrier(next_set=si)
            blocks = [(0, 16)] + [(16 + 512 * i, 512) for i in range(T // 512)]
            HTv = HT.rearrange("(c p) l -> p c l", p=128)
            YTv = YT.rearrange("(c p) l -> p c l", p=128)

            with contextlib.ExitStack() as st:
                xt = [sb(st, "xt%d" % i, [128, 1024], F32) for i in range(2)]
                hb = [sb(st, "hb%d" % i, [128, 8, 512], F32) for i in range(2)]
                ti = 0
                for bi, (c0, nb) in enumerate(blocks):
                    h = hb[bi % 2]
                    for j in range((nb + 127) // 128):
                        qs = min(128, nb)
                        x_t = xt[ti % 2]
                        ti += 1
                        if bi == 0:
                            load(x_t[:qs, :], meta_d[:, :], [x_t.tok])
                        else:
                            t0 = c0 - 16 + j * 128
                            load(x_t[:qs, :], xs[si][t0:t0 + qs, :], [x_t.tok])
                        for c in range(8):
                            b = bank()
                            P.op("pe", lambda e, b=b, x_t=x_t, c=c, qs=qs: e.transpose(
                                out=b[:, :qs], in_=x_t[:qs, c * 128:(c + 1) * 128], identity=ident[:qs, :qs]),
                                [x_t.tok, ident.tok], [b.tok])
                            if c % 2 == 0:
                                P.op("dve", lambda e, b=b, h=h, c=c, j=j, qs=qs: e.tensor_copy(
                                    out=h[:, c, j * 128:j * 128 + qs], in_=b[:, :qs]), [b.tok], [h.tok])
                            else:
                                P.op("act", lambda e, b=b, h=h, c=c, j=j, qs=qs: e.copy(
                                    out=h[:, c, j * 128:j * 128 + qs], in_=b[:, :qs]), [b.tok], [h.tok])
                    store(HTv[:, :, c0:c0 + nb], h[:, :, :nb], [h.tok])
                CVv = CV.rearrange("(c p) l -> p c l", p=128)
                for c in range(2):
                    store(CVv[:, c, 0:1], zero[:, 0:1], [zero.tok])
                    store(CVv[:, c, L + 1:L + 2], zero[:, 1:2], [zero.tok])
            P.barrier()

            for l in range(depth + 1):
                with contextlib.ExitStack() as st:
                    hblk = sb(st, "hblk", [128, 8, 512], F32)
                    hb_tok = [Tok() for _ in range(8)]
                    xn = sb(st, "xn", [128, 8, 512], BF16)
                    xn_tok = [Tok() for _ in range(8)]
                    actT = sb(st, "actT", [128, 22, 512], BF16)
                    actT_tok = [Tok() for _ in range(22)]
                    obuf = sb(st, "obuf", [128, 8, 512], F32)
                    obuf_tok = [Tok() for _ in range(8)]
                    ring = [sb(st, "ring%d" % i, [128, 4096], BF16) for i in range(6)]
                    ring_i = [0]
                    sq8 = sb(st, "sq8", [128, 8, 512], BF16)
                    sq8_tok = [Tok() for _ in range(8)]
                    rtm = [sb(st, "rtm%d" % i, [128, 8], F32) for i in range(2)]
                    rtm_i = [0]
                    dmt = [sb(st, "dm%d" % i, [128, 4, 128], F32) for i in range(2)]
                    dm_i = [0]
                    RB = banks[6]
                    SS = banks[7]
                    sgb = [sb(st, "sg%d" % i, [128, 512], F32) for i in range(2)]
                    sg_i = [0]
                    ytb = sb(st, "ytb", [128, 8, 512], BF16)
                    qn = sb(st, "qn", [128, 6, 512], BF16)
                    kvn = sb(st, "kvn", [128, 2, 512], BF16)
                    cct = sb(st, "cct", [128, 2, 512], F32)
                    cbt = sb(st, "cbt", [128, 2, 512], F32)
                    cvt = sb(st, "cvt", [128, 2, 512], F32)
                    nqt = sb(st, "nqt", [128, 2, 512], BF16)
                    nkt = sb(st, "nkt", [128, 2, 512], BF16)
                    nvt = sb(st, "nvt", [128, 4, 260], BF16)
                    qct = sb(st, "qct", [96, 8, 512], BF16)
                    kct = sb(st, "kct", [96, 8, 512], BF16)
                    mvt = sb(st, "mvt", [128, 4, 520], BF16)
                    kper = sb(st, "kper", [96, 512], BF16)
                    rt1 = sb(st, "rt1", [96, 512], F32)
                    rt2 = sb(st, "rt2", [96, 512], F32)
                    rC = sb(st, "rC", [96, 512], F32)
                    rS = sb(st, "rS", [96, 512], F32)
                    yout = [sb(st, "yout%d" % i, [128, 1024], F32) for i in range(2)]
                    P.op("pool", lambda e, nvt=nvt: e.memset(nvt[:, :, :], 1.0), writes=[nvt.tok])
                    P.op("pool", lambda e, mvt=mvt: e.memset(mvt[:, :, :], 1.0), writes=[mvt.tok])

                    def load_w(W, KC_, c0, ncols, r0=0):
                        s = ring[ring_i[0] % len(ring)]
                        ring_i[0] += 1
                        view = s.t[:, 0:KC_ * ncols].rearrange("p (c n) -> p c n", c=KC_)
                        src = W.rearrange("(c p) n -> p c n", p=128)[:, r0:r0 + KC_, c0:c0 + ncols]
                        load(view, src, [s.tok])
                        return view, s.tok

                    def square_to(c, ap, tk, nb, from_psum, k):
                        if from_psum or k % 2 == 0:
                            act(sq8[:, c, :nb], ap, AF.Square, [tk], [sq8_tok[c]])
                        else:
                            tt(sq8[:, c, :nb], ap, ap, ALU.mult, [tk], [sq8_tok[c]])

                    def rstd_bcast(clist, nb, D):
                        ntile = (nb + 127) // 128
                        qs = min(128, nb)
                        nch = len(clist)
                        for j in range(ntile):
                            for i, c in enumerate(clist):
                                mm(SS[:qs, j:j + 1], sq8[:, c, j * 128:j * 128 + qs], ones[:, 0:1], i == 0, i == nch - 1,
                                   [sq8_tok[c], ones.tok], [SS.tok])
                        r = rtm[rtm_i[0] % 2]
                        rtm_i[0] += 1
                        act(r[:qs, 0:ntile], SS[:qs, 0:ntile], AF.Sqrt, [SS.tok], [r.tok], scale=1.0 / D, bias=EPS)
                        recip(r[:qs, 0:ntile], r[:qs, 0:ntile], [r.tok], [r.tok])
                        dm = dmt[dm_i[0] % 2]
                        dm_i[0] += 1
                        tt(dm[:qs, 0:ntile, :qs], ident[:qs, :qs].unsqueeze(1).to_broadcast([qs, ntile, qs]),
                           r[:qs, 0:ntile].unsqueeze(2).to_broadcast([qs, ntile, qs]), ALU.mult, [ident.tok, r.tok], [dm.tok])
                        for j in range(ntile):
                            P.op("pe", lambda e, j=j, dm=dm, qs=qs: e.matmul(
                                RB[:, j * 128:j * 128 + qs], lhsT=ones_f[:qs, :], rhs=dm[:qs, j, :qs], start=True, stop=True),
                                [dm.tok, ones_f.tok], [RB.tok])
                        return RB

                    def rms_sbuf(srcs, nb, D, c0_=0, act_only=False):
                        for i, (ap, tk) in enumerate(srcs):
                            square_to(c0_ + i, ap, tk, nb, False, 0 if act_only else i)
                        return rstd_bcast([c0_ + i for i in range(len(srcs))], nb, D)

                    def post_chunk(dc, b, nb):
                        P.op("dve", lambda e, dc=dc, b=b, nb=nb, obuf=obuf: e.tensor_copy(out=obuf[:, dc, :nb], in_=b[:, :nb]),
                             [b.tok], [obuf_tok[dc]])
                        square_to(dc, obuf[:, dc, :nb], obuf_tok[dc], nb, True, 0)

                    def post_end(l_, gbase, coef, nb):
                        r = rstd_bcast(list(range(8)), nb, 1024)
                        for dc in range(8):
                            g_ = gcolh(l_, gbase + dc) if coef == 0.5 else gcol(l_, gbase + dc)
                            stt(obuf[:, dc, :nb], obuf[:, dc, :nb], g_, r[:, :nb], ALU.mult, ALU.mult,
                                [obuf_tok[dc], r.tok, gv.tok, gvh.tok], [obuf_tok[dc]])
                            tt(hblk[:, dc, :nb], hblk[:, dc, :nb], obuf[:, dc, :nb], ALU.add,
                               [obuf_tok[dc], hb_tok[dc]], [hb_tok[dc]], eng=("pool" if dc % 4 == 3 else "dve"))

                    def make_xn(l_, gbase, nb):
                        r = rms_sbuf([(hblk[:, c, :nb], hb_tok[c]) for c in range(8)], nb, 1024, act_only=True)
                        for c in range(8):
                            stt(xn[:, c, :nb], hblk[:, c, :nb], gcol(l_, gbase + c), r[:, :nb], ALU.mult, ALU.mult,
                                [hb_tok[c], r.tok, gv.tok], [xn_tok[c]])

                    def ffn(l_, which, nb):
                        make_xn(l_, G_F1PRE if which == 1 else G_F2PRE, nb)
                        Wgu = wb["gu%d" % which][l_]
                        Wd = wb["d%d" % which][l_]
                        for fg in range(6):
                            f0 = fg * 512
                            nf = min(512, FFN - f0)
                            wg, wg_tok = load_w(Wgu, 8, f0, nf)
                            wu, wu_tok = load_w(Wgu, 8, FFN + f0, nf)
                            for jj in range(nf // 128):
                                j = fg * 4 + jj
                                bg = bank()
                                bu = bank()
                                for kc in range(8):
                                    mm(bg[:, :nb], wg[:, kc, jj * 128:(jj + 1) * 128], xn[:, kc, :nb], kc == 0, kc == 7,
                                       [wg_tok, xn_tok[kc]], [bg.tok])
                                for kc in range(8):
                                    mm(bu[:, :nb], wu[:, kc, jj * 128:(jj + 1) * 128], xn[:, kc, :nb], kc == 0, kc == 7,
                                       [wu_tok, xn_tok[kc]], [bu.tok])
                                s = sgb[sg_i[0] % 2]
                                sg_i[0] += 1
                                act(s[:, :nb], bg[:, :nb], AF.Silu, [bg.tok], [s.tok])
                                tt(actT[:, j, :nb], s[:, :nb], bu[:, :nb], ALU.mult, [s.tok, bu.tok], [actT_tok[j]])
                        parts = [(0, 8), (8, 16), (16, 22)]
                        for half in range(2):
                            sl = [load_w(Wd, j1 - j0, half * 512, 512, r0=j0) for (j0, j1) in parts]
                            for dcr in range(4):
                                dc = half * 4 + dcr
                                b = bank()
                                for j in range(22):
                                    pi = 0 if j < 8 else (1 if j < 16 else 2)
                                    w, wt = sl[pi]
                                    mm(b[:, :nb], w[:, j - parts[pi][0], dcr * 128:(dcr + 1) * 128], actT[:, j, :nb],
                                       j == 0, j == 21, [wt, actT_tok[j]], [b.tok])
                                post_chunk(dc, b, nb)
                        post_end(l_, G_F1POST if which == 1 else G_F2POST, 0.5, nb)

                    def mixer_out(l_, c0, nb):
                        load(ytb[:, :, :nb], YTv[:, :, c0:c0 + nb], [ytb.tok])
                        for half in range(2):
                            w, wt = load_w(wb["o"][l_], 8, half * 512, 512)
                            for dcr in range(4):
                                dc = half * 4 + dcr
                                b = bank()
                                for c in range(8):
                                    mm(b[:, :nb], w[:, c, dcr * 128:(dcr + 1) * 128], ytb[:, c, :nb], c == 0, c == 7,
                                       [wt, ytb.tok], [b.tok])
                                post_chunk(dc, b, nb)
                        post_end(l_, G_MPOST, 1.0, nb)

                    def projections(l_, c0, nb):
                        make_xn(l_, G_MPRE, nb)
                        Win = wb["in"][l_]
                        ql = obuf
                        load(rC[64:96, :nb], ropeC_d[:, c0:c0 + nb], [rC.tok])
                        load(rS[64:96, :nb], ropeS_d[:, c0:c0 + nb], [rS.tok])
                        ntile = (nb + 127) // 128
                        qs = min(128, nb)
                        for g in range(6):
                            gc0 = g * 512
                            ncol = min(512, 2752 - gc0)
                            w, wt = load_w(Win, 8, gc0, ncol)
                            if g == 5:
                                bA = bank()
                                bB = bank()
                                for kc in range(8):
                                    mm(bA[:96, :nb], w[:, kc, 0:96], xn[:, kc, :nb], kc == 0, kc == 7, [wt, xn_tok[kc]], [bA.tok])
                                for kc in range(8):
                                    mm(bB[:96, :nb], w[:, kc, 96:192], xn[:, kc, :nb], kc == 0, kc == 7, [wt, xn_tok[kc]], [bB.tok])
                                tt(rt1[64:96, :nb], bA[64:96, :nb], rC[64:96, :nb], ALU.mult, [bA.tok, rC.tok], [rt1.tok])
                                tt(rt2[64:96, :nb], bB[64:96, :nb], rS[64:96, :nb], ALU.mult, [bB.tok, rS.tok], [rt2.tok])
                                tt(kper[64:96, :nb], rt1[64:96, :nb], rt2[64:96, :nb], ALU.add, [rt1.tok, rt2.tok], [kper.tok])
                                continue
                            for oc_r in range(4):
                                oc = g * 4 + oc_r
                                if oc in (10, 11):
                                    continue
                                b = bank()
                                for kc in range(8):
                                    mm(b[:, :nb], w[:, kc, oc_r * 128:(oc_r + 1) * 128], xn[:, kc, :nb], kc == 0, kc == 7,
                                       [wt, xn_tok[kc]], [b.tok])
                                if oc < 2:
                                    act(cbt[:, oc, :nb], b[:, :nb], AF.Copy, [b.tok], [cbt.tok])
                                elif oc < 4:
                                    act(cct[:, oc - 2, :nb], b[:, :nb], AF.Copy, [b.tok], [cct.tok])
                                elif oc < 6:
                                    tt(cvt[:, oc - 4, :nb], cct[:, oc - 4, :nb], b[:, :nb], ALU.mult, [cct.tok, b.tok], [cvt.tok])
                                elif oc < 8:
                                    act(nqt[:, oc - 6, :nb], b[:, :nb], AF.Copy, [b.tok], [nqt.tok])
                                elif oc < 10:
                                    act(nkt[:, oc - 8, :nb], b[:, :nb], AF.Copy, [b.tok], [nkt.tok])
                                else:
                                    act(ql[:, oc - 12, :nb], b[:, :nb], AF.Copy, [b.tok], [obuf_tok[oc - 12]])
                            if g == 2:
                                for j in range(ntile):
                                    b = bank()
                                    for kc in range(8):
                                        mm(b[:qs, 0:256], xn[:, kc, j * 128:j * 128 + qs], w[:, kc, 256:512], kc == 0, kc == 7,
                                           [wt, xn_tok[kc]], [b.tok])
                                    P.op("dve", lambda e, b=b, j=j, nvt=nvt, qs=qs: e.tensor_copy(
                                        out=nvt[:qs, j, :].rearrange("p (h d) -> p h d", h=4)[:, :, 0:64],
                                        in_=b[:qs, 0:256].rearrange("p (h d) -> p h d", h=4)), [b.tok], [nvt.tok])
                        CBv = CB.rearrange("(c p) l -> p c l", p=128)
                        CVv = CV.rearrange("(c p) l -> p c l", p=128)
                        NQv = NQ.rearrange("(c p) l -> p c l", p=128)
                        NKv = NK.rearrange("(c p) l -> p c l", p=128)
                        store(CBv[:, :, c0:c0 + nb], cbt[:, :, :nb], [cbt.tok])
                        store(CVv[:, :, c0 + 1:c0 + 1 + nb], cvt[:, :, :nb], [cvt.tok])
                        store(NQv[:, :, c0:c0 + nb], nqt[:, :, :nb], [nqt.tok])
                        store(NKv[:, :, c0:c0 + nb], nkt[:, :, :nb], [nkt.tok])
                        if nb == 16:
                            store(NV[0:16, :], nvt[:16, 0, :], [nvt.tok])
                        else:
                            store(NV[c0:c0 + nb, :].rearrange("(t p) f -> p t f", p=128), nvt[:, :, :], [nvt.tok])
                        r = rms_sbuf([(ql[:, c, :nb], obuf_tok[c]) for c in range(6)], nb, 768)
                        for c in range(6):
                            stt(qn[:, c, :nb], ql[:, c, :nb], gcol(l_, G_QN + c), r[:, :nb], ALU.mult, ALU.mult,
                                [obuf_tok[c], r.tok, gv.tok], [qn.tok])
                        for hg in range(2):
                            w, wt = load_w(wb["uq"][l_], 6, hg * 512, 512)
                            for hr in range(4):
                                h = hg * 4 + hr
                                bm = bank()
                                bs = bank()
                                for c in range(6):
                                    mm(bm[:96, :nb], w[:, c, hr * 128:hr * 128 + 96], qn[:, c, :nb], c == 0, c == 5,
                                       [wt, qn.tok], [bm.tok])
                                for c in range(6):
                                    mm(bs[:96, :nb], w[:, c, hr * 128 + 32:hr * 128 + 128], qn[:, c, :nb], c == 0, c == 5,
                                       [wt, qn.tok], [bs.tok])
                                tt(rt1[64:96, :nb], bm[64:96, :nb], rC[64:96, :nb], ALU.mult, [bm.tok, rC.tok], [rt1.tok])
                                tt(rt2[64:96, :nb], bs[64:96, :nb], rS[64:96, :nb], ALU.mult, [bs.tok, rS.tok], [rt2.tok])
                                tt(qct[64:96, h, :nb], rt1[64:96, :nb], rt2[64:96, :nb], ALU.add, [rt1.tok, rt2.tok], [qct.tok])
                                act(qct[0:64, h, :nb], bm[0:64, :nb], AF.Copy, [bm.tok], [qct.tok])
                        store(QC.rearrange("h p l -> p h l")[:, :, c0:c0 + nb], qct[:, :, :nb], [qct.tok])
                        r = rms_sbuf([(ql[:, 6 + c, :nb], obuf_tok[6 + c]) for c in range(2)], nb, 256, c0_=6)
                        for c in range(2):
                            stt(kvn[:, c, :nb], ql[:, 6 + c, :nb], gcol(l_, G_KVN + c), r[:, :nb], ALU.mult, ALU.mult,
                                [obuf_tok[6 + c], r.tok, gv.tok], [kvn.tok])
                        w, wt = load_w(wb["ukv"][l_], 2, 0, 1024)
                        for h in range(8):
                            b = bank()
                            for c in range(2):
                                mm(b[:64, :nb], w[:, c, h * 128:h * 128 + 64], kvn[:, c, :nb], c == 0, c == 1,
                                   [wt, kvn.tok], [b.tok])
                            act(kct[0:64, h, :nb], b[0:64, :nb], AF.Copy, [b.tok], [kct.tok])
                            P.op("pool", lambda e, h=h, kct=kct, kper=kper, nb=nb: e.tensor_copy(out=kct[64:96, h, :nb], in_=kper[64:96, :nb]),
                                 [kper.tok], [kct.tok])
                        store(KC.rearrange("h p l -> p h l")[:, :, c0:c0 + nb], kct[:, :, :nb], [kct.tok])
                        wv = w.rearrange("p c (h d) -> p c h d", h=8)[:, :, :, 64:128]
                        for j in range(ntile):
                            b = bank()
                            for c in range(2):
                                mm(b[:qs, :].rearrange("p (h d) -> p h d", h=8), kvn[:, c, j * 128:j * 128 + qs], wv[:, c],
                                   c == 0, c == 1, [wt, kvn.tok], [b.tok])
                            P.op("dve", lambda e, b=b, j=j, mvt=mvt, qs=qs: e.tensor_copy(
                                out=mvt[:qs, j, :].rearrange("p (h d) -> p h d", h=8)[:, :, 0:64],
                                in_=b[:qs, :].rearrange("p (h d) -> p h d", h=8)), [b.tok], [mvt.tok])
                        if nb == 16:
                            store(MV[0:16, :], mvt[:16, 0, :], [mvt.tok])
                        else:
                            store(MV[c0:c0 + nb, :].rearrange("(t p) f -> p t f", p=128), mvt[:, :, :], [mvt.tok])

                    def final_out(c0, nb):
                        for j in range(nb // 128):
                            yo = yout[j % 2]
                            for half in range(2):
                                b = bank()
                                for cr in range(4):
                                    c = half * 4 + cr
                                    P.op("pe", lambda e, b=b, c=c, cr=cr, j=j, hblk=hblk: e.transpose(
                                        out=b[:, cr * 128:(cr + 1) * 128], in_=hblk[:, c, j * 128:(j + 1) * 128],
                                        identity=ident[:, :]), hb_tok + [ident.tok], [b.tok])
                                if half == 0:
                                    P.op("dve", lambda e, b=b, yo=yo: e.tensor_copy(out=yo[:, 0:512], in_=b[:, :]),
                                         [b.tok], [yo.tok])
                                else:
                                    P.op("act", lambda e, b=b, yo=yo: e.copy(out=yo[:, 512:1024], in_=b[:, :]),
                                         [b.tok], [yo.tok])
                            t0 = c0 - 16 + j * 128
                            store(ys[si][t0:t0 + 128, :], yo[:, :], [yo.tok])

                    for (c0, nb) in blocks:
                        if l == depth and nb == 16:
                            continue
                        load(hblk[:, :, :nb], HTv[:, :, c0:c0 + nb], hb_tok)
                        if l > 0:
                            mixer_out(l - 1, c0, nb)
                            ffn(l - 1, 2, nb)
                        if l < depth:
                            ffn(l, 1, nb)
                            store(HTv[:, :, c0:c0 + nb], hblk[:, :, :nb], hb_tok)
                            projections(l, c0, nb)
                        else:
                            final_out(c0, nb)
                P.barrier()
                if l == depth:
                    break

                with contextlib.ExitStack() as st:
                    ntile_k = 1 + T // 128
                    kcsb = sb(st, "kcsb", [128, 8, L], BF16)
                    mvsb = sb(st, "mvsb", [128, ntile_k, 520], BF16)
                    qcb = [sb(st, "qcb%d" % i, [128, 8, 512], BF16) for i in range(2)]
                    ptb = [sb(st, "pt%d" % i, [128, 2, 512], BF16) for i in range(3)]
                    pt_i = [0]
                    nkw = sb(st, "nkw", [128, 2, 16 + 1024], BF16)
                    nvw = sb(st, "nvw", [128, 9, 260], BF16)
                    nqb = sb(st, "nqb", [128, 2, 512], BF16)
                    nbias = [sb(st, "nbias%d" % i, [128, 512], F32) for i in range(3)]
                    nb_i = [0]
                    sbb = [sb(st, "sbb%d" % i, [128, 2, 512], F32) for i in range(2)]
                    sb_i = [0]
                    ym = sb(st, "ym", [128, 4, 768], F32)
                    ym_tok = [Tok() for _ in range(4)]
                    yt = sb(st, "yt", [128, 8, 512], BF16)
                    cvh = sb(st, "cvh", [128, 2, 514], F32)
                    cbb = sb(st, "cbb", [128, 2, 512], F32)
                    ctmp = sb(st, "ctmp", [128, 2, 512], F32)
                    csq = sb(st, "csq", [128, 2, 512], BF16)
                    crs = sb(st, "crs", [128, 512], F32)
                    junk = sb(st, "junk", [128, 512], F32)
                    ssm = [sb(st, "ssm%d" % i, [128, 4], F32) for i in range(2)]
                    ss_i = [0]
                    rdt = [sb(st, "rdt%d" % i, [128, 1], F32) for i in range(4)]
                    rd_i = [0]

                    P.op("pool", lambda e, kcsb=kcsb: e.memset(kcsb[96:128, :, :], 0.0), writes=[kcsb.tok])
                    for q_ in qcb:
                        P.op("pool", lambda e, q_=q_: e.memset(q_[96:128, :, :], 0.0), writes=[q_.tok])
                    bg = None
                    if si == 0 and l + 1 < depth:
                        cfin = [sb(st, "cfin%d" % i, [128, CW], F32) for i in range(3)]
                        cfout = [sb(st, "cfout%d" % i, [128, CW], BF16) for i in range(3)]
                        bg = cast_iter(l + 1, cfin, cfout, ("dve", "pool", "dve"))

                    def bg_step(k):
                        if bg is not None:
                            for _ in range(k):
                                next(bg, None)

                    for h in range(8):
                        load(kcsb[0:96, h, :], KC[h, :, 0:L], [kcsb.tok])
                    load(mvsb[:16, 0, :], MV[0:16, :], [mvsb.tok])
                    load(mvsb[:, 1:, :], MV[16:16 + T, :].rearrange("(t p) f -> p t f", p=128), [mvsb.tok])
                    obanks = banks[0:4]

                    spair = [Tile(ps_t[:, 4:6, :]), Tile(ps_t[:, 6:8, :])]
                    sp_i = [0]

                    def attention(nq, nheads, units, qk_ops, vsrc, ycol0, scale, post):
                        qtiles = (nq + 127) // 128
                        qs = min(128, nq)
                        groups = [[0]] + [[u, u + 1] for u in range(1, len(units), 2)]
                        assert all(g[-1] < len(units) for g in groups)
                        last_u = len(units) - 1
                        for h in range(nheads):
                            def qk(g):
                                bp = spair[sp_i[0] % 2]
                                sp_i[0] += 1
                                for i, u in enumerate(groups[g]):
                                    lhsT, rhs, rd = qk_ops(h, u)
                                    mm(bp[:units[u]["nk"], i, :nq], lhsT, rhs, True, True, rd, [bp.tok])
                                return bp

                            def pv(g, bp):
                                pt = post(h, groups[g], bp)
                                for i, u in enumerate(groups[g]):
                                    nk = units[u]["nk"]
                                    vap, vtok = vsrc(h, u)
                                    for j in range(qtiles):
                                        mm(obanks[j][:qs, 0:65], pt[:nk, i, j * 128:j * 128 + qs], vap, u == 0, u == last_u,
                                           [pt.tok, vtok], [obanks[j].tok], contig=False)

                            bg_step(3)
                            bcur = qk(0)
                            for g in range(len(groups)):
                                bnext = qk(g + 1) if g + 1 < len(groups) else None
                                pv(g, bcur)
                                bcur = bnext
                            for j in range(qtiles):
                                rd = rdt[rd_i[0] % 4]
                                rd_i[0] += 1
                                recip(rd[:qs, :], obanks[j][:qs, 64:65], [obanks[j].tok], [rd.tok])
                                ts(ym[:qs, j, ycol0 + h * 64:ycol0 + (h + 1) * 64], obanks[j][:qs, 0:64], rd[:qs, 0:1], None,
                                   ALU.mult, None, [obanks[j].tok, rd.tok], [ym_tok[j]])

                    nblk = T // 512
                    NQv = NQ.rearrange("(c p) l -> p c l", p=128)
                    NKv = NK.rearrange("(c p) l -> p c l", p=128)
                    CBv = CB.rearrange("(c p) l -> p c l", p=128)
                    CVv = CV.rearrange("(c p) l -> p c l", p=128)
                    for bi, (c0, nb) in enumerate(blocks):
                        qtiles = (nb + 127) // 128
                        qs = min(128, nb)
                        qc = qcb[bi % 2]
                        load(qc[0:96, :, :nb], QC.rearrange("h p l -> p h l")[:, :, c0:c0 + nb], [qc.tok])
                        m_units = [dict(nk=16, k0=0, vt=0)] + [dict(nk=128, k0=16 + 128 * t, vt=1 + t) for t in range(T // 128)]

                        def m_qk(h, u, qc=qc, nb=nb):
                            un = m_units[u]
                            return (kcsb[:, h, un["k0"]:un["k0"] + un["nk"]], qc[:, h, :nb], [kcsb.tok, qc.tok])

                        def m_post(h, grp, bp, nb=nb):
                            nk = m_units[grp[0]]["nk"]
                            ng = len(grp)
                            pt = ptb[pt_i[0] % 3]
                            pt_i[0] += 1
                            act(pt[:nk, 0:ng, :nb], bp[:nk, 0:ng, :nb], AF.Exp, [bp.tok], [pt.tok], scale=MLA_SCALE)
                            return pt

                        def m_v(h, u):
                            un = m_units[u]
                            return mvsb[:un["nk"], un["vt"], h * 65:(h + 1) * 65], mvsb.tok

                        attention(nb, 8, m_units, m_qk, m_v, 256, MLA_SCALE, m_post)

                        load(nqb[:, :, :nb], NQv[:, :, c0:c0 + nb], [nqb.tok])
                        load(nkw[:, :, 0:16], NKv[:, :, 0:16], [nkw.tok])
                        load(nvw[:16, 0, :], NV[0:16, :], [nvw.tok])
                        n_units = [dict(nk=16, k0=0, vt=0, bias=None)]
                        if bi > 0:
                            i = bi - 1
                            cls = 0 if i == 0 else (2 if i == nblk - 1 else 1)
                            kt_lo = 4 * i + (0 if cls == 0 else -2)
                            nkt = 8 if cls == 1 else 6
                            kcol0 = 16 + 128 * kt_lo
                            load(nkw[:, :, 16:16 + 128 * nkt], NKv[:, :, kcol0:kcol0 + 128 * nkt], [nkw.tok])
                            load(nvw[:, 1:1 + nkt, :], NV[kcol0:kcol0 + 128 * nkt, :].rearrange("(t p) f -> p t f", p=128), [nvw.tok])
                            for kr in range(nkt):
                                n_units.append(dict(nk=128, k0=16 + 128 * kr, vt=1 + kr, bias=(cls, kr)))

                        def n_qk(h, u, nb=nb):
                            un = n_units[u]
                            po = 64 * (h % 2)
                            return (nkw[po:po + 64, h // 2, un["k0"]:un["k0"] + un["nk"]], nqb[po:po + 64, h // 2, :nb],
                                    [nkw.tok, nqb.tok])

                        def n_post(h, grp, bp, nb=nb):
                            un = n_units[grp[0]]
                            nk = un["nk"]
                            ng = len(grp)
                            pt = ptb[pt_i[0] % 3]
                            pt_i[0] += 1
                            if un["bias"] is None:
                                act(pt[:nk, 0, :nb], bp[:nk, 0, :nb], AF.Exp, [bp.tok, mb.tok], [pt.tok], scale=NA_SCALE,
                                    bias=mb[:, l * 4 + h:l * 4 + h + 1])
                            else:
                                s = sbb[sb_i[0] % 2]
                                sb_i[0] += 1
                                for i, u in enumerate(grp):
                                    cls_, kr = n_units[u]["bias"]
                                    bt = nbias[nb_i[0] % 3]
                                    nb_i[0] += 1
                                    load(bt[:, :], nab_d[l, cls_, kr, h], [bt.tok])
                                    stt(s[:, i, :nb], bp[:, i, :nb], NA_SCALE, bt[:, :nb], ALU.mult, ALU.add,
                                        [bp.tok, bt.tok], [s.tok])
                                act(pt[:nk, 0:ng, :nb], s[:nk, 0:ng, :nb], AF.Exp, [s.tok], [pt.tok])
                            return pt

                        def n_v(h, u):
                            un = n_units[u]
                            return nvw[:un["nk"], un["vt"], h * 65:(h + 1) * 65], nvw.tok

                        attention(nb, 4, n_units, n_qk, n_v, 0, NA_SCALE, n_post)

                        for j in range(qtiles):
                            ss = ssm[ss_i[0] % 2]
                            ss_i[0] += 1
                            act(junk[:qs, 0:256], ym[:qs, j, 0:256], AF.Square, [ym_tok[j]], [junk.tok, ss.tok], accum=ss[:qs, 0:1])
                            act(junk[:qs, 0:512], ym[:qs, j, 256:768], AF.Square, [ym_tok[j]], [junk.tok, ss.tok], accum=ss[:qs, 1:2])
                            act(ss[:qs, 2:3], ss[:qs, 0:1], AF.Sqrt, [ss.tok], [ss.tok], scale=1.0 / 256, bias=EPS)
                            act(ss[:qs, 3:4], ss[:qs, 1:2], AF.Sqrt, [ss.tok], [ss.tok], scale=1.0 / 512, bias=EPS)
                            recip(ss[:qs, 2:4], ss[:qs, 2:4], [ss.tok], [ss.tok])
                            ts(ym[:qs, j, 0:256], ym[:qs, j, 0:256], ss[:qs, 2:3], None, ALU.mult, None, [ym_tok[j], ss.tok], [ym_tok[j]])
                            ts(ym[:qs, j, 256:768], ym[:qs, j, 256:768], ss[:qs, 3:4], None, ALU.mult, None, [ym_tok[j], ss.tok], [ym_tok[j]])
                            for c in range(6):
                                b = bank((0, 1, 2, 3))
                                P.op("pe", lambda e, b=b, c=c, j=j, qs=qs, ym=ym: e.transpose(
                                    out=b[:, :qs], in_=ym[:qs, j, c * 128:(c + 1) * 128], identity=ident[:qs, :qs]),
                                    [ym_tok[j], ident.tok], [b.tok])
                                gb = (G_NON + c) if c < 2 else (G_MON + c - 2)
                                act(yt[:, 2 + c, j * 128:j * 128 + qs], b[:, :qs], AF.Copy, [b.tok, gv.tok], [yt.tok],
                                    scale=gcol(l, gb))

                        load(cvh[:, :, 0:nb + 2], CVv[:, :, c0:c0 + nb + 2], [cvh.tok])
                        load(cbb[:, :, :nb], CBv[:, :, c0:c0 + nb], [cbb.tok])
                        for c in range(2):
                            ts(ctmp[:, c, :nb], cvh[:, c, 0:nb], gcol(l, G_CONVW + 0 + c), None, ALU.mult, None,
                               [cvh.tok, gv.tok], [ctmp.tok])
                            stt(ctmp[:, c, :nb], cvh[:, c, 1:nb + 1], gcol(l, G_CONVW + 2 + c), ctmp[:, c, :nb], ALU.mult, ALU.add,
                                [cvh.tok, ctmp.tok, gv.tok], [ctmp.tok])
                            stt(ctmp[:, c, :nb], cvh[:, c, 2:nb + 2], gcol(l, G_CONVW + 4 + c), ctmp[:, c, :nb], ALU.mult, ALU.add,
                                [cvh.tok, ctmp.tok, gv.tok], [ctmp.tok])
                            tt(ctmp[:, c, :nb], ctmp[:, c, :nb], cbb[:, c, :nb], ALU.mult, [ctmp.tok, cbb.tok], [ctmp.tok])
                            act(csq[:, c, :nb], ctmp[:, c, :nb], AF.Square, [ctmp.tok], [csq.tok])
                        for c in range(2):
                            mm(banks[0][:, :nb], ones[:, :], csq[:, c, :nb], c == 0, c == 1, [csq.tok, ones.tok], [banks[0].tok],
                               contig=False)
                        act(crs[:, :nb], banks[0][:, :nb], AF.Sqrt, [banks[0].tok], [crs.tok], scale=1.0 / 256, bias=EPS)
                        recip(crs[:, :nb], crs[:, :nb], [crs.tok], [crs.tok])
                        for c in range(2):
                            stt(yt[:, c, :nb], ctmp[:, c, :nb], gcol(l, G_CON + c), crs[:, :nb], ALU.mult, ALU.mult,
                                [ctmp.tok, crs.tok, gv.tok], [yt.tok])
                        store(YTv[:, :, c0:c0 + nb], yt[:, :, :nb], [yt.tok])
                    bg_step(10000)
                P.barrier()

        P.barrier()
        P.run()
    return nc


def _na_tables(rpb):
    D = rpb.shape[0]
    R = 32
    out = np.full((D, 3, 8, 4, 128, 512), NEG, dtype=np.float32)
    qc = np.arange(64)
    kc = np.arange(64)
    cs = np.clip(qc - 8, 0, 48)
    col_valid = (kc[:, None] >= cs[None, :]) & (kc[:, None] < cs[None, :] + 16)
    col_rel = np.clip(kc[:, None] - qc[None, :] + 15, 0, 30)
    for cls, (r0, kt_lo, nkt) in enumerate(((0, 0, 6), (8, 2, 8), (24, 10, 6))):
        for kti in range(nkt):
            for p in range(2):
                kr = 2 * (kt_lo + kti) + p
                for j in range(8):
                    r = r0 + j
                    rs = min(max(r - 4, 0), R - 8)
                    if not (rs <= kr < rs + 8):
                        continue
                    row_rel = kr - r + 7
                    vals = rpb[:, :, row_rel, :][:, :, col_rel]
                    vals = np.where(col_valid[None, None], vals, np.float32(NEG))
                    out[:, cls, kti, :, p * 64:(p + 1) * 64, j * 64:(j + 1) * 64] = vals
    return out


def _prep_shared(inp, depth, Lmax):
    f = lambda a: np.ascontiguousarray(np.asarray(a, dtype=np.float32))
    w_in = f(inp["w_in"])
    pad64 = np.zeros((depth, 1024, 64), np.float32)
    w_in_ext = np.concatenate([w_in[:, :, :2560], pad64, w_in[:, :, 2560:2592], pad64,
                               w_in[:, :, 2576:2592], w_in[:, :, 2560:2576]], axis=2)
    w_uq = f(inp["mla_w_uq"]).reshape(depth, 768, 8, 96)
    w_uq_ext = np.concatenate([w_uq[..., 0:64], w_uq[..., 64:96], w_uq[..., 80:96], w_uq[..., 64:80]], axis=3)
    w_uq_ext = w_uq_ext.reshape(depth, 768, 1024)
    w_ukv_ext = f(inp["mla_w_ukv"])

    def cols(v):
        v = f(v)
        return v.reshape(depth, -1, 128).transpose(2, 0, 1)

    gparts = [cols(inp[k]) for k in ("ffn1_pre_norm", "ffn1_post_norm", "mix_pre_norm", "mla_q_norm", "mla_kv_norm",
                                     "conv_out_norm", "na_out_norm", "mla_out_norm", "mix_post_norm",
                                     "ffn2_pre_norm", "ffn2_post_norm")]
    cw = f(inp["conv_w"]).reshape(depth, 3, 2, 128).transpose(3, 0, 1, 2).reshape(128, depth, 6)
    gv = np.concatenate(gparts + [cw], axis=2)
    assert gv.shape == (128, depth, NG), gv.shape
    gv = np.ascontiguousarray(gv.reshape(128, depth * NG))
    mb = np.ascontiguousarray(f(inp["na_meta_bias"]).transpose(2, 0, 1).reshape(16, depth * 4))
    pos = np.arange(Lmax, dtype=np.float32)
    inv_freq = (np.float32(10000.0) ** (-np.arange(16, dtype=np.float32) / np.float32(16))).astype(np.float32)
    ang = (pos[None, :] * inv_freq[:, None]).astype(np.float32)
    cos = np.cos(ang).astype(np.float32)
    sin = np.sin(ang).astype(np.float32)
    ropeC = np.ascontiguousarray(np.concatenate([cos, cos], axis=0))
    ropeS = np.ascontiguousarray(np.concatenate([-sin, sin], axis=0))
    return {
        "meta": f(inp["meta_tokens"]),
        "w_gu1": f(inp["ffn1_w_gu"]), "w_d1": f(inp["ffn1_w_down"]),
        "w_gu2": f(inp["ffn2_w_gu"]), "w_d2": f(inp["ffn2_w_down"]),
        "w_in": np.ascontiguousarray(w_in_ext), "w_uq": np.ascontiguousarray(w_uq_ext),
        "w_ukv": np.ascontiguousarray(w_ukv_ext), "w_o": f(inp["w_o"]),
        "gv": gv, "mb": mb, "ropeC": ropeC, "ropeS": ropeS,
        "nab": _na_tables(f(inp["na_rpb"])), "ident": np.eye(128, dtype=np.float32),
    }


def kernel(**inputs):
    depth = 4
    ncores = 8
    xp = np.asarray(inputs["x_prompt"], dtype=np.float32)
    xsmp = np.asarray(inputs["x_sample"], dtype=np.float32)
    seqs = [xp.shape[1], xp.shape[1], xsmp.shape[1]]
    Lmax = 16 + max(seqs)
    shared = _prep_shared(inputs, depth, Lmax)
    nc = build(seqs, depth)
    in_maps = []
    for c in range(ncores):
        m = dict(shared)
        m["x0"] = np.ascontiguousarray(xp[2 * c])
        m["x1"] = np.ascontiguousarray(xp[2 * c + 1])
        m["x2"] = np.ascontiguousarray(xsmp[c])
        in_maps.append(m)
    res = run_bass_kernel_spmd(nc, in_maps, core_ids=list(range(ncores)))
    yp = np.empty_like(xp)
    ysm = np.empty_like(xsmp)
    for c in range(ncores):
        r = res.results[c]
        yp[2 * c] = r["y0"]
        yp[2 * c + 1] = r["y1"]
        ysm[c] = r["y2"]
    return (yp, ysm)
```
